# Optimizing a Trainium2 kernel written in Bass

```python
import math
import jax, jax.numpy as jnp
from jax import lax
import numpy as np

D_MODEL = 1024
BATCH = 16
SEQ = 2048
DEPTH = 1

D_MIX = D_MODEL
DIFF_WIDTH = D_MIX // 2
GLA_WIDTH = D_MIX - DIFF_WIDTH
DIFF_HEADS = 4
DIFF_V_DIM = DIFF_WIDTH // DIFF_HEADS
DIFF_QK_DIM = DIFF_V_DIM // 2
ROT_DIM = DIFF_QK_DIM // 4
ROPE_THETA = 500000.0
Q_BLOCK = 128
GLA_HEADS = 4
GLA_V_DIM = GLA_WIDTH // GLA_HEADS
GLA_K_DIM = GLA_V_DIM // 2
GK_RANK = 16
GATE_NORMALIZER = 16.0
CHUNK = 64
D_DQ = DIFF_HEADS * 2 * DIFF_QK_DIM
D_DK = DIFF_HEADS * 2 * DIFF_QK_DIM
D_DV = DIFF_WIDTH
D_DG = DIFF_WIDTH
D_GQ = GLA_HEADS * GLA_K_DIM
D_GK = GLA_HEADS * GLA_K_DIM
D_GV = GLA_WIDTH
D_GG = GLA_WIDTH
D_IN = D_DQ + D_DK + D_DV + D_DG + D_GQ + D_GK + D_GV + D_GG + GK_RANK
NORM_EPS = 1e-6
NEG_INF = -1e30

kernel_name = "hybrid_diffattn_gla_parallel_heads"


def rms_norm(x, gain, eps=NORM_EPS):
    xf = x.astype(jnp.float32)
    y = xf * lax.rsqrt(jnp.mean(xf * xf, axis=-1, keepdims=True) + eps)
    return (y * gain.astype(jnp.float32)).astype(x.dtype)


def partial_rope(x, cos, sin):
    half = ROT_DIM // 2
    x1 = x[..., :half]
    x2 = x[..., half:ROT_DIM]
    return jnp.concatenate([x1 * cos - x2 * sin, x2 * cos + x1 * sin, x[..., ROT_DIM:]], axis=-1)


def split_columns(proj):
    sizes = [D_DQ, D_DK, D_DV, D_DG, D_GQ, D_GK, D_GV, D_GG, GK_RANK]
    idx = [int(v) for v in np.cumsum(sizes)[:-1]]
    return jnp.split(proj, idx, axis=-1)


def diff_attention(q, k, v, lam):
    b, s, h, _, d = q.shape
    qs = q.transpose(0, 2, 3, 1, 4)
    ks = k.transpose(0, 2, 3, 1, 4)
    vs = v.transpose(0, 2, 1, 3)
    scale = 1.0 / math.sqrt(d)
    kpos = jnp.arange(s)
    n_blocks = s // Q_BLOCK

    def block(i):
        qb = lax.dynamic_slice_in_dim(qs, i * Q_BLOCK, Q_BLOCK, axis=3)
        sc = jnp.einsum('bhcqd,bhckd->bhcqk', qb, ks).astype(jnp.float32) * scale
        qpos = i * Q_BLOCK + jnp.arange(Q_BLOCK)
        sc = jnp.where(kpos[None, :] <= qpos[:, None], sc, NEG_INF)
        p = jax.nn.softmax(sc, axis=-1)
        a = p[:, :, 0] - lam * p[:, :, 1]
        return jnp.einsum('bhqk,bhkv->bhqv', a.astype(vs.dtype), vs)

    o = lax.map(block, jnp.arange(n_blocks))
    return o.transpose(1, 0, 3, 2, 4).reshape(b, s, h, vs.shape[-1])


def gla_chunked(q, k, v, g):
    b, s, h, dk = q.shape
    dv = v.shape[-1]
    nc = s // CHUNK

    def to_chunks(t):
        return t.astype(jnp.float32).reshape(b, nc, CHUNK, h, t.shape[-1]).transpose(1, 0, 3, 2, 4)

    qc, kc, vc, gc = to_chunks(q), to_chunks(k), to_chunks(v), to_chunks(g)
    tril = jnp.tril(jnp.ones((CHUNK, CHUNK), dtype=bool))

    def step(state, inp):
        qi, ki, vi, gi = inp
        bcum = jnp.cumsum(gi, axis=2)
        o_inter = jnp.einsum('bhcd,bhde->bhce', qi * jnp.exp(bcum), state)
        diff = bcum[:, :, :, None, :] - bcum[:, :, None, :, :]
        m = tril[None, None, :, :, None]
        decay = jnp.where(m, jnp.exp(jnp.where(m, diff, 0.0)), 0.0)
        attn = jnp.einsum('bhid,bhjd,bhijd->bhij', qi, ki, decay)
        o_intra = jnp.einsum('bhij,bhje->bhie', attn, vi)
        b_last = bcum[:, :, -1, :]
        k_dec = ki * jnp.exp(b_last[:, :, None, :] - bcum)
        new_state = jnp.exp(b_last)[..., None] * state + jnp.einsum('bhcd,bhce->bhde', k_dec, vi)
        return new_state, o_inter + o_intra

    s0 = jnp.zeros((b, h, dk, dv), jnp.float32)
    _, o = lax.scan(step, s0, (qc, kc, vc, gc))
    return o.transpose(1, 0, 3, 2, 4).reshape(b, s, h, dv).astype(v.dtype)


def setup_inputs(seed: int = 0) -> dict:
    key = jax.random.key(seed)
    ks = jax.random.split(key, 16)
    f = jnp.float32
    x = jax.random.normal(ks[0], (BATCH, SEQ, D_MODEL), f)
    norm_gain = 1.0 + 0.02 * jax.random.normal(ks[1], (DEPTH, D_MODEL), f)
    w_in = jax.random.normal(ks[2], (DEPTH, D_MODEL, D_IN), f) * D_MODEL ** -0.5
    q_norm_gain = 1.0 + 0.02 * jax.random.normal(ks[3], (DEPTH, DIFF_QK_DIM), f)
    k_norm_gain = 1.0 + 0.02 * jax.random.normal(ks[4], (DEPTH, DIFF_QK_DIM), f)
    lambda_q1 = 0.1 * jax.random.normal(ks[5], (DEPTH, DIFF_QK_DIM), f)
    lambda_k1 = 0.1 * jax.random.normal(ks[6], (DEPTH, DIFF_QK_DIM), f)
    lambda_q2 = 0.1 * jax.random.normal(ks[7], (DEPTH, DIFF_QK_DIM), f)
    lambda_k2 = 0.1 * jax.random.normal(ks[8], (DEPTH, DIFF_QK_DIM), f)
    diff_out_gain = 1.0 + 0.02 * jax.random.normal(ks[9], (DEPTH, DIFF_V_DIM), f)
    gk_up = jax.random.normal(ks[10], (DEPTH, GK_RANK, D_GK), f) * GK_RANK ** -0.5
    gk_bias = 0.01 * jax.random.normal(ks[11], (DEPTH, D_GK), f)
    gla_out_gain = 1.0 + 0.02 * jax.random.normal(ks[12], (DEPTH, GLA_V_DIM), f)
    w_out = jax.random.normal(ks[13], (DEPTH, D_MIX, D_MODEL), f) * D_MIX ** -0.5
    return {"x": x, "norm_gain": norm_gain, "w_in": w_in,
            "q_norm_gain": q_norm_gain, "k_norm_gain": k_norm_gain,
            "lambda_q1": lambda_q1, "lambda_k1": lambda_k1,
            "lambda_q2": lambda_q2, "lambda_k2": lambda_k2,
            "diff_out_gain": diff_out_gain, "gk_up": gk_up, "gk_bias": gk_bias,
            "gla_out_gain": gla_out_gain, "w_out": w_out}


def reference(x, norm_gain, w_in, q_norm_gain, k_norm_gain, lambda_q1, lambda_k1,
              lambda_q2, lambda_k2, diff_out_gain, gk_up, gk_bias, gla_out_gain, w_out):
    b, s, _ = x.shape
    pos = jnp.arange(s, dtype=jnp.float32)
    inv_freq = ROPE_THETA ** (-jnp.arange(0, ROT_DIM, 2, dtype=jnp.float32) / ROT_DIM)
    ang = pos[:, None] * inv_freq[None, :]
    cos = jnp.cos(ang)[:, None, None, :].astype(x.dtype)
    sin = jnp.sin(ang)[:, None, None, :].astype(x.dtype)

    for l in range(DEPTH):
        lambda_init = 0.8 - 0.6 * math.exp(-0.3 * l)
        h = rms_norm(x, norm_gain[l])
        proj = jnp.einsum('bsd,de->bse', h, w_in[l])
        dq, dk, dv, dg, gq, gk, gv, gg, gk_low = split_columns(proj)

        dq = dq.reshape(b, s, DIFF_HEADS, 2, DIFF_QK_DIM)
        dk = dk.reshape(b, s, DIFF_HEADS, 2, DIFF_QK_DIM)
        dv = dv.reshape(b, s, DIFF_HEADS, DIFF_V_DIM)
        dq = partial_rope(rms_norm(dq, q_norm_gain[l]), cos, sin)
        dk = partial_rope(rms_norm(dk, k_norm_gain[l]), cos, sin)
        lam = (jnp.exp(jnp.sum(lambda_q1[l].astype(jnp.float32) * lambda_k1[l].astype(jnp.float32)))
               - jnp.exp(jnp.sum(lambda_q2[l].astype(jnp.float32) * lambda_k2[l].astype(jnp.float32)))
               + lambda_init)
        o_a = diff_attention(dq, dk, dv, lam)
        o_a = rms_norm(o_a, diff_out_gain[l]) * (1.0 - lambda_init)
        o_a = o_a.reshape(b, s, DIFF_WIDTH) * jax.nn.silu(dg)

        gq = gq.reshape(b, s, GLA_HEADS, GLA_K_DIM) * (GLA_K_DIM ** -0.5)
        gk = gk.reshape(b, s, GLA_HEADS, GLA_K_DIM)
        gv = gv.reshape(b, s, GLA_HEADS, GLA_V_DIM)
        g_logit = jnp.einsum('bsr,re->bse', gk_low, gk_up[l]) + gk_bias[l]
        g_log = (jax.nn.log_sigmoid(g_logit.astype(jnp.float32)) / GATE_NORMALIZER)
        g_log = g_log.reshape(b, s, GLA_HEADS, GLA_K_DIM)
        o_b = gla_chunked(gq, gk, gv, g_log)
        o_b = rms_norm(o_b, gla_out_gain[l]).reshape(b, s, GLA_WIDTH) * jax.nn.silu(gg)

        o = jnp.concatenate([o_a, o_b], axis=-1)
        x = x + jnp.einsum('bse,ed->bsd', o, w_out[l])
    return x
```

```python
import numpy as np
from contextlib import ExitStack

import concourse.bass as bass
import concourse.mybir as mybir
from concourse.bass_utils import run_bass_kernel_spmd

F32 = mybir.dt.float32
BF16 = mybir.dt.bfloat16
AF = mybir.ActivationFunctionType
ALU = mybir.AluOpType
AX = mybir.AxisListType

N_CORES = 8
SCHED_LOG = []
SCHED = {'heads': -1, 'mp': 1, 'mg': 0, 'tail': 0, 'fa': 1, 'fp': -1, 'fv': 1, 'margin': 150.0, 'hold': 0.0}
SEQ = 2048
D = 1024
NSEQ = 2
D_IN = 3600
EPS = 1e-6
LAMBDA_INIT = 0.8 - 0.6
GT = 256
NG_SEQ = SEQ // GT
NGROUPS = NSEQ * NG_SEQ

C_NG = 0
C_OG = 8
C_QKG = 16
C_LAM = 144
C_GB = 400
C_COS = 656
C_SIN = 784
C_ID = 912
C_TRI = 1040
C_AFT = 1168
C_ONE = 1296
C_GUP = 1424
NCST = 1680


class Tok:
    __slots__ = ("key", "val", "t")

    def __init__(self, key, val=None, t=0.0):
        self.key = key
        self.val = val
        self.t = t


def _nfree(ap):
    n = 1
    for d in ap.shape[1:]:
        n *= int(d)
    return n


class _CostProxy:
    def __init__(self, tracker, ename, eng):
        self._t = tracker
        self._e = ename
        self._eng = eng

    def __getattr__(self, name):
        real = getattr(self._eng, name)
        e = self._e

        def call(*a, **kw):
            ap = kw.get("rhs") if e == "pe" and name == "matmul" else None
            if ap is None:
                ap = kw.get("in_", kw.get("in0", kw.get("out", a[0] if a else None)))
            n = _nfree(ap) if ap is not None else 64
            if e == "pe":
                c = max(107.0, 0.42 * n)
                if name == "matmul" and kw["rhs"].dtype == F32:
                    c *= 4.0
            elif e == "act":
                c = 190.0 + 0.83 * n + (90.0 if kw.get("accum_out") is not None else 0.0)
            elif e == "dve":
                two = name in ("tensor_tensor", "scalar_tensor_tensor")
                c = 70.0 + (1.3 if not two else 1.5) * n
                if name == "reciprocal":
                    c = 70.0 + 8.0 * n
            else:
                c = 120.0 + 3.0 * n
            self._t._cost = c
            return real(*a, **kw)

        return call


class Tracker:
    def __init__(self, nc, stack):
        self.nc = nc
        self.stack = stack
        self.eng = {"pe": nc.tensor, "act": nc.scalar, "dve": nc.vector,
                    "pool": nc.gpsimd, "sp": nc.sync}
        self.sems = {}
        self.count = {}
        self.waited = {e: {} for e in self.eng}
        self.pending = {e: [] for e in self.eng}
        self.last_w = {}
        self.readers = {}
        self.n_wait = 0
        self.n_ins = 0
        self.dry = False
        self.proxy = {e: _CostProxy(self, e, en) for e, en in self.eng.items()}
        self.eng_t = {e: 0.0 for e in self.eng}
        self._cost = 100.0
        self._ready = 0.0
        self.step_t = 0.0

    def sem(self, key):
        if key not in self.sems:
            self.sems[key] = self.stack.enter_context(self.nc.semaphore("s_" + key))
            self.count[key] = 0
        return self.sems[key]

    def _deps(self, e, reads, writes, is_dma):
        need = {}
        self._ready = 0.0

        def add(tok, force):
            if tok is None:
                return
            if tok.t > self._ready:
                self._ready = tok.t
            if not force and tok.key == e and e == "pe":
                return
            assert tok.val is not None, ("unresolved token", tok.key, e, reads, writes)
            if need.get(tok.key, 0) < tok.val:
                need[tok.key] = tok.val

        for r in reads:
            add(self.last_w.get(r), True)
        for w in writes:
            add(self.last_w.get(w), is_dma)
            for tok in self.readers.get(w, {}).values():
                add(tok, is_dma)
        for key, val in need.items():
            if self.waited[e].get(key, 0) >= val:
                continue
            self.eng[e].wait_ge(self.sem(key), val)
            self.waited[e][key] = val
            self.n_wait += 1

    def _record(self, tok, reads, writes):
        for w in writes:
            self.last_w[w] = tok
            self.readers[w] = {}
        for r in reads:
            if r in writes:
                continue
            self.readers.setdefault(r, {})[tok.key] = tok

    def op(self, e, fn, reads=(), writes=(), sig=True):
        if self.dry:
            return
        excl = [r for r in reads if isinstance(r, tuple) and r[0] == "ps" and r not in writes]
        if excl:
            writes = list(writes) + excl
        self._deps(e, reads, writes, False)
        ins = fn(self.proxy[e])
        self.n_ins += 1
        fin = max(self.eng_t[e], self._ready + 60.0) + self._cost
        self.eng_t[e] = fin
        if fin > self.step_t:
            self.step_t = fin
        tok = Tok(e, None, fin)
        if sig:
            s = self.sem(e)
            self.count[e] += 1
            ins.then_inc(s, 1)
            tok.val = self.count[e]
            for p in self.pending[e]:
                p.val = tok.val
            self.pending[e] = []
        else:
            self.pending[e].append(tok)
        self._record(tok, reads, writes)

    def dma(self, q, out, in_, reads, writes, semkey):
        if self.dry:
            return
        self._deps(q, reads, writes, True)
        s = self.sem(semkey)
        self.count[semkey] += 16
        self.eng[q].dma_start(out=out, in_=in_).then_inc(s, 16)
        self.n_ins += 1
        fin = max(self.eng_t[q], self._ready + 60.0) + 2500.0 + _nfree(out) * 128 * 4 / 160.0
        self.eng_t[q] = max(self.eng_t[q], self._ready) + 100.0
        tok = Tok(semkey, self.count[semkey], fin)
        self._record(tok, reads, writes)

    def wait_all(self, e, keys):
        for key in keys:
            if key in self.sems and self.waited[e].get(key, 0) < self.count[key]:
                self.eng[e].wait_ge(self.sems[key], self.count[key])
                self.waited[e][key] = self.count[key]


def build_program(ngroups=NGROUPS, upto=99):
    nc = bass.Bass("TRN2", target_bir_lowering=False)
    x_d = nc.dram_tensor("x", [NSEQ * SEQ, D], F32, kind="ExternalInput").ap()
    win_d = nc.dram_tensor("w_in", [D, D_IN], F32, kind="ExternalInput").ap()
    wout_d = nc.dram_tensor("w_out", [D, D], F32, kind="ExternalInput").ap()
    cst_d = nc.dram_tensor("cst", [128, NCST], F32, kind="ExternalInput").ap()
    out_d = nc.dram_tensor("out", [NSEQ * SEQ, D], F32, kind="ExternalOutput").ap()

    with ExitStack() as st:
        def sb(name, shape, dt):
            return st.enter_context(nc.sbuf_tensor(name, shape, dt))

        CST = sb("cst_sb", [128, NCST], F32)
        W = sb("w_sb", [128, 8, D_IN], BF16)
        WO = sb("wo_sb", [128, 8, D], BF16)
        XR = sb("xr", [128, 2, 2, D], F32)
        XB = sb("xb", [128, D], BF16)
        XT = sb("xt", [128, 2, 8, GT], BF16)
        QKR = sb("qkr", [128, 2, 1024], F32)
        SQ = sb("sq", [128, 1024], F32)
        QKB = sb("qkb", [128, 1, 1024], BF16)
        KT = sb("kt", [128, 4, SEQ], BF16)
        VX = sb("vx", [128, 16, 4, 130], BF16)
        QT = sb("qt", [128, 2, 4, 2, GT], BF16)
        DGS = sb("dgs", [128, 2, 2, 512], F32)
        GGS = sb("ggs", [128, 2, 512], F32)
        GQK = sb("gqk", [128, 2, 512], F32)
        GV = sb("gv", [128, 2, 512], BF16)
        GLT = sb("glt", [16, GT], F32)
        PT = sb("pt", [128, 3, 2, GT], BF16)
        OAH = sb("oah", [128, 2, 128], F32)
        OCA = sb("oca", [128, 2, 512], BF16)
        OCB = sb("ocb", [128, 2, 512], BF16)
        OCT = sb("oct", [128, 2, 2, 8, 128], BF16)
        ZL = sb("zl", [128, 1, 256], F32)
        EQ = sb("eq", [128, 256], F32)
        EK = sb("ek", [128, 256], F32)
        ED = sb("ed", [128, 256], F32)
        QG = sb("qg", [128, 256], BF16)
        KG = sb("kg", [128, 256], BF16)
        KD = sb("kd", [128, 256], BF16)
        QKTG = sb("qktg", [64, 8, 128], BF16)
        AT = sb("at", [128, 4, 128], BF16)
        S = sb("s_st", [64, 4, 128], F32)
        SB = sb("s_bf", [64, 4, 128], BF16)
        EB = sb("eb", [64, 4, 2], F32)
        STAT = sb("stat", [128, 320], F32)
        IDB = sb("idb", [128, 128], BF16)
        TRI2 = sb("tri2", [128, 2, 128], BF16)
        JUNK = sb("junk", [128, 128], BF16)
        PS = st.enter_context(nc.psum_tensor("ps", [128, 8 * 512], F32))

        T = Tracker(nc, st)

        def PE(fn, r=(), w=(), sig=True):
            T.op("pe", fn, r, w, sig)

        def ACT(fn, r=(), w=()):
            T.op("act", fn, r, w)

        def DVE(fn, r=(), w=()):
            T.op("dve", fn, r, w)

        def POOL(fn, r=(), w=()):
            T.op("pool", fn, r, w)

        def bank(b, n=512, off=0):
            return PS[:, b * 512 + off: b * 512 + off + n]

        def bank_bf(b):
            return PS[:, b * 512:(b + 1) * 512].bitcast(BF16)

        cst = lambda c, n: CST[:, c:c + n]
        ident_f = cst(C_ID, 128)
        tri_f = cst(C_TRI, 128)
        aft_f = cst(C_AFT, 128)
        ones_f = cst(C_ONE, 128)

        T.dma("sp", CST[:, :], cst_d[:, :], [], ["cst"], "d_cst")
        if upto >= -2:
            DVE(lambda e: e.tensor_copy(out=IDB[:, :], in_=ident_f), ["cst"], ["idb"])
            for m in range(2):
                DVE(lambda e, m=m: e.tensor_copy(out=TRI2[:, m, :], in_=tri_f), ["cst"], ["tri2"])
            DVE(lambda e: e.memset(VX[:, :, :, 128:130], 1.0), [], ["vx_ones"])
            DVE(lambda e: e.memset(QT[:, :, :, :, :].rearrange("p a h m c -> p (a h m c)"), 0.0), [], [("qt", 0), ("qt", 1)])
            DVE(lambda e: e.memset(S[:, :, :], 0.0), [], ["S"])
            DVE(lambda e: e.memset(SB[:, :, :], 0.0), [], ["SB"])

            LAMP = STAT[:, 0:128]
            LS = STAT[:, 128:130]
            LE = STAT[:, 130:132]
            NLAM = STAT[:, 132:133]
            for i in range(2):
                DVE(lambda e, i=i: e.tensor_tensor(out=LAMP[:, 0:64], in0=cst(C_LAM + 128 * i, 64),
                                                   in1=cst(C_LAM + 128 * i + 64, 64), op=ALU.mult),
                    ["cst"], ["lamp"])
                DVE(lambda e, i=i: e.tensor_reduce(out=LS[:, i:i + 1], in_=LAMP[:, 0:64], axis=AX.X,
                                                   op=ALU.add), ["lamp"], ["ls"])
            ACT(lambda e: e.activation(out=LE, in_=LS, func=AF.Exp), ["ls"], ["le"])
            DVE(lambda e: e.tensor_tensor(out=NLAM, in0=LE[:, 1:2], in1=LE[:, 0:1], op=ALU.subtract),
                ["le"], ["nlam"])
            DVE(lambda e: e.tensor_scalar(out=NLAM, in0=NLAM, scalar1=-LAMBDA_INIT, scalar2=None,
                                          op0=ALU.add), ["nlam"], ["nlam"])


        stage = lambda k: XR[:, k % 2, :, :].rearrange("p t d -> p (t d)")
        k = 0
        engs = None
        def load_xt(g, t):
            i = 2 * g + t
            wr = [("xf", t)] + ([("X", 0)] if g == 0 else [])
            T.dma("sp", XR[:, 0, t, :], x_d[i * 128:(i + 1) * 128, :], [], wr, "d_xf%d" % t)

        def load_y(g, t):
            i = 2 * g + t
            wr = [("y", t)] + ([("X", 1)] if g == 0 else [])
            T.dma("sp", XR[:, 1, t, :], x_d[i * 128:(i + 1) * 128, :], [], wr, "d_y%d" % t)

        load_xt(0, 0)
        load_xt(0, 1)
        win3 = win_d.rearrange("(kc p) c -> p kc c", p=128)
        stg = [XR[:, 1, :, :].rearrange("p t d -> p (t d)"),
               DGS[:, :, :, :].rearrange("p a t d -> p (a t d)")]
        stg_res = [("X", 1), "stgB"]
        ngb = CST[:, C_NG:C_NG + 8]
        order = [0, 1, 2, 3, 8, 9, 10, 11, 12, 13, 14, 4, 5, 6, 7]

        def wl_in(lo, hi):
            for k in range(lo, hi):
                cb = order[k]
                s_ = k % 2
                ncol = 256 if cb < 14 else 16
                c0 = cb * 256
                st3 = stg[s_][:, 0:8 * ncol].rearrange("p (kc c) -> p kc c", kc=8)
                T.dma("sp", st3, win3[:, :, c0:c0 + ncol], [], [stg_res[s_]], "d_x%d" % s_)
                for half, eng in ((0, DVE), (1, POOL)):
                    eng(lambda e: e.tensor_tensor(
                        out=W[:, 4 * half:4 * half + 4, c0:c0 + ncol], in0=st3[:, 4 * half:4 * half + 4, :],
                        in1=ngb[:, 4 * half:4 * half + 4][:, :, None].broadcast_to([128, 4, ncol]), op=ALU.mult),
                        [stg_res[s_], "cst"], [("W", cb, half)])
                yield

        def wl_out():
            for kc in range(8):
                T.dma("sp", stg[0][:, 0:1024], wout_d[kc * 128:(kc + 1) * 128, :], [], [stg_res[0]], "d_x0")
                gmul = (1.0 - LAMBDA_INIT) if kc < 4 else 1.0
                DVE(lambda e: e.tensor_scalar(out=WO[:, kc, 0:512], in0=stg[0][:, 0:512], scalar1=cst(C_OG + kc, 1),
                                              scalar2=gmul, op0=ALU.mult, op1=ALU.mult),
                    [stg_res[0], "cst"], [("WO", kc, 0)])
                POOL(lambda e: e.tensor_scalar(out=WO[:, kc, 512:1024], in0=stg[0][:, 512:1024],
                                               scalar1=cst(C_OG + kc, 1), scalar2=gmul, op0=ALU.mult, op1=ALU.mult),
                     [stg_res[0], "cst"], [("WO", kc, 1)])
                yield

        Wn = lambda n: [("W", cb, half) for cb in (2 * n, 2 * n + 1) for half in range(2)]
        Wlow = [("W", 14, 0), ("W", 14, 1)]
        WOres = [("WO", kc, q) for kc in range(8) for q in range(2)]

        evac_rr = [0]

        def evac_scaled(out, in_, sc, r, w):
            evac_rr[0] ^= 1
            if False:
                ACT(lambda e: e.activation(out=out, in_=in_, func=AF.Copy, scale=sc), r, w)
            else:
                DVE(lambda e: e.tensor_scalar(out=out, in0=in_, scalar1=sc, scalar2=None,
                                              op0=ALU.mult), r, w)

        pj = [0]
        SS = STAT[:, 136:138]
        LNV = STAT[:, 138:140]
        RSTD2 = lambda xs: STAT[:, 288 + 2 * xs:290 + 2 * xs]

        def proj(xs, t, n, ncols, dst, dst_res):
            b = pj[0]
            pj[0] ^= 1
            XTr = [("xt", xs, t)]
            for kc in range(8):
                PE(lambda e: e.matmul(out=bank(b, ncols), lhsT=XT[:, xs, kc, t * 128:(t + 1) * 128],
                                      rhs=W[:, kc, n * 512:n * 512 + ncols],
                                      start=(kc == 0), stop=(kc == 7)),
                   XTr + Wn(n), [("ps", b)], sig=(kc == 7))
            src = bank(b, ncols)
            if len(dst.shape) == 3:
                src = src.rearrange("p (h c) -> p h c", h=dst.shape[1])
            evac_scaled(dst, src, RSTD2(xs)[:, t:t + 1], [("ps", b), ("rstd", xs)], dst_res)

        def prep_qk(g, t):
            j = g % NG_SEQ
            xs = g % 2
            jt = 2 * j + t
            qk3 = QKR[:, t, :].rearrange("p (g d) -> p g d", d=64)
            SSQ = STAT[:, 144 + 16 * t:160 + 16 * t]
            LNQ = STAT[:, 176 + 16 * t:192 + 16 * t]
            RQ = STAT[:, 208 + 16 * t:224 + 16 * t]
            qr = ("qkr", t)
            ACT(lambda e: e.activation(out=SQ[:, :], in_=QKR[:, t, :], func=AF.Square), [qr], ["sq"])
            DVE(lambda e: e.tensor_reduce(out=SSQ, in_=SQ[:, :].rearrange("p (g d) -> p g d", d=64),
                                          axis=AX.X, op=ALU.add), ["sq"], [("ssq", t)])
            yield
            ACT(lambda e: e.activation(out=LNQ, in_=SSQ, func=AF.Ln, scale=1.0 / 64, bias=EPS),
                [("ssq", t)], [("lnq", t)])
            ACT(lambda e: e.activation(out=RQ, in_=LNQ, func=AF.Exp, scale=-0.5), [("lnq", t)], [("rq", t)])
            POOL(lambda e: e.tensor_tensor(out=qk3, in0=qk3, in1=RQ[:, :, None].broadcast_to([128, 16, 64]),
                                           op=ALU.mult), [qr, ("rq", t)], [qr])
            yield
            gain4 = CST[:, C_QKG:C_QKG + 128].rearrange("p (a d) -> p a d", a=2)[:, :, None, :] \
                .broadcast_to([128, 2, 8, 64])
            qk4 = QKR[:, t, :].rearrange("p (a g d) -> p a g d", a=2, g=8)
            DVE(lambda e: e.tensor_tensor(out=qk4, in0=qk4, in1=gain4, op=ALU.mult), [qr, "cst"], [qr])
            ACT(lambda e: e.activation(out=QKB[:, 0, :], in_=QKR[:, t, :], func=AF.Copy), [qr], ["qkb"])
            yield
            cosb = CST[:, C_COS + jt * 8:C_COS + jt * 8 + 8][:, None, :].broadcast_to([128, 16, 8])
            sinb = CST[:, C_SIN + jt * 8:C_SIN + jt * 8 + 8][:, None, :].broadcast_to([128, 16, 8])
            x1 = qk3[:, :, 0:8]
            x2 = qk3[:, :, 8:16]
            RA = SQ[:, 0:128].rearrange("p (g d) -> p g d", d=8)
            RB = SQ[:, 128:256].rearrange("p (g d) -> p g d", d=8)
            RC = SQ[:, 256:384].rearrange("p (g d) -> p g d", d=8)
            RD = SQ[:, 384:512].rearrange("p (g d) -> p g d", d=8)
            POOL(lambda e: e.tensor_tensor(out=RA, in0=x1, in1=cosb, op=ALU.mult), [qr, "cst"], ["ra", "sq"])
            POOL(lambda e: e.tensor_tensor(out=RB, in0=x2, in1=sinb, op=ALU.mult), [qr, "cst"], ["rb", "sq"])
            POOL(lambda e: e.tensor_tensor(out=RC, in0=x2, in1=cosb, op=ALU.mult), [qr, "cst"], ["rc", "sq"])
            POOL(lambda e: e.tensor_tensor(out=RD, in0=x1, in1=sinb, op=ALU.mult), [qr, "cst"], ["rd", "sq"])
            yield
            qb3 = QKB[:, 0, :].rearrange("p (g d) -> p g d", d=64)
            DVE(lambda e: e.tensor_tensor(out=qb3[:, :, 0:8], in0=RA, in1=RB, op=ALU.subtract),
                ["ra", "rb", "sq"], ["qkb"])
            DVE(lambda e: e.tensor_tensor(out=qb3[:, :, 8:16], in0=RC, in1=RD, op=ALU.add),
                ["rc", "rd", "sq"], ["qkb"])
            yield
            pb = 7
            pbf = bank_bf(pb).rearrange("p (k c) -> p k c", k=8)
            for blk in range(8):
                PE(lambda e: e.transpose(out=pbf[:, blk, :], in_=QKB[:, 0, blk * 128:(blk + 1) * 128],
                                         identity=IDB[:, :]),
                   ["qkb", "idb"], [("ps", pb)], sig=(blk == 7))
            DVE(lambda e: e.tensor_copy(out=QT[0:64, xs, :, 0, t * 128:(t + 1) * 128], in_=pbf[0:64, 0:4, :]),
                [("ps", pb)], [("qt", xs)])
            DVE(lambda e: e.tensor_copy(out=QT[64:128, xs, :, 1, t * 128:(t + 1) * 128], in_=pbf[64:128, 0:4, :]),
                [("ps", pb)], [("qt", xs)])
            DVE(lambda e: e.tensor_copy(out=KT[:, :, jt * 128:(jt + 1) * 128], in_=pbf[:, 4:8, :]),
                [("ps", pb)], [("kt", jt)])
            yield

        def gla(g, t):
            xs = g % 2
            gb = 6
            PE(lambda e: e.matmul(out=bank(gb, 256), lhsT=GLT[:, t * 128:(t + 1) * 128],
                                  rhs=CST[0:16, C_GUP:C_GUP + 256], start=True, stop=True),
               [("glt", t), "cst"], [("ps", gb)])
            zl = "zl"
            DVE(lambda e: e.scalar_tensor_tensor(out=ZL[:, 0, :], in0=bank(gb, 256), scalar=RSTD2(xs)[:, t:t + 1],
                                                 in1=cst(C_GB, 256), op0=ALU.mult, op1=ALU.add),
                [("ps", gb), ("rstd", xs), "cst"], [zl])
            ACT(lambda e: e.activation(out=ZL[:, 0, :], in_=ZL[:, 0, :], func=AF.Exp, scale=-1.0), [zl], [zl])
            ACT(lambda e: e.activation(out=ZL[:, 0, :], in_=ZL[:, 0, :], func=AF.Ln, bias=1.0), [zl], [zl])
            yield
            PE(lambda e: e.matmul(out=bank(gb, 256), lhsT=tri_f, rhs=ZL[:, 0, :], start=True, stop=True),
               [zl, "cst"], [("ps", gb)], sig=False)
            PE(lambda e: e.matmul(out=bank(gb, 256, 256), lhsT=aft_f, rhs=ZL[:, 0, :], start=True, stop=True),
               [zl, "cst"], [("ps", gb)])
            ACT(lambda e: e.activation(out=EQ[:, :], in_=bank(gb, 256), func=AF.Exp, scale=-1.0 / 16),
                [("ps", gb)], ["eq"])
            ACT(lambda e: e.activation(out=EK[:, :], in_=bank(gb, 256), func=AF.Exp, scale=1.0 / 16),
                [("ps", gb)], ["ek"])
            ACT(lambda e: e.activation(out=ED[:, :], in_=bank(gb, 256, 256), func=AF.Exp, scale=-1.0 / 16),
                [("ps", gb)], ["ed"])
            gq = ("gqk", t)
            DVE(lambda e: e.scalar_tensor_tensor(out=QG[:, :], in0=GQK[:, t, 0:256], scalar=0.125,
                                                 in1=EQ[:, :], op0=ALU.mult, op1=ALU.mult), [gq, "eq"], ["qg"])
            POOL(lambda e: e.tensor_tensor(out=KG[:, :], in0=GQK[:, t, 256:512], in1=EK[:, :], op=ALU.mult),
                 [gq, "ek"], ["kg"])
            POOL(lambda e: e.tensor_tensor(out=KD[:, :], in0=GQK[:, t, 256:512], in1=ED[:, :], op=ALU.mult),
                 [gq, "ed"], ["kd"])
            yield
            for h in range(4):
                PE(lambda e: e.matmul(out=PS[0:64, gb * 512 + 2 * h:gb * 512 + 2 * h + 2],
                                      lhsT=ZL[:, 0, h * 64:(h + 1) * 64], rhs=ones_f[:, 0:2],
                                      start=True, stop=True),
                   [zl, "cst"], [("ps", gb)], sig=(h == 3))
            ACT(lambda e: e.activation(out=EB[:, :, :].rearrange("p h c -> p (h c)"),
                                       in_=PS[0:64, gb * 512:gb * 512 + 8], func=AF.Exp, scale=-1.0 / 16),
                [("ps", gb)], ["eb"])
            yield
            tbf = PS[0:64, gb * 512:(gb + 1) * 512].bitcast(BF16).rearrange("p (k c) -> p k c", k=8)
            for h in range(4):
                PE(lambda e: e.transpose(out=tbf[:, h, :], in_=QG[:, h * 64:(h + 1) * 64],
                                         identity=IDB[:, :]), ["qg", "idb"], [("ps", gb)], sig=False)
                PE(lambda e: e.transpose(out=tbf[:, 4 + h, :], in_=KG[:, h * 64:(h + 1) * 64],
                                         identity=IDB[:, :]), ["kg", "idb"], [("ps", gb)], sig=(h == 3))
            DVE(lambda e: e.tensor_copy(out=QKTG[:, :, :], in_=tbf), [("ps", gb)], ["qktg"])
            yield
            a3 = bank(gb).rearrange("p (h c) -> p h c", h=4)
            for h in range(4):
                PE(lambda e: e.matmul(out=a3[:, h, :], lhsT=QKTG[:, 4 + h, :], rhs=QKTG[:, h, :],
                                      start=True, stop=True), ["qktg"], [("ps", gb)], sig=(h == 3))
            DVE(lambda e: e.tensor_tensor(out=AT[:, :, :], in0=a3,
                                          in1=tri_f[:, None, :].broadcast_to([128, 4, 128]), op=ALU.mult),
                [("ps", gb), "cst"], ["at"])
            yield
            o3 = bank(gb).rearrange("p (h c) -> p h c", h=4)
            for h in range(4):
                PE(lambda e: e.matmul(out=o3[:, h, :], lhsT=AT[:, h, :], rhs=GV[:, t, h * 128:(h + 1) * 128],
                                      start=(h == 0), stop=False, skip_group_check=True),
                   ["at", ("gv", t)], [("ps", gb)], sig=False)
                PE(lambda e: e.matmul(out=o3[:, h, :], lhsT=QKTG[:, h, :], rhs=SB[:, h, :],
                                      start=False, stop=True, skip_group_check=True),
                   ["qktg", "SB"], [("ps", gb)], sig=(h == 3))
            SSB = STAT[:, 276:280]
            LNB = STAT[:, 280:284]
            RNB = STAT[:, 284:288]
            for h in range(4):
                ACT(lambda e: e.activation(out=JUNK[:, 0:128], in_=o3[:, h, :], func=AF.Square,
                                           accum_out=SSB[:, h:h + 1]), [("ps", gb)], ["junk", "ssb"])
            ACT(lambda e: e.activation(out=LNB, in_=SSB, func=AF.Ln, scale=1.0 / 128, bias=EPS), ["ssb"], ["lnb"])
            ACT(lambda e: e.activation(out=RNB, in_=LNB, func=AF.Exp, scale=-0.5), ["lnb"], ["rnb"])
            yield
            for h in range(4):
                DVE(lambda e: e.scalar_tensor_tensor(
                    out=OCB[:, t, h * 128:(h + 1) * 128], in0=o3[:, h, :], scalar=RNB[:, h:h + 1],
                    in1=GGS[:, t, h * 128:(h + 1) * 128], op0=ALU.mult, op1=ALU.mult),
                    [("ps", gb), "rnb", ("ggs", t)], [("ocb", t)])
            yield
            d3 = PS[0:64, gb * 512:(gb + 1) * 512].rearrange("p (h c) -> p h c", h=4)
            for h in range(4):
                PE(lambda e: e.matmul(out=d3[:, h, :], lhsT=KD[:, h * 64:(h + 1) * 64],
                                      rhs=GV[:, t, h * 128:(h + 1) * 128], start=(h == 0), stop=True,
                                      skip_group_check=True),
                   ["kd", ("gv", t)], [("ps", gb)], sig=(h == 3))
            for h in range(4):
                DVE(lambda e: e.scalar_tensor_tensor(out=S[:, h, :], in0=S[:, h, :], scalar=EB[:, h, 0:1],
                                                     in1=d3[:, h, :], op0=ALU.mult, op1=ALU.add),
                    ["S", "eb", ("ps", gb)], ["S"])
            POOL(lambda e: e.tensor_copy(out=SB[:, :, :], in_=S[:, :, :]), ["S"], ["SB"])
            yield

        def chain(*gens):
            for gn in gens:
                yield from gn

        def merge(gens):
            live = list(gens)
            while live:
                for gn in list(live):
                    try:
                        next(gn)
                        yield
                    except StopIteration:
                        live.remove(gn)

        def front_a(g):
            xs = g % 2
            RS = RSTD2(xs)
            for t in range(2):
                ACT(lambda e: e.activation(out=SQ[:, :], in_=XR[:, 0, t, :], func=AF.Square,
                                           accum_out=SS[:, t:t + 1]), [("xf", t)], ["sq", "ss"])
            ACT(lambda e: e.activation(out=LNV, in_=SS, func=AF.Ln, scale=1.0 / D, bias=EPS), ["ss"], ["lnv"])
            ACT(lambda e: e.activation(out=RS, in_=LNV, func=AF.Exp, scale=-0.5), ["lnv"], [("rstd", xs)])
            for t in range(2):
                pb = 7
                pbf = bank_bf(pb).rearrange("p (k c) -> p k c", k=8)
                POOL(lambda e: e.tensor_copy(out=XB[:, 0:512], in_=XR[:, 0, t, 0:512]), [("xf", t)], ["xb0"])
                DVE(lambda e: e.tensor_copy(out=XB[:, 512:1024], in_=XR[:, 0, t, 512:1024]), [("xf", t)], ["xb1"])
                if g + 1 < ngroups:
                    load_xt(g + 1, t)
                yield
                for kc in range(8):
                    PE(lambda e: e.transpose(out=pbf[:, kc, :], in_=XB[:, kc * 128:(kc + 1) * 128],
                                             identity=IDB[:, :]), ["xb%d" % (kc // 4), "idb"], [("ps", pb)],
                       sig=(kc == 7))
                DVE(lambda e: e.tensor_copy(out=XT[:, xs, :, t * 128:(t + 1) * 128], in_=pbf),
                    [("ps", pb)], [("xt", xs, t)])
                yield
            for t in range(2):
                proj(xs, t, 0, 512, QKR[:, t, 0:512], [("qkr", t)])
                yield
                proj(xs, t, 1, 512, QKR[:, t, 512:1024], [("qkr", t)])
                yield

        def front_v(g):
            xs = g % 2
            jt0 = 2 * (g % NG_SEQ)
            for t in range(2):
                proj(xs, t, 2, 512, VX[:, jt0 + t, :, 0:128], [("vx", jt0 + t)])
                yield
                proj(xs, t, 3, 512, DGS[:, xs, t, :], [("dgs", xs, t)] + (["stgB"] if g == 0 else []))
                yield
            for t in range(2):
                ACT(lambda e: e.activation(out=DGS[:, xs, t, :], in_=DGS[:, xs, t, :], func=AF.Silu),
                    [("dgs", xs, t)], [("dgs", xs, t)])
            yield

        def front_p(g):
            yield from chain(prep_qk(g, 0), prep_qk(g, 1))

        def mid_p(g, t):
            xs = g % 2
            proj(xs, t, 4, 512, GQK[:, t, :], [("gqk", t)])
            yield
            proj(xs, t, 5, 512, GV[:, t, :], [("gv", t)])
            yield
            proj(xs, t, 6, 512, GGS[:, t, :], [("ggs", t)])
            yield
            b = pj[0]
            pj[0] ^= 1
            for kc in range(8):
                PE(lambda e: e.matmul(out=PS[0:16, b * 512:b * 512 + 128], lhsT=W[:, kc, 3584:3600],
                                      rhs=XT[:, xs, kc, t * 128:(t + 1) * 128], start=(kc == 0), stop=(kc == 7)),
                   [("xt", xs, t)] + Wlow, [("ps", b)], sig=(kc == 7))
            DVE(lambda e: e.tensor_copy(out=GLT[:, t * 128:(t + 1) * 128], in_=PS[0:16, b * 512:b * 512 + 128]),
                [("ps", b)], [("glt", t)])
            ACT(lambda e: e.activation(out=GGS[:, t, :], in_=GGS[:, t, :], func=AF.Silu),
                [("ggs", t)], [("ggs", t)])
            yield

        def mid_g(g, t):
            if t == 0 and g % NG_SEQ == 0 and g > 0:
                DVE(lambda e: e.memset(S[:, :, :], 0.0), [], ["S"])
                DVE(lambda e: e.memset(SB[:, :, :], 0.0), [], ["SB"])
            yield from gla(g, t)
            pbf = bank_bf(6).rearrange("p (k c) -> p k c", k=8)
            for c in range(4):
                PE(lambda e: e.transpose(out=pbf[:, c, :], in_=OCB[:, t, c * 128:(c + 1) * 128],
                                         identity=IDB[:, :]), [("ocb", t), "idb"], [("ps", 6)], sig=(c == 3))
            DVE(lambda e: e.tensor_copy(out=OCT[:, g % 2, t, 4:8, :], in_=pbf[:, 0:4, :]), [("ps", 6)], [("oct", g % 2, t, 1)])
            yield

        def fin2b(g, h):
            xs = g % 2
            pbf = bank_bf(7).rearrange("p (k c) -> p k c", k=8)
            for qb in range(2):
                PE(lambda e: e.transpose(out=pbf[:, qb, :], in_=OCA[:, qb, h * 128:(h + 1) * 128],
                                         identity=IDB[:, :]), [("oca", h), "idb"], [("ps", 7)], sig=(qb == 1))
            DVE(lambda e: e.tensor_copy(out=OCT[:, xs, :, h, :], in_=pbf[:, 0:2, :]), [("ps", 7)], [("oct", xs, h, 0)])

        def back_heads(g):
            j = g % NG_SEQ
            xs = g % 2
            nkb = 2 * j + 2
            ptb = [0]
            stb = [0]
            oacc = lambda qb, m: (PS[:, 4 * 512 + (2 * qb + m) * 129: 4 * 512 + (2 * qb + m) * 129 + 129]
                                  if (2 * qb + m) < 3 else PS[:, 5 * 512:5 * 512 + 129])
            obank = lambda qb, m: 4 if (2 * qb + m) < 3 else 5
            RL = STAT[:, 256:260]
            NR = STAT[:, 260:264]
            SSA = STAT[:, 264:266]
            LNA = STAT[:, 268:270]
            RNA = STAT[:, 272:274]

            def fin2(h):
                for qb in range(2):
                    ACT(lambda e: e.activation(out=JUNK[:, 0:128], in_=OAH[:, qb, :], func=AF.Square,
                                               accum_out=SSA[:, qb:qb + 1]), ["oah"], ["junk", "ssa"])
                ACT(lambda e: e.activation(out=LNA, in_=SSA, func=AF.Ln, scale=1.0 / 128, bias=EPS), ["ssa"], ["lna"])
                ACT(lambda e: e.activation(out=RNA, in_=LNA, func=AF.Exp, scale=-0.5), ["lna"], ["rna"])
                for qb in range(2):
                    DVE(lambda e: e.scalar_tensor_tensor(
                        out=OCA[:, qb, h * 128:(h + 1) * 128], in0=OAH[:, qb, :], scalar=RNA[:, qb:qb + 1],
                        in1=DGS[:, xs, qb, h * 128:(h + 1) * 128], op0=ALU.mult, op1=ALU.mult),
                        ["oah", "rna", ("dgs", xs, qb)], [("oca", h)])

            pend = None
            pendb = None
            for h in range(4):
                first = {4: True, 5: True}
                ptis = {}

                def qk_exp(kb):
                    q0 = 128 if kb == nkb - 1 else 0
                    sbk = 2 + stb[0]
                    stb[0] ^= 1
                    st3 = bank(sbk).rearrange("p (m c) -> p m c", m=2)
                    if q0 == 0:
                        PE(lambda e: e.matmul(
                            out=bank(sbk), lhsT=KT[:, h, kb * 128:(kb + 1) * 128],
                            rhs=QT[:, xs, h, :, :].rearrange("p m c -> p (m c)"), start=True, stop=True),
                           [("qt", xs), ("kt", kb)], [("ps", sbk)])
                    else:
                        for m in range(2):
                            PE(lambda e: e.matmul(
                                out=st3[:, m, q0:GT], lhsT=KT[:, h, kb * 128:(kb + 1) * 128],
                                rhs=QT[:, xs, h, m, q0:GT], start=True, stop=True),
                               [("qt", xs), ("kt", kb)], [("ps", sbk)], sig=(m == 1))
                    pti = ptb[0]
                    ptb[0] = (ptb[0] + 1) % 3
                    ptis[kb] = pti
                    ptr = ("pt", pti)
                    ACT(lambda e: e.activation(out=PT[:, pti, :, q0:GT], in_=st3[:, :, q0:GT], func=AF.Exp,
                                               scale=0.125), [("ps", sbk)], [ptr])
                    if kb >= nkb - 2:
                        qd = kb - 2 * j
                        DVE(lambda e: e.tensor_tensor(
                            out=PT[:, pti, :, qd * 128:(qd + 1) * 128], in0=PT[:, pti, :, qd * 128:(qd + 1) * 128],
                            in1=TRI2[:, :, :], op=ALU.mult), [ptr, "tri2"], [ptr])

                def pv(kb):
                    pti = ptis[kb]
                    ptr = ("pt", pti)
                    qbs = [1] if kb == nkb - 1 else [0, 1]
                    for qb in qbs:
                        for m in range(2):
                            ob = obank(qb, m)
                            stt = first[ob]
                            first[ob] = False
                            PE(lambda e: e.matmul(
                                out=oacc(qb, m), lhsT=PT[:, pti, m, qb * 128:(qb + 1) * 128],
                                rhs=VX[:, kb, h, 0:129], start=stt, stop=(kb == 2 * j + qb),
                                skip_group_check=True),
                               [ptr, ("vx", kb), "vx_ones"], [("ps", ob)], sig=(qb == qbs[-1] and m == 1))

                qk_exp(0)
                qk_exp(1)
                for kb in range(nkb):
                    if kb + 2 < nkb:
                        qk_exp(kb + 2)
                    pv(kb)
                    if kb == 1 and pend is not None:
                        fin2(pend)
                        pendb = pend
                        pend = None
                    elif kb == min(6, nkb - 1) and pendb is not None:
                        fin2b(g, pendb)
                        pendb = None
                    yield
                if pend is not None:
                    fin2(pend)
                    pendb = pend
                    pend = None
                if pendb is not None:
                    fin2b(g, pendb)
                    pendb = None
                DVE(lambda e: e.reciprocal(out=RL[:, 0:3], in_=PS[:, 4 * 512:4 * 512 + 3 * 129]
                                           .rearrange("p (a c) -> p a c", c=129)[:, :, 128]), [("ps", 4)], ["rl"])
                DVE(lambda e: e.reciprocal(out=RL[:, 3:4], in_=PS[:, 5 * 512 + 128:5 * 512 + 129]),
                    [("ps", 5)], ["rl"])
                DVE(lambda e: e.tensor_scalar(out=NR[:, :], in0=RL[:, :], scalar1=NLAM, scalar2=None, op0=ALU.mult),
                    ["rl", "nlam"], ["nr"])
                for qb in range(2):
                    DVE(lambda e: e.tensor_scalar(out=OAH[:, qb, :], in0=oacc(qb, 1)[:, 0:128],
                                                  scalar1=NR[:, 2 * qb + 1:2 * qb + 2], scalar2=None, op0=ALU.mult),
                        [("ps", obank(qb, 1)), "nr"], ["oah"])
                    DVE(lambda e: e.scalar_tensor_tensor(
                        out=OAH[:, qb, :], in0=oacc(qb, 0)[:, 0:128], scalar=RL[:, 2 * qb:2 * qb + 1],
                        in1=OAH[:, qb, :], op0=ALU.mult, op1=ALU.add),
                        [("ps", obank(qb, 0)), "rl", "oah"], ["oah"])
                pend = h
                yield
            if g - 1 in TL:
                drain(TL[g - 1])
            fin2(pend)
            yield

        def back_tail(g):
            xs = g % 2
            octr = [("oct", xs, h, 0) for h in range(4)]
            load_y(g, 0)
            load_y(g, 1)
            fin2b(g, 3)
            yield
            for t in range(2):
                for n in range(2):
                    b = pj[0]
                    pj[0] ^= 1
                    for kc in range(8):
                        PE(lambda e: e.matmul(out=bank(b), lhsT=OCT[:, xs, t, kc, :], rhs=WO[:, kc, n * 512:(n + 1) * 512],
                                              start=(kc == 0), stop=(kc == 7)),
                           octr + [("oct", xs, t, 1)] + WOres, [("ps", b)], sig=(kc == 7))
                    DVE(lambda e: e.tensor_tensor(out=XR[:, 1, t, n * 512:(n + 1) * 512], in0=bank(b),
                                                  in1=XR[:, 1, t, n * 512:(n + 1) * 512], op=ALU.add),
                        [("ps", b), ("y", t)], [("y", t)])
                    yield
                i = 2 * g + t
                T.dma("sp", out_d[i * 128:(i + 1) * 128, :], XR[:, 1, t, :], [("y", t)], [], "d_o%d" % t)
            yield

        def count_steps(make):
            T.dry = True
            save = (pj[0], evac_rr[0])
            n = sum(1 for _ in make())
            pj[0], evac_rr[0] = save
            T.dry = False
            return max(1, n)

        class Stream:
            def __init__(self, make, prio=1, after=(), ready0=0.0, name=""):
                self.name = name
                self.n = count_steps(make)
                self.gen = make()
                self.k = 0
                self.done = False
                self.prio = prio
                self.after = after
                self.ready = ready0

            def eligible(self):
                ok = (not self.done) and all(a.done for a in self.after)
                if ok and self.k == 0 and self.after:
                    self.ready = max(self.ready, max(a.ready for a in self.after))
                return ok

            def frac(self):
                return (self.k + 0.5) / self.n

            def step(self):
                T.step_t = 0.0
                try:
                    t_pe = T.eng_t["pe"]
                    next(self.gen)
                    self.k += 1
                    self.ready = T.step_t
                    SCHED_LOG.append((self.name, self.k, t_pe, T.step_t))
                except StopIteration:
                    self.done = True

        def drain(st):
            if st.done or T.dry:
                return
            for a_ in st.after:
                drain(a_)
            while not st.done:
                st.step()

        def run_streams(streams):
            while not all(s.done for s in streams):
                el = [s for s in streams if s.eligible()]
                now = T.eng_t["pe"]
                rdy = [s for s in el if s.ready <= now + SCHED['margin']]
                top = [s for s in el if s.prio < 0 and s.ready <= now + SCHED['hold']]
                if top:
                    rdy = top
                if rdy:
                    hp = min(s.prio for s in rdy)
                    pick = min([s for s in rdy if s.prio == hp], key=lambda s: s.frac())
                else:
                    pick = min(el, key=lambda s: s.ready)
                pick.step()

        FA, FP, FV, MP, MG, H, TL = {}, {}, {}, {}, {}, {}, {}

        def dep(*xs):
            return tuple(x for x in xs if x is not None)

        WL1 = Stream(name='WLa', make=lambda: wl_in(0, 4), prio=-2)
        WL2 = Stream(name='WLb', make=lambda: wl_in(4, 11), prio=0, after=(WL1,))
        WL3 = Stream(name='WLc', make=lambda: wl_in(11, 15), prio=0, after=(WL2,))
        WL4 = Stream(name='WLd', make=lambda: wl_out(), prio=0, after=(WL3,))
        for g in range(ngroups):
            new_seq = (g % NG_SEQ == 0 and g > 0)
            FA[g] = Stream(name='FA%d' % g, make=lambda g=g: front_a(g), prio=SCHED['fa'],
                           after=dep(WL1 if g == 0 else None, FA.get(g - 1), FP.get(g - 1), FV.get(g - 1),
                                     MP.get((g - 2, 0)), MP.get((g - 2, 1)), MG.get((g - 2, 1))))
            FP[g] = Stream(name='FP%d' % g, make=lambda g=g: front_p(g), prio=SCHED['fp'],
                           after=dep(FA[g], H.get(g - 2), H.get(g - 1) if new_seq else None))
            FV[g] = Stream(name='FV%d' % g, make=lambda g=g: front_v(g), prio=SCHED['fv'],
                           after=dep(FA[g], H.get(g - 2), H.get(g - 1) if new_seq else None, WL3 if g == 0 else None))
            for t in range(2):
                MP[(g, t)] = Stream(name='MP%d_%d' % (g, t), make=lambda g=g, t=t: mid_p(g, t), prio=SCHED['mp'],
                                    after=dep(FA[g], MG.get((g - 1, t)), WL2 if g == 0 else None))
            MG[(g, 0)] = Stream(name='MG%d_0' % g, make=lambda g=g: mid_g(g, 0), prio=SCHED['mg'],
                                after=dep(MP[(g, 0)], MG.get((g - 1, 1)), TL.get(g - 2)))
            MG[(g, 1)] = Stream(name='MG%d_1' % g, make=lambda g=g: mid_g(g, 1), prio=SCHED['mg'],
                                after=dep(MP[(g, 1)], MG[(g, 0)]))
            H[g] = Stream(name='H%d' % g, make=lambda g=g: back_heads(g), prio=SCHED['heads'],
                          after=dep(H.get(g - 1), FP[g], FV[g], TL.get(g - 2)))
            TL[g] = Stream(name='TL%d' % g, make=lambda g=g: back_tail(g), prio=SCHED['tail'],
                           after=dep(H[g], MG[(g, 1)], TL.get(g - 1), WL4 if g == 0 else None))
        allst = [WL1, WL2, WL3, WL4]
        for g in range(ngroups):
            allst += [FA[g], FP[g], FV[g], MP[(g, 0)], MP[(g, 1)], MG[(g, 0)], MG[(g, 1)], H[g], TL[g]]
        run_streams(allst)

        T.wait_all("sp", ["d_o0", "d_o1"])
        build_program.stats = (T.n_ins, T.n_wait, dict(T.count), dict(T.eng_t))
    return nc


def _host_constants(inp):
    c = np.zeros((128, NCST), np.float32)
    c[:, C_NG:C_NG + 8] = inp["norm_gain"][0].reshape(8, 128).T
    og = np.concatenate([np.tile(inp["diff_out_gain"][0], 4), np.tile(inp["gla_out_gain"][0], 4)])
    c[:, C_OG:C_OG + 8] = og.reshape(8, 128).T
    c[:, C_QKG:C_QKG + 64] = inp["q_norm_gain"][0][None, :]
    c[:, C_QKG + 64:C_QKG + 128] = inp["k_norm_gain"][0][None, :]
    for i, nm in enumerate(["lambda_q1", "lambda_k1", "lambda_q2", "lambda_k2"]):
        c[:, C_LAM + 64 * i:C_LAM + 64 * (i + 1)] = inp[nm][0][None, :]
    c[:, C_GB:C_GB + 256] = inp["gk_bias"][0][None, :]
    pos = np.arange(SEQ, dtype=np.float32)
    inv_freq = (np.float32(500000.0) ** (-np.arange(0, 16, 2, dtype=np.float32) / np.float32(16))).astype(np.float32)
    ang = (pos[:, None] * inv_freq[None, :]).astype(np.float32)
    cos = np.cos(ang).astype(np.float32).reshape(16, 128, 8).transpose(1, 0, 2).reshape(128, 128)
    sin = np.sin(ang).astype(np.float32).reshape(16, 128, 8).transpose(1, 0, 2).reshape(128, 128)
    c[:, C_COS:C_COS + 128] = cos
    c[:, C_SIN:C_SIN + 128] = sin
    p = np.arange(128)
    c[:, C_ID:C_ID + 128] = np.eye(128, dtype=np.float32)
    c[:, C_TRI:C_TRI + 128] = (p[:, None] <= p[None, :]).astype(np.float32)
    c[:, C_AFT:C_AFT + 128] = (p[:, None] > p[None, :]).astype(np.float32)
    c[:, C_ONE:C_ONE + 128] = 1.0
    c[0:16, C_GUP:C_GUP + 256] = inp["gk_up"][0]
    return c


_NC_CACHE = {}


def kernel(**inputs):
    inp = {k: np.asarray(v) for k, v in inputs.items()}
    x = np.ascontiguousarray(inp["x"], dtype=np.float32)
    cst = _host_constants(inp)
    w_in = np.ascontiguousarray(inp["w_in"][0], dtype=np.float32)
    w_out = np.ascontiguousarray(inp["w_out"][0], dtype=np.float32)
    if "nc" not in _NC_CACHE:
        _NC_CACHE["nc"] = build_program()
    nc = _NC_CACHE["nc"]
    in_maps = []
    for c in range(N_CORES):
        xs = x[NSEQ * c:NSEQ * (c + 1)].reshape(NSEQ * SEQ, D)
        in_maps.append({"x": xs, "w_in": w_in, "w_out": w_out, "cst": cst})
    res = run_bass_kernel_spmd(nc, in_maps, core_ids=list(range(N_CORES)))
    outs = [np.asarray(r["out"]).reshape(NSEQ, SEQ, D) for r in res.results]
    return np.concatenate(outs, axis=0).astype(np.float32)
```

```python
import numpy as np
from contextlib import ExitStack

import concourse.bass as bass
import concourse.mybir as mybir
from concourse.bass_utils import run_bass_kernel_spmd

F32 = mybir.dt.float32
BF16 = mybir.dt.bfloat16
AF = mybir.ActivationFunctionType
ALU = mybir.AluOpType
AX = mybir.AxisListType

N_CORES = 8
SCHED_LOG = []
SCHED = {'heads': -1, 'mp': 1, 'mg': 0, 'tail': 0, 'fa': 1, 'fp': 0, 'fv': 1, 'margin': 150.0, 'hold': 0.0}
SEQ = 2048
D = 1024
NSEQ = 2
D_IN = 3600
EPS = 1e-6
LAMBDA_INIT = 0.8 - 0.6
GT = 256
NG_SEQ = SEQ // GT
NGROUPS = NSEQ * NG_SEQ

C_NG = 0
C_OG = 8
C_QKG = 16
C_LAM = 144
C_GB = 400
C_COS = 656
C_SIN = 784
C_ID = 912
C_TRI = 1040
C_AFT = 1168
C_ONE = 1296
C_GUP = 1424
NCST = 1680


class Tok:
    __slots__ = ("key", "val", "t")

    def __init__(self, key, val=None, t=0.0):
        self.key = key
        self.val = val
        self.t = t


def _nfree(ap):
    n = 1
    for d in ap.shape[1:]:
        n *= int(d)
    return n


class _CostProxy:
    def __init__(self, tracker, ename, eng):
        self._t = tracker
        self._e = ename
        self._eng = eng

    def __getattr__(self, name):
        real = getattr(self._eng, name)
        e = self._e

        def call(*a, **kw):
            ap = kw.get("rhs") if e == "pe" and name == "matmul" else None
            if ap is None:
                ap = kw.get("in_", kw.get("in0", kw.get("out", a[0] if a else None)))
            n = _nfree(ap) if ap is not None else 64
            if e == "pe":
                c = max(107.0, 0.42 * n)
                if name == "matmul" and kw["rhs"].dtype == F32:
                    c *= 4.0
            elif e == "act":
                c = 190.0 + 0.83 * n + (90.0 if kw.get("accum_out") is not None else 0.0)
            elif e == "dve":
                two = name in ("tensor_tensor", "scalar_tensor_tensor")
                c = 70.0 + (1.3 if not two else 1.5) * n
                if name == "reciprocal":
                    c = 70.0 + 8.0 * n
            else:
                c = 120.0 + 3.0 * n
            self._t._cost = c
            return real(*a, **kw)

        return call


class Tracker:
    def __init__(self, nc, stack):
        self.nc = nc
        self.stack = stack
        self.eng = {"pe": nc.tensor, "act": nc.scalar, "dve": nc.vector,
                    "pool": nc.gpsimd, "sp": nc.sync}
        self.sems = {}
        self.count = {}
        self.waited = {e: {} for e in self.eng}
        self.pending = {e: [] for e in self.eng}
        self.last_w = {}
        self.readers = {}
        self.n_wait = 0
        self.n_ins = 0
        self.dry = False
        self.proxy = {e: _CostProxy(self, e, en) for e, en in self.eng.items()}
        self.eng_t = {e: 0.0 for e in self.eng}
        self._cost = 100.0
        self._ready = 0.0
        self.step_t = 0.0

    def sem(self, key):
        if key not in self.sems:
            self.sems[key] = self.stack.enter_context(self.nc.semaphore("s_" + key))
            self.count[key] = 0
        return self.sems[key]

    def _deps(self, e, reads, writes, is_dma):
        need = {}
        self._ready = 0.0

        def add(tok, force):
            if tok is None:
                return
            if tok.t > self._ready:
                self._ready = tok.t
            if not force and tok.key == e and e == "pe":
                return
            assert tok.val is not None, ("unresolved token", tok.key, e, reads, writes)
            if need.get(tok.key, 0) < tok.val:
                need[tok.key] = tok.val

        for r in reads:
            add(self.last_w.get(r), True)
        for w in writes:
            add(self.last_w.get(w), is_dma)
            for tok in self.readers.get(w, {}).values():
                add(tok, is_dma)
        for key, val in need.items():
            if self.waited[e].get(key, 0) >= val:
                continue
            self.eng[e].wait_ge(self.sem(key), val)
            self.waited[e][key] = val
            self.n_wait += 1

    def _record(self, tok, reads, writes):
        for w in writes:
            self.last_w[w] = tok
            self.readers[w] = {}
        for r in reads:
            if r in writes:
                continue
            self.readers.setdefault(r, {})[tok.key] = tok

    def op(self, e, fn, reads=(), writes=(), sig=True):
        if self.dry:
            return
        excl = [r for r in reads if isinstance(r, tuple) and r[0] == "ps" and r not in writes]
        if excl:
            writes = list(writes) + excl
        self._deps(e, reads, writes, False)
        ins = fn(self.proxy[e])
        self.n_ins += 1
        fin = max(self.eng_t[e], self._ready + 60.0) + self._cost
        self.eng_t[e] = fin
        if fin > self.step_t:
            self.step_t = fin
        tok = Tok(e, None, fin)
        if sig:
            s = self.sem(e)
            self.count[e] += 1
            ins.then_inc(s, 1)
            tok.val = self.count[e]
            for p in self.pending[e]:
                p.val = tok.val
            self.pending[e] = []
        else:
            self.pending[e].append(tok)
        self._record(tok, reads, writes)

    def dma(self, q, out, in_, reads, writes, semkey):
        if self.dry:
            return
        self._deps(q, reads, writes, True)
        s = self.sem(semkey)
        self.count[semkey] += 16
        self.eng[q].dma_start(out=out, in_=in_).then_inc(s, 16)
        self.n_ins += 1
        fin = max(self.eng_t[q], self._ready + 60.0) + 2500.0 + _nfree(out) * 128 * 4 / 160.0
        self.eng_t[q] = max(self.eng_t[q], self._ready) + 100.0
        tok = Tok(semkey, self.count[semkey], fin)
        self._record(tok, reads, writes)

    def wait_all(self, e, keys):
        for key in keys:
            if key in self.sems and self.waited[e].get(key, 0) < self.count[key]:
                self.eng[e].wait_ge(self.sems[key], self.count[key])
                self.waited[e][key] = self.count[key]


def build_program(ngroups=NGROUPS, upto=99):
    nc = bass.Bass("TRN2", target_bir_lowering=False)
    x_d = nc.dram_tensor("x", [NSEQ * SEQ, D], F32, kind="ExternalInput").ap()
    win_d = nc.dram_tensor("w_in", [D, D_IN], F32, kind="ExternalInput").ap()
    wout_d = nc.dram_tensor("w_out", [D, D], F32, kind="ExternalInput").ap()
    cst_d = nc.dram_tensor("cst", [128, NCST], F32, kind="ExternalInput").ap()
    out_d = nc.dram_tensor("out", [NSEQ * SEQ, D], F32, kind="ExternalOutput").ap()

    with ExitStack() as st:
        def sb(name, shape, dt):
            return st.enter_context(nc.sbuf_tensor(name, shape, dt))

        CST = sb("cst_sb", [128, NCST], F32)
        W = sb("w_sb", [128, 8, D_IN], BF16)
        WO = sb("wo_sb", [128, 8, D], BF16)
        XR = sb("xr", [128, 2, 2, D], F32)
        XB = sb("xb", [128, D], BF16)
        XT = sb("xt", [128, 2, 8, GT], BF16)
        QKR = sb("qkr", [128, 2, 1024], F32)
        SQ = sb("sq", [128, 1024], F32)
        QKB = sb("qkb", [128, 1, 1024], BF16)
        KT = sb("kt", [128, 4, SEQ], BF16)
        VX = sb("vx", [128, 16, 4, 130], BF16)
        QT = sb("qt", [128, 2, 4, 2, GT], BF16)
        DGS = sb("dgs", [128, 2, 2, 512], F32)
        GGS = sb("ggs", [128, 2, 512], F32)
        GQK = sb("gqk", [128, 2, 512], F32)
        GV = sb("gv", [128, 2, 512], BF16)
        GLT = sb("glt", [16, GT], BF16)
        GUPB = sb("gupb", [16, 256], BF16)
        PT = sb("pt", [128, 3, 2, GT], BF16)
        OAH = sb("oah", [128, 2, 128], F32)
        OCA = sb("oca", [128, 2, 512], BF16)
        OCB = sb("ocb", [128, 2, 512], BF16)
        OCT = sb("oct", [128, 2, 2, 8, 128], BF16)
        ZL = sb("zl", [128, 1, 256], F32)
        EQ = sb("eq", [128, 256], F32)
        EK = sb("ek", [128, 256], F32)
        ED = sb("ed", [128, 256], F32)
        QG = sb("qg", [128, 256], BF16)
        KG = sb("kg", [128, 256], BF16)
        KD = sb("kd", [128, 256], BF16)
        QKTG = sb("qktg", [64, 8, 128], BF16)
        AT = sb("at", [128, 4, 128], BF16)
        S = sb("s_st", [64, 4, 128], F32)
        SB = sb("s_bf", [64, 4, 128], BF16)
        EB = sb("eb", [64, 4, 2], F32)
        STAT = sb("stat", [128, 320], F32)
        IDB = sb("idb", [128, 128], BF16)
        TRI2 = sb("tri2", [128, 2, 128], BF16)
        JUNK = sb("junk", [128, 128], BF16)
        PS = st.enter_context(nc.psum_tensor("ps", [128, 8 * 512], F32))

        T = Tracker(nc, st)

        def PE(fn, r=(), w=(), sig=True):
            T.op("pe", fn, r, w, sig)

        def ACT(fn, r=(), w=()):
            T.op("act", fn, r, w)

        def DVE(fn, r=(), w=()):
            T.op("dve", fn, r, w)

        def POOL(fn, r=(), w=()):
            T.op("pool", fn, r, w)

        def bank(b, n=512, off=0):
            return PS[:, b * 512 + off: b * 512 + off + n]

        def bank_bf(b):
            return PS[:, b * 512:(b + 1) * 512].bitcast(BF16)

        cst = lambda c, n: CST[:, c:c + n]
        ident_f = cst(C_ID, 128)
        tri_f = cst(C_TRI, 128)
        aft_f = cst(C_AFT, 128)
        ones_f = cst(C_ONE, 128)

        T.dma("sp", CST[:, :], cst_d[:, :], [], ["cst"], "d_cst")
        if upto >= -2:
            DVE(lambda e: e.tensor_copy(out=IDB[:, :], in_=ident_f), ["cst"], ["idb"])
            DVE(lambda e: e.tensor_copy(out=GUPB[:, :], in_=CST[0:16, C_GUP:C_GUP + 256]), ["cst"], ["gupb"])
            for m in range(2):
                DVE(lambda e, m=m: e.tensor_copy(out=TRI2[:, m, :], in_=tri_f), ["cst"], ["tri2"])
            DVE(lambda e: e.memset(VX[:, :, :, 128:130], 1.0), [], ["vx_ones"])
            DVE(lambda e: e.memset(QT[:, :, :, :, :].rearrange("p a h m c -> p (a h m c)"), 0.0), [], [("qt", 0), ("qt", 1)])
            DVE(lambda e: e.memset(S[:, :, :], 0.0), [], ["S"])
            DVE(lambda e: e.memset(SB[:, :, :], 0.0), [], ["SB"])

            LAMP = STAT[:, 0:128]
            LS = STAT[:, 128:130]
            LE = STAT[:, 130:132]
            NLAM = STAT[:, 132:133]
            for i in range(2):
                DVE(lambda e, i=i: e.tensor_tensor(out=LAMP[:, 0:64], in0=cst(C_LAM + 128 * i, 64),
                                                   in1=cst(C_LAM + 128 * i + 64, 64), op=ALU.mult),
                    ["cst"], ["lamp"])
                DVE(lambda e, i=i: e.tensor_reduce(out=LS[:, i:i + 1], in_=LAMP[:, 0:64], axis=AX.X,
                                                   op=ALU.add), ["lamp"], ["ls"])
            ACT(lambda e: e.activation(out=LE, in_=LS, func=AF.Exp), ["ls"], ["le"])
            DVE(lambda e: e.tensor_tensor(out=NLAM, in0=LE[:, 1:2], in1=LE[:, 0:1], op=ALU.subtract),
                ["le"], ["nlam"])
            DVE(lambda e: e.tensor_scalar(out=NLAM, in0=NLAM, scalar1=-LAMBDA_INIT, scalar2=None,
                                          op0=ALU.add), ["nlam"], ["nlam"])


        stage = lambda k: XR[:, k % 2, :, :].rearrange("p t d -> p (t d)")
        k = 0
        engs = None
        def load_xt(g, t):
            i = 2 * g + t
            wr = [("xf", t)] + ([("X", 0)] if g == 0 else [])
            T.dma("sp", XR[:, 0, t, :], x_d[i * 128:(i + 1) * 128, :], [], wr, "d_xf%d" % t)

        def load_y(g, t):
            i = 2 * g + t
            wr = [("y", t)] + ([("X", 1)] if g == 0 else [])
            T.dma("sp", XR[:, 1, t, :], x_d[i * 128:(i + 1) * 128, :], [], wr, "d_y%d" % t)

        load_xt(0, 0)
        load_xt(0, 1)
        win3 = win_d.rearrange("(kc p) c -> p kc c", p=128)
        stg = [XR[:, 1, :, :].rearrange("p t d -> p (t d)"),
               DGS[:, :, :, :].rearrange("p a t d -> p (a t d)")]
        stg_res = [("X", 1), "stgB"]
        ngb = CST[:, C_NG:C_NG + 8]
        order = [0, 1, 2, 3, 8, 9, 10, 11, 12, 13, 14, 4, 5, 6, 7]

        def wl_in(lo, hi):
            for k in range(lo, hi):
                cb = order[k]
                s_ = k % 2
                ncol = 256 if cb < 14 else 16
                c0 = cb * 256
                st3 = stg[s_][:, 0:8 * ncol].rearrange("p (kc c) -> p kc c", kc=8)
                T.dma("sp", st3, win3[:, :, c0:c0 + ncol], [], [stg_res[s_]], "d_x%d" % s_)
                for half, eng in ((0, DVE), (1, POOL)):
                    eng(lambda e: e.tensor_tensor(
                        out=W[:, 4 * half:4 * half + 4, c0:c0 + ncol], in0=st3[:, 4 * half:4 * half + 4, :],
                        in1=ngb[:, 4 * half:4 * half + 4][:, :, None].broadcast_to([128, 4, ncol]), op=ALU.mult),
                        [stg_res[s_], "cst"], [("W", cb, half)])
                yield

        def wl_out():
            for kc in range(8):
                T.dma("sp", stg[0][:, 0:1024], wout_d[kc * 128:(kc + 1) * 128, :], [], [stg_res[0]], "d_x0")
                gmul = (1.0 - LAMBDA_INIT) if kc < 4 else 1.0
                DVE(lambda e: e.tensor_scalar(out=WO[:, kc, 0:512], in0=stg[0][:, 0:512], scalar1=cst(C_OG + kc, 1),
                                              scalar2=gmul, op0=ALU.mult, op1=ALU.mult),
                    [stg_res[0], "cst"], [("WO", kc, 0)])
                POOL(lambda e: e.tensor_scalar(out=WO[:, kc, 512:1024], in0=stg[0][:, 512:1024],
                                               scalar1=cst(C_OG + kc, 1), scalar2=gmul, op0=ALU.mult, op1=ALU.mult),
                     [stg_res[0], "cst"], [("WO", kc, 1)])
                yield

        Wn = lambda n: [("W", cb, half) for cb in (2 * n, 2 * n + 1) for half in range(2)]
        Wlow = [("W", 14, 0), ("W", 14, 1)]
        WOres = [("WO", kc, q) for kc in range(8) for q in range(2)]

        evac_rr = [0]

        def evac_scaled(out, in_, sc, r, w):
            evac_rr[0] ^= 1
            if False:
                ACT(lambda e: e.activation(out=out, in_=in_, func=AF.Copy, scale=sc), r, w)
            else:
                DVE(lambda e: e.tensor_scalar(out=out, in0=in_, scalar1=sc, scalar2=None,
                                              op0=ALU.mult), r, w)

        pj = [0]
        SS = STAT[:, 136:138]
        LNV = STAT[:, 138:140]
        RSTD2 = lambda xs: STAT[:, 288 + 2 * xs:290 + 2 * xs]

        def proj(xs, t, n, ncols, dst, dst_res):
            b = pj[0]
            pj[0] ^= 1
            XTr = [("xt", xs, t)]
            for kc in range(8):
                PE(lambda e: e.matmul(out=bank(b, ncols), lhsT=XT[:, xs, kc, t * 128:(t + 1) * 128],
                                      rhs=W[:, kc, n * 512:n * 512 + ncols],
                                      start=(kc == 0), stop=(kc == 7)),
                   XTr + Wn(n), [("ps", b)], sig=(kc == 7))
            src = bank(b, ncols)
            if len(dst.shape) == 3:
                src = src.rearrange("p (h c) -> p h c", h=dst.shape[1])
            evac_scaled(dst, src, RSTD2(xs)[:, t:t + 1], [("ps", b), ("rstd", xs)], dst_res)

        def prep_qk(g, t):
            j = g % NG_SEQ
            xs = g % 2
            jt = 2 * j + t
            qk3 = QKR[:, t, :].rearrange("p (g d) -> p g d", d=64)
            SSQ = STAT[:, 144 + 16 * t:160 + 16 * t]
            LNQ = STAT[:, 176 + 16 * t:192 + 16 * t]
            RQ = STAT[:, 208 + 16 * t:224 + 16 * t]
            qr = ("qkr", t)
            ACT(lambda e: e.activation(out=SQ[:, :], in_=QKR[:, t, :], func=AF.Square), [qr], ["sq"])
            DVE(lambda e: e.tensor_reduce(out=SSQ, in_=SQ[:, :].rearrange("p (g d) -> p g d", d=64),
                                          axis=AX.X, op=ALU.add), ["sq"], [("ssq", t)])
            yield
            ACT(lambda e: e.activation(out=LNQ, in_=SSQ, func=AF.Ln, scale=1.0 / 64, bias=EPS),
                [("ssq", t)], [("lnq", t)])
            ACT(lambda e: e.activation(out=RQ, in_=LNQ, func=AF.Exp, scale=-0.5), [("lnq", t)], [("rq", t)])
            POOL(lambda e: e.tensor_tensor(out=qk3, in0=qk3, in1=RQ[:, :, None].broadcast_to([128, 16, 64]),
                                           op=ALU.mult), [qr, ("rq", t)], [qr])
            yield
            gain4 = CST[:, C_QKG:C_QKG + 128].rearrange("p (a d) -> p a d", a=2)[:, :, None, :] \
                .broadcast_to([128, 2, 8, 64])
            qk4 = QKR[:, t, :].rearrange("p (a g d) -> p a g d", a=2, g=8)
            DVE(lambda e: e.tensor_tensor(out=qk4, in0=qk4, in1=gain4, op=ALU.mult), [qr, "cst"], [qr])
            ACT(lambda e: e.activation(out=QKB[:, 0, :], in_=QKR[:, t, :], func=AF.Copy), [qr], ["qkb"])
            yield
            cosb = CST[:, C_COS + jt * 8:C_COS + jt * 8 + 8][:, None, :].broadcast_to([128, 16, 8])
            sinb = CST[:, C_SIN + jt * 8:C_SIN + jt * 8 + 8][:, None, :].broadcast_to([128, 16, 8])
            x1 = qk3[:, :, 0:8]
            x2 = qk3[:, :, 8:16]
            RA = SQ[:, 0:128].rearrange("p (g d) -> p g d", d=8)
            RB = SQ[:, 128:256].rearrange("p (g d) -> p g d", d=8)
            RC = SQ[:, 256:384].rearrange("p (g d) -> p g d", d=8)
            RD = SQ[:, 384:512].rearrange("p (g d) -> p g d", d=8)
            POOL(lambda e: e.tensor_tensor(out=RA, in0=x1, in1=cosb, op=ALU.mult), [qr, "cst"], ["ra", "sq"])
            POOL(lambda e: e.tensor_tensor(out=RB, in0=x2, in1=sinb, op=ALU.mult), [qr, "cst"], ["rb", "sq"])
            POOL(lambda e: e.tensor_tensor(out=RC, in0=x2, in1=cosb, op=ALU.mult), [qr, "cst"], ["rc", "sq"])
            POOL(lambda e: e.tensor_tensor(out=RD, in0=x1, in1=sinb, op=ALU.mult), [qr, "cst"], ["rd", "sq"])
            yield
            qb3 = QKB[:, 0, :].rearrange("p (g d) -> p g d", d=64)
            DVE(lambda e: e.tensor_tensor(out=qb3[:, :, 0:8], in0=RA, in1=RB, op=ALU.subtract),
                ["ra", "rb", "sq"], ["qkb"])
            DVE(lambda e: e.tensor_tensor(out=qb3[:, :, 8:16], in0=RC, in1=RD, op=ALU.add),
                ["rc", "rd", "sq"], ["qkb"])
            yield
            pb = 7
            pbf = bank_bf(pb).rearrange("p (k c) -> p k c", k=8)
            for blk in range(8):
                PE(lambda e: e.transpose(out=pbf[:, blk, :], in_=QKB[:, 0, blk * 128:(blk + 1) * 128],
                                         identity=IDB[:, :]),
                   ["qkb", "idb"], [("ps", pb)], sig=(blk == 7))
            DVE(lambda e: e.tensor_copy(out=QT[0:64, xs, :, 0, t * 128:(t + 1) * 128], in_=pbf[0:64, 0:4, :]),
                [("ps", pb)], [("qt", xs)])
            DVE(lambda e: e.tensor_copy(out=QT[64:128, xs, :, 1, t * 128:(t + 1) * 128], in_=pbf[64:128, 0:4, :]),
                [("ps", pb)], [("qt", xs)])
            DVE(lambda e: e.tensor_copy(out=KT[:, :, jt * 128:(jt + 1) * 128], in_=pbf[:, 4:8, :]),
                [("ps", pb)], [("kt", jt)])
            yield

        def gla(g, t):
            xs = g % 2
            gb = 6
            PE(lambda e: e.matmul(out=bank(gb, 256), lhsT=GLT[:, t * 128:(t + 1) * 128],
                                  rhs=GUPB[:, :], start=True, stop=True),
               [("glt", t), "gupb"], [("ps", gb)])
            zl = "zl"
            DVE(lambda e: e.scalar_tensor_tensor(out=ZL[:, 0, :], in0=bank(gb, 256), scalar=RSTD2(xs)[:, t:t + 1],
                                                 in1=cst(C_GB, 256), op0=ALU.mult, op1=ALU.add),
                [("ps", gb), ("rstd", xs), "cst"], [zl])
            ACT(lambda e: e.activation(out=ZL[:, 0, :], in_=ZL[:, 0, :], func=AF.Exp, scale=-1.0), [zl], [zl])
            ACT(lambda e: e.activation(out=ZL[:, 0, :], in_=ZL[:, 0, :], func=AF.Ln, bias=1.0), [zl], [zl])
            yield
            PE(lambda e: e.matmul(out=bank(gb, 256), lhsT=tri_f, rhs=ZL[:, 0, :], start=True, stop=True),
               [zl, "cst"], [("ps", gb)], sig=False)
            PE(lambda e: e.matmul(out=bank(gb, 256, 256), lhsT=aft_f, rhs=ZL[:, 0, :], start=True, stop=True),
               [zl, "cst"], [("ps", gb)])
            ACT(lambda e: e.activation(out=EQ[:, :], in_=bank(gb, 256), func=AF.Exp, scale=-1.0 / 16),
                [("ps", gb)], ["eq"])
            ACT(lambda e: e.activation(out=EK[:, :], in_=bank(gb, 256), func=AF.Exp, scale=1.0 / 16),
                [("ps", gb)], ["ek"])
            ACT(lambda e: e.activation(out=ED[:, :], in_=bank(gb, 256, 256), func=AF.Exp, scale=-1.0 / 16),
                [("ps", gb)], ["ed"])
            gq = ("gqk", t)
            DVE(lambda e: e.scalar_tensor_tensor(out=QG[:, :], in0=GQK[:, t, 0:256], scalar=0.125,
                                                 in1=EQ[:, :], op0=ALU.mult, op1=ALU.mult), [gq, "eq"], ["qg"])
            POOL(lambda e: e.tensor_tensor(out=KG[:, :], in0=GQK[:, t, 256:512], in1=EK[:, :], op=ALU.mult),
                 [gq, "ek"], ["kg"])
            POOL(lambda e: e.tensor_tensor(out=KD[:, :], in0=GQK[:, t, 256:512], in1=ED[:, :], op=ALU.mult),
                 [gq, "ed"], ["kd"])
            yield
            for h in range(4):
                PE(lambda e: e.matmul(out=PS[0:64, gb * 512 + 2 * h:gb * 512 + 2 * h + 2],
                                      lhsT=ZL[:, 0, h * 64:(h + 1) * 64], rhs=ones_f[:, 0:2],
                                      start=True, stop=True),
                   [zl, "cst"], [("ps", gb)], sig=(h == 3))
            ACT(lambda e: e.activation(out=EB[:, :, :].rearrange("p h c -> p (h c)"),
                                       in_=PS[0:64, gb * 512:gb * 512 + 8], func=AF.Exp, scale=-1.0 / 16),
                [("ps", gb)], ["eb"])
            yield
            tbf = PS[0:64, gb * 512:(gb + 1) * 512].bitcast(BF16).rearrange("p (k c) -> p k c", k=8)
            for h in range(4):
                PE(lambda e: e.transpose(out=tbf[:, h, :], in_=QG[:, h * 64:(h + 1) * 64],
                                         identity=IDB[:, :]), ["qg", "idb"], [("ps", gb)], sig=False)
                PE(lambda e: e.transpose(out=tbf[:, 4 + h, :], in_=KG[:, h * 64:(h + 1) * 64],
                                         identity=IDB[:, :]), ["kg", "idb"], [("ps", gb)], sig=(h == 3))
            DVE(lambda e: e.tensor_copy(out=QKTG[:, :, :], in_=tbf), [("ps", gb)], ["qktg"])
            yield
            a3 = bank(gb).rearrange("p (h c) -> p h c", h=4)
            for h in range(4):
                PE(lambda e: e.matmul(out=a3[:, h, :], lhsT=QKTG[:, 4 + h, :], rhs=QKTG[:, h, :],
                                      start=True, stop=True), ["qktg"], [("ps", gb)], sig=(h == 3))
            DVE(lambda e: e.tensor_tensor(out=AT[:, :, :], in0=a3,
                                          in1=tri_f[:, None, :].broadcast_to([128, 4, 128]), op=ALU.mult),
                [("ps", gb), "cst"], ["at"])
            yield
            o3 = bank(gb).rearrange("p (h c) -> p h c", h=4)
            for h in range(4):
                PE(lambda e: e.matmul(out=o3[:, h, :], lhsT=AT[:, h, :], rhs=GV[:, t, h * 128:(h + 1) * 128],
                                      start=(h == 0), stop=False, skip_group_check=True),
                   ["at", ("gv", t)], [("ps", gb)], sig=False)
                PE(lambda e: e.matmul(out=o3[:, h, :], lhsT=QKTG[:, h, :], rhs=SB[:, h, :],
                                      start=False, stop=True, skip_group_check=True),
                   ["qktg", "SB"], [("ps", gb)], sig=(h == 3))
            SSB = STAT[:, 276:280]
            LNB = STAT[:, 280:284]
            RNB = STAT[:, 284:288]
            for h in range(4):
                ACT(lambda e: e.activation(out=JUNK[:, 0:128], in_=o3[:, h, :], func=AF.Square,
                                           accum_out=SSB[:, h:h + 1]), [("ps", gb)], ["junk", "ssb"])
            ACT(lambda e: e.activation(out=LNB, in_=SSB, func=AF.Ln, scale=1.0 / 128, bias=EPS), ["ssb"], ["lnb"])
            ACT(lambda e: e.activation(out=RNB, in_=LNB, func=AF.Exp, scale=-0.5), ["lnb"], ["rnb"])
            yield
            for h in range(4):
                DVE(lambda e: e.scalar_tensor_tensor(
                    out=OCB[:, t, h * 128:(h + 1) * 128], in0=o3[:, h, :], scalar=RNB[:, h:h + 1],
                    in1=GGS[:, t, h * 128:(h + 1) * 128], op0=ALU.mult, op1=ALU.mult),
                    [("ps", gb), "rnb", ("ggs", t)], [("ocb", t)])
            yield
            d3 = PS[0:64, gb * 512:(gb + 1) * 512].rearrange("p (h c) -> p h c", h=4)
            for h in range(4):
                PE(lambda e: e.matmul(out=d3[:, h, :], lhsT=KD[:, h * 64:(h + 1) * 64],
                                      rhs=GV[:, t, h * 128:(h + 1) * 128], start=(h == 0), stop=True,
                                      skip_group_check=True),
                   ["kd", ("gv", t)], [("ps", gb)], sig=(h == 3))
            for h in range(4):
                DVE(lambda e: e.scalar_tensor_tensor(out=S[:, h, :], in0=S[:, h, :], scalar=EB[:, h, 0:1],
                                                     in1=d3[:, h, :], op0=ALU.mult, op1=ALU.add),
                    ["S", "eb", ("ps", gb)], ["S"])
            POOL(lambda e: e.tensor_copy(out=SB[:, :, :], in_=S[:, :, :]), ["S"], ["SB"])
            yield

        def chain(*gens):
            for gn in gens:
                yield from gn

        def merge(gens):
            live = list(gens)
            while live:
                for gn in list(live):
                    try:
                        next(gn)
                        yield
                    except StopIteration:
                        live.remove(gn)

        def front_a(g):
            xs = g % 2
            RS = RSTD2(xs)
            for t in range(2):
                ACT(lambda e: e.activation(out=SQ[:, :], in_=XR[:, 0, t, :], func=AF.Square,
                                           accum_out=SS[:, t:t + 1]), [("xf", t)], ["sq", "ss"])
            ACT(lambda e: e.activation(out=LNV, in_=SS, func=AF.Ln, scale=1.0 / D, bias=EPS), ["ss"], ["lnv"])
            ACT(lambda e: e.activation(out=RS, in_=LNV, func=AF.Exp, scale=-0.5), ["lnv"], [("rstd", xs)])
            for t in range(2):
                pb = 7
                pbf = bank_bf(pb).rearrange("p (k c) -> p k c", k=8)
                POOL(lambda e: e.tensor_copy(out=XB[:, 0:512], in_=XR[:, 0, t, 0:512]), [("xf", t)], ["xb0"])
                DVE(lambda e: e.tensor_copy(out=XB[:, 512:1024], in_=XR[:, 0, t, 512:1024]), [("xf", t)], ["xb1"])
                if g + 1 < ngroups:
                    load_xt(g + 1, t)
                yield
                for kc in range(8):
                    PE(lambda e: e.transpose(out=pbf[:, kc, :], in_=XB[:, kc * 128:(kc + 1) * 128],
                                             identity=IDB[:, :]), ["xb%d" % (kc // 4), "idb"], [("ps", pb)],
                       sig=(kc == 7))
                DVE(lambda e: e.tensor_copy(out=XT[:, xs, :, t * 128:(t + 1) * 128], in_=pbf),
                    [("ps", pb)], [("xt", xs, t)])
                yield
            for t in range(2):
                proj(xs, t, 0, 512, QKR[:, t, 0:512], [("qkr", t)])
                yield
                proj(xs, t, 1, 512, QKR[:, t, 512:1024], [("qkr", t)])
                yield

        def front_v(g):
            xs = g % 2
            jt0 = 2 * (g % NG_SEQ)
            for t in range(2):
                proj(xs, t, 2, 512, VX[:, jt0 + t, :, 0:128], [("vx", jt0 + t)])
                yield
                proj(xs, t, 3, 512, DGS[:, xs, t, :], [("dgs", xs, t)] + (["stgB"] if g == 0 else []))
                yield
            for t in range(2):
                ACT(lambda e: e.activation(out=DGS[:, xs, t, :], in_=DGS[:, xs, t, :], func=AF.Silu),
                    [("dgs", xs, t)], [("dgs", xs, t)])
            yield

        def front_p(g):
            yield from chain(prep_qk(g, 0), prep_qk(g, 1))

        def mid_p(g, t):
            xs = g % 2
            proj(xs, t, 4, 512, GQK[:, t, :], [("gqk", t)])
            yield
            proj(xs, t, 5, 512, GV[:, t, :], [("gv", t)])
            yield
            proj(xs, t, 6, 512, GGS[:, t, :], [("ggs", t)])
            yield
            b = pj[0]
            pj[0] ^= 1
            for kc in range(8):
                PE(lambda e: e.matmul(out=PS[0:16, b * 512:b * 512 + 128], lhsT=W[:, kc, 3584:3600],
                                      rhs=XT[:, xs, kc, t * 128:(t + 1) * 128], start=(kc == 0), stop=(kc == 7)),
                   [("xt", xs, t)] + Wlow, [("ps", b)], sig=(kc == 7))
            DVE(lambda e: e.tensor_copy(out=GLT[:, t * 128:(t + 1) * 128], in_=PS[0:16, b * 512:b * 512 + 128]),
                [("ps", b)], [("glt", t)])
            ACT(lambda e: e.activation(out=GGS[:, t, :], in_=GGS[:, t, :], func=AF.Silu),
                [("ggs", t)], [("ggs", t)])
            yield

        def mid_g(g, t):
            if t == 0 and g % NG_SEQ == 0 and g > 0:
                DVE(lambda e: e.memset(S[:, :, :], 0.0), [], ["S"])
                DVE(lambda e: e.memset(SB[:, :, :], 0.0), [], ["SB"])
            yield from gla(g, t)
            pbf = bank_bf(6).rearrange("p (k c) -> p k c", k=8)
            for c in range(4):
                PE(lambda e: e.transpose(out=pbf[:, c, :], in_=OCB[:, t, c * 128:(c + 1) * 128],
                                         identity=IDB[:, :]), [("ocb", t), "idb"], [("ps", 6)], sig=(c == 3))
            DVE(lambda e: e.tensor_copy(out=OCT[:, g % 2, t, 4:8, :], in_=pbf[:, 0:4, :]), [("ps", 6)], [("oct", g % 2, t, 1)])
            yield

        def fin2b(g, h):
            xs = g % 2
            pbf = bank_bf(7).rearrange("p (k c) -> p k c", k=8)
            for qb in range(2):
                PE(lambda e: e.transpose(out=pbf[:, qb, :], in_=OCA[:, qb, h * 128:(h + 1) * 128],
                                         identity=IDB[:, :]), [("oca", h), "idb"], [("ps", 7)], sig=(qb == 1))
            DVE(lambda e: e.tensor_copy(out=OCT[:, xs, :, h, :], in_=pbf[:, 0:2, :]), [("ps", 7)], [("oct", xs, h, 0)])

        def back_heads(g):
            j = g % NG_SEQ
            xs = g % 2
            nkb = 2 * j + 2
            ptb = [0]
            stb = [0]
            oacc = lambda qb, m: (PS[:, 4 * 512 + (2 * qb + m) * 129: 4 * 512 + (2 * qb + m) * 129 + 129]
                                  if (2 * qb + m) < 3 else PS[:, 5 * 512:5 * 512 + 129])
            obank = lambda qb, m: 4 if (2 * qb + m) < 3 else 5
            RL = STAT[:, 256:260]
            NR = STAT[:, 260:264]
            SSA = STAT[:, 264:266]
            LNA = STAT[:, 268:270]
            RNA = STAT[:, 272:274]

            def fin2(h):
                for qb in range(2):
                    ACT(lambda e: e.activation(out=JUNK[:, 0:128], in_=OAH[:, qb, :], func=AF.Square,
                                               accum_out=SSA[:, qb:qb + 1]), ["oah"], ["junk", "ssa"])
                ACT(lambda e: e.activation(out=LNA, in_=SSA, func=AF.Ln, scale=1.0 / 128, bias=EPS), ["ssa"], ["lna"])
                ACT(lambda e: e.activation(out=RNA, in_=LNA, func=AF.Exp, scale=-0.5), ["lna"], ["rna"])
                for qb in range(2):
                    DVE(lambda e: e.scalar_tensor_tensor(
                        out=OCA[:, qb, h * 128:(h + 1) * 128], in0=OAH[:, qb, :], scalar=RNA[:, qb:qb + 1],
                        in1=DGS[:, xs, qb, h * 128:(h + 1) * 128], op0=ALU.mult, op1=ALU.mult),
                        ["oah", "rna", ("dgs", xs, qb)], [("oca", h)])

            pend = None
            pendb = None
            for h in range(4):
                first = {4: True, 5: True}
                ptis = {}

                def qk_exp(kb):
                    q0 = 128 if kb == nkb - 1 else 0
                    sbk = 2 + stb[0]
                    stb[0] ^= 1
                    st3 = bank(sbk).rearrange("p (m c) -> p m c", m=2)
                    if q0 == 0:
                        PE(lambda e: e.matmul(
                            out=bank(sbk), lhsT=KT[:, h, kb * 128:(kb + 1) * 128],
                            rhs=QT[:, xs, h, :, :].rearrange("p m c -> p (m c)"), start=True, stop=True),
                           [("qt", xs), ("kt", kb)], [("ps", sbk)])
                    else:
                        for m in range(2):
                            PE(lambda e: e.matmul(
                                out=st3[:, m, q0:GT], lhsT=KT[:, h, kb * 128:(kb + 1) * 128],
                                rhs=QT[:, xs, h, m, q0:GT], start=True, stop=True),
                               [("qt", xs), ("kt", kb)], [("ps", sbk)], sig=(m == 1))
                    pti = ptb[0]
                    ptb[0] = (ptb[0] + 1) % 3
                    ptis[kb] = pti
                    ptr = ("pt", pti)
                    ACT(lambda e: e.activation(out=PT[:, pti, :, q0:GT], in_=st3[:, :, q0:GT], func=AF.Exp,
                                               scale=0.125), [("ps", sbk)], [ptr])
                    if kb >= nkb - 2:
                        qd = kb - 2 * j
                        DVE(lambda e: e.tensor_tensor(
                            out=PT[:, pti, :, qd * 128:(qd + 1) * 128], in0=PT[:, pti, :, qd * 128:(qd + 1) * 128],
                            in1=TRI2[:, :, :], op=ALU.mult), [ptr, "tri2"], [ptr])

                def pv(kb):
                    pti = ptis[kb]
                    ptr = ("pt", pti)
                    qbs = [1] if kb == nkb - 1 else [0, 1]
                    for qb in qbs:
                        for m in range(2):
                            ob = obank(qb, m)
                            stt = first[ob]
                            first[ob] = False
                            PE(lambda e: e.matmul(
                                out=oacc(qb, m), lhsT=PT[:, pti, m, qb * 128:(qb + 1) * 128],
                                rhs=VX[:, kb, h, 0:129], start=stt, stop=(kb == 2 * j + qb),
                                skip_group_check=True),
                               [ptr, ("vx", kb), "vx_ones"], [("ps", ob)], sig=(qb == qbs[-1] and m == 1))

                qk_exp(0)
                qk_exp(1)
                for kb in range(nkb):
                    if kb + 2 < nkb:
                        qk_exp(kb + 2)
                    pv(kb)
                    if kb == 1 and pend is not None:
                        fin2(pend)
                        pendb = pend
                        pend = None
                    elif kb == min(6, nkb - 1) and pendb is not None:
                        fin2b(g, pendb)
                        pendb = None
                    yield
                if pend is not None:
                    fin2(pend)
                    pendb = pend
                    pend = None
                if pendb is not None:
                    fin2b(g, pendb)
                    pendb = None
                DVE(lambda e: e.reciprocal(out=RL[:, 0:3], in_=PS[:, 4 * 512:4 * 512 + 3 * 129]
                                           .rearrange("p (a c) -> p a c", c=129)[:, :, 128]), [("ps", 4)], ["rl"])
                DVE(lambda e: e.reciprocal(out=RL[:, 3:4], in_=PS[:, 5 * 512 + 128:5 * 512 + 129]),
                    [("ps", 5)], ["rl"])
                DVE(lambda e: e.tensor_scalar(out=NR[:, :], in0=RL[:, :], scalar1=NLAM, scalar2=None, op0=ALU.mult),
                    ["rl", "nlam"], ["nr"])
                for qb in range(2):
                    DVE(lambda e: e.tensor_scalar(out=OAH[:, qb, :], in0=oacc(qb, 1)[:, 0:128],
                                                  scalar1=NR[:, 2 * qb + 1:2 * qb + 2], scalar2=None, op0=ALU.mult),
                        [("ps", obank(qb, 1)), "nr"], ["oah"])
                    DVE(lambda e: e.scalar_tensor_tensor(
                        out=OAH[:, qb, :], in0=oacc(qb, 0)[:, 0:128], scalar=RL[:, 2 * qb:2 * qb + 1],
                        in1=OAH[:, qb, :], op0=ALU.mult, op1=ALU.add),
                        [("ps", obank(qb, 0)), "rl", "oah"], ["oah"])
                pend = h
                yield
            if g - 1 in TL:
                drain(TL[g - 1])
            fin2(pend)
            yield

        def back_tail(g):
            xs = g % 2
            octr = [("oct", xs, h, 0) for h in range(4)]
            load_y(g, 0)
            load_y(g, 1)
            fin2b(g, 3)
            yield
            for t in range(2):
                for n in range(2):
                    b = pj[0]
                    pj[0] ^= 1
                    for kc in range(8):
                        PE(lambda e: e.matmul(out=bank(b), lhsT=OCT[:, xs, t, kc, :], rhs=WO[:, kc, n * 512:(n + 1) * 512],
                                              start=(kc == 0), stop=(kc == 7)),
                           octr + [("oct", xs, t, 1)] + WOres, [("ps", b)], sig=(kc == 7))
                    DVE(lambda e: e.tensor_tensor(out=XR[:, 1, t, n * 512:(n + 1) * 512], in0=bank(b),
                                                  in1=XR[:, 1, t, n * 512:(n + 1) * 512], op=ALU.add),
                        [("ps", b), ("y", t)], [("y", t)])
                    yield
                i = 2 * g + t
                T.dma("sp", out_d[i * 128:(i + 1) * 128, :], XR[:, 1, t, :], [("y", t)], [], "d_o%d" % t)
            yield

        def count_steps(make):
            T.dry = True
            save = (pj[0], evac_rr[0])
            n = sum(1 for _ in make())
            pj[0], evac_rr[0] = save
            T.dry = False
            return max(1, n)

        class Stream:
            def __init__(self, make, prio=1, after=(), ready0=0.0, name=""):
                self.name = name
                self.n = count_steps(make)
                self.gen = make()
                self.k = 0
                self.done = False
                self.prio = prio
                self.after = after
                self.ready = ready0

            def eligible(self):
                ok = (not self.done) and all(a.done for a in self.after)
                if ok and self.k == 0 and self.after:
                    self.ready = max(self.ready, max(a.ready for a in self.after))
                return ok

            def frac(self):
                return (self.k + 0.5) / self.n

            def step(self):
                T.step_t = 0.0
                try:
                    t_pe = T.eng_t["pe"]
                    next(self.gen)
                    self.k += 1
                    self.ready = T.step_t
                    SCHED_LOG.append((self.name, self.k, t_pe, T.step_t))
                except StopIteration:
                    self.done = True

        def drain(st):
            if st.done or T.dry:
                return
            for a_ in st.after:
                drain(a_)
            while not st.done:
                st.step()

        def run_streams(streams):
            while not all(s.done for s in streams):
                el = [s for s in streams if s.eligible()]
                now = T.eng_t["pe"]
                rdy = [s for s in el if s.ready <= now + SCHED['margin']]
                top = [s for s in el if s.prio < 0 and s.ready <= now + SCHED['hold']]
                if top:
                    rdy = top
                if rdy:
                    hp = min(s.prio for s in rdy)
                    pick = min([s for s in rdy if s.prio == hp], key=lambda s: s.frac())
                else:
                    pick = min(el, key=lambda s: s.ready)
                pick.step()

        FA, FP, FV, MP, MG, H, TL = {}, {}, {}, {}, {}, {}, {}

        def dep(*xs):
            return tuple(x for x in xs if x is not None)

        WL1 = Stream(name='WLa', make=lambda: wl_in(0, 4), prio=-2)
        WL2 = Stream(name='WLb', make=lambda: wl_in(4, 11), prio=0, after=(WL1,))
        WL3 = Stream(name='WLc', make=lambda: wl_in(11, 15), prio=0, after=(WL2,))
        WL4 = Stream(name='WLd', make=lambda: wl_out(), prio=0, after=(WL3,))
        for g in range(ngroups):
            new_seq = (g % NG_SEQ == 0 and g > 0)
            FA[g] = Stream(name='FA%d' % g, make=lambda g=g: front_a(g), prio=SCHED['fa'],
                           after=dep(WL1 if g == 0 else None, FA.get(g - 1), FP.get(g - 1), FV.get(g - 1),
                                     MP.get((g - 2, 0)), MP.get((g - 2, 1)), MG.get((g - 2, 1))))
            FP[g] = Stream(name='FP%d' % g, make=lambda g=g: front_p(g), prio=SCHED['fp'],
                           after=dep(FA[g], H.get(g - 2), H.get(g - 1) if new_seq else None))
            FV[g] = Stream(name='FV%d' % g, make=lambda g=g: front_v(g), prio=SCHED['fv'],
                           after=dep(FA[g], H.get(g - 2), H.get(g - 1) if new_seq else None, WL3 if g == 0 else None))
            for t in range(2):
                MP[(g, t)] = Stream(name='MP%d_%d' % (g, t), make=lambda g=g, t=t: mid_p(g, t), prio=SCHED['mp'],
                                    after=dep(FA[g], MG.get((g - 1, t)), WL2 if g == 0 else None))
            MG[(g, 0)] = Stream(name='MG%d_0' % g, make=lambda g=g: mid_g(g, 0), prio=SCHED['mg'],
                                after=dep(MP[(g, 0)], MG.get((g - 1, 1)), TL.get(g - 2)))
            MG[(g, 1)] = Stream(name='MG%d_1' % g, make=lambda g=g: mid_g(g, 1), prio=SCHED['mg'],
                                after=dep(MP[(g, 1)], MG[(g, 0)]))
            H[g] = Stream(name='H%d' % g, make=lambda g=g: back_heads(g), prio=SCHED['heads'],
                          after=dep(H.get(g - 1), FP[g], FV[g], TL.get(g - 2)))
            TL[g] = Stream(name='TL%d' % g, make=lambda g=g: back_tail(g), prio=SCHED['tail'],
                           after=dep(H[g], MG[(g, 1)], TL.get(g - 1), WL4 if g == 0 else None))
        allst = [WL1, WL2, WL3, WL4]
        for g in range(ngroups):
            allst += [FA[g], FP[g], FV[g], MP[(g, 0)], MP[(g, 1)], MG[(g, 0)], MG[(g, 1)], H[g], TL[g]]
        run_streams(allst)

        T.wait_all("sp", ["d_o0", "d_o1"])
        build_program.stats = (T.n_ins, T.n_wait, dict(T.count), dict(T.eng_t))
    return nc


def _host_constants(inp):
    c = np.zeros((128, NCST), np.float32)
    c[:, C_NG:C_NG + 8] = inp["norm_gain"][0].reshape(8, 128).T
    og = np.concatenate([np.tile(inp["diff_out_gain"][0], 4), np.tile(inp["gla_out_gain"][0], 4)])
    c[:, C_OG:C_OG + 8] = og.reshape(8, 128).T
    c[:, C_QKG:C_QKG + 64] = inp["q_norm_gain"][0][None, :]
    c[:, C_QKG + 64:C_QKG + 128] = inp["k_norm_gain"][0][None, :]
    for i, nm in enumerate(["lambda_q1", "lambda_k1", "lambda_q2", "lambda_k2"]):
        c[:, C_LAM + 64 * i:C_LAM + 64 * (i + 1)] = inp[nm][0][None, :]
    c[:, C_GB:C_GB + 256] = inp["gk_bias"][0][None, :]
    pos = np.arange(SEQ, dtype=np.float32)
    inv_freq = (np.float32(500000.0) ** (-np.arange(0, 16, 2, dtype=np.float32) / np.float32(16))).astype(np.float32)
    ang = (pos[:, None] * inv_freq[None, :]).astype(np.float32)
    cos = np.cos(ang).astype(np.float32).reshape(16, 128, 8).transpose(1, 0, 2).reshape(128, 128)
    sin = np.sin(ang).astype(np.float32).reshape(16, 128, 8).transpose(1, 0, 2).reshape(128, 128)
    c[:, C_COS:C_COS + 128] = cos
    c[:, C_SIN:C_SIN + 128] = sin
    p = np.arange(128)
    c[:, C_ID:C_ID + 128] = np.eye(128, dtype=np.float32)
    c[:, C_TRI:C_TRI + 128] = (p[:, None] <= p[None, :]).astype(np.float32)
    c[:, C_AFT:C_AFT + 128] = (p[:, None] > p[None, :]).astype(np.float32)
    c[:, C_ONE:C_ONE + 128] = 1.0
    c[0:16, C_GUP:C_GUP + 256] = inp["gk_up"][0]
    return c


_NC_CACHE = {}


def kernel(**inputs):
    inp = {k: np.asarray(v) for k, v in inputs.items()}
    x = np.ascontiguousarray(inp["x"], dtype=np.float32)
    cst = _host_constants(inp)
    w_in = np.ascontiguousarray(inp["w_in"][0], dtype=np.float32)
    w_out = np.ascontiguousarray(inp["w_out"][0], dtype=np.float32)
    if "nc" not in _NC_CACHE:
        _NC_CACHE["nc"] = build_program()
    nc = _NC_CACHE["nc"]
    in_maps = []
    for c in range(N_CORES):
        xs = x[NSEQ * c:NSEQ * (c + 1)].reshape(NSEQ * SEQ, D)
        in_maps.append({"x": xs, "w_in": w_in, "w_out": w_out, "cst": cst})
    res = run_bass_kernel_spmd(nc, in_maps, core_ids=list(range(N_CORES)))
    outs = [np.asarray(r["out"]).reshape(NSEQ, SEQ, D) for r in res.results]
    return np.concatenate(outs, axis=0).astype(np.float32)
```

```python
import numpy as np
from contextlib import ExitStack

import concourse.bass as bass
import concourse.mybir as mybir
from concourse.bass_utils import run_bass_kernel_spmd

F32 = mybir.dt.float32
BF16 = mybir.dt.bfloat16
AF = mybir.ActivationFunctionType
ALU = mybir.AluOpType
AX = mybir.AxisListType

N_CORES = 8
SCHED_LOG = []
SCHED = {'heads': -1, 'mp': 1, 'mg': 0, 'tail': 0, 'fa': 1, 'fp': 0, 'fv': 1, 'margin': 150.0, 'hold': 0.0}
SEQ = 2048
D = 1024
NSEQ = 2
D_IN = 3600
EPS = 1e-6
LAMBDA_INIT = 0.8 - 0.6
GT = 256
NG_SEQ = SEQ // GT
NGROUPS = NSEQ * NG_SEQ

C_NG = 0
C_OG = 8
C_QKG = 16
C_LAM = 144
C_GB = 400
C_COS = 656
C_SIN = 784
C_ID = 912
C_TRI = 1040
C_AFT = 1168
C_ONE = 1296
C_GUP = 1424
NCST = 1680


class Tok:
    __slots__ = ("key", "val", "t")

    def __init__(self, key, val=None, t=0.0):
        self.key = key
        self.val = val
        self.t = t


def _nfree(ap):
    n = 1
    for d in ap.shape[1:]:
        n *= int(d)
    return n


class _CostProxy:
    def __init__(self, tracker, ename, eng):
        self._t = tracker
        self._e = ename
        self._eng = eng

    def __getattr__(self, name):
        real = getattr(self._eng, name)
        e = self._e

        def call(*a, **kw):
            ap = kw.get("rhs") if e == "pe" and name == "matmul" else None
            if ap is None:
                ap = kw.get("in_", kw.get("in0", kw.get("out", a[0] if a else None)))
            n = _nfree(ap) if ap is not None else 64
            if e == "pe":
                c = max(107.0, 0.42 * n)
                if name == "matmul" and kw["rhs"].dtype == F32:
                    c *= 4.0
            elif e == "act":
                c = 190.0 + 0.83 * n + (90.0 if kw.get("accum_out") is not None else 0.0)
            elif e == "dve":
                two = name in ("tensor_tensor", "scalar_tensor_tensor")
                c = 70.0 + (1.3 if not two else 1.5) * n
                if name == "reciprocal":
                    c = 70.0 + 8.0 * n
            else:
                c = 120.0 + 3.0 * n
            self._t._cost = c
            return real(*a, **kw)

        return call


class Tracker:
    def __init__(self, nc, stack):
        self.nc = nc
        self.stack = stack
        self.eng = {"pe": nc.tensor, "act": nc.scalar, "dve": nc.vector,
                    "pool": nc.gpsimd, "sp": nc.sync}
        self.sems = {}
        self.count = {}
        self.waited = {e: {} for e in self.eng}
        self.pending = {e: [] for e in self.eng}
        self.last_w = {}
        self.readers = {}
        self.n_wait = 0
        self.n_ins = 0
        self.dry = False
        self.proxy = {e: _CostProxy(self, e, en) for e, en in self.eng.items()}
        self.eng_t = {e: 0.0 for e in self.eng}
        self._cost = 100.0
        self._ready = 0.0
        self.step_t = 0.0

    def sem(self, key):
        if key not in self.sems:
            self.sems[key] = self.stack.enter_context(self.nc.semaphore("s_" + key))
            self.count[key] = 0
        return self.sems[key]

    def _deps(self, e, reads, writes, is_dma):
        need = {}
        self._ready = 0.0

        def add(tok, force):
            if tok is None:
                return
            if tok.t > self._ready:
                self._ready = tok.t
            if not force and tok.key == e and e == "pe":
                return
            assert tok.val is not None, ("unresolved token", tok.key, e, reads, writes)
            if need.get(tok.key, 0) < tok.val:
                need[tok.key] = tok.val

        for r in reads:
            add(self.last_w.get(r), True)
        for w in writes:
            add(self.last_w.get(w), is_dma)
            for tok in self.readers.get(w, {}).values():
                add(tok, is_dma)
        for key, val in need.items():
            if self.waited[e].get(key, 0) >= val:
                continue
            self.eng[e].wait_ge(self.sem(key), val)
            self.waited[e][key] = val
            self.n_wait += 1

    def _record(self, tok, reads, writes):
        for w in writes:
            self.last_w[w] = tok
            self.readers[w] = {}
        for r in reads:
            if r in writes:
                continue
            self.readers.setdefault(r, {})[tok.key] = tok

    def op(self, e, fn, reads=(), writes=(), sig=True):
        if self.dry:
            return
        excl = [r for r in reads if isinstance(r, tuple) and r[0] == "ps" and r not in writes]
        if excl:
            writes = list(writes) + excl
        self._deps(e, reads, writes, False)
        ins = fn(self.proxy[e])
        self.n_ins += 1
        fin = max(self.eng_t[e], self._ready + 60.0) + self._cost
        self.eng_t[e] = fin
        if fin > self.step_t:
            self.step_t = fin
        tok = Tok(e, None, fin)
        if sig:
            s = self.sem(e)
            self.count[e] += 1
            ins.then_inc(s, 1)
            tok.val = self.count[e]
            for p in self.pending[e]:
                p.val = tok.val
            self.pending[e] = []
        else:
            self.pending[e].append(tok)
        self._record(tok, reads, writes)

    def dma(self, q, out, in_, reads, writes, semkey):
        if self.dry:
            return
        self._deps(q, reads, writes, True)
        s = self.sem(semkey)
        self.count[semkey] += 16
        self.eng[q].dma_start(out=out, in_=in_).then_inc(s, 16)
        self.n_ins += 1
        fin = max(self.eng_t[q], self._ready + 60.0) + 2500.0 + _nfree(out) * 128 * 4 / 160.0
        self.eng_t[q] = max(self.eng_t[q], self._ready) + 100.0
        tok = Tok(semkey, self.count[semkey], fin)
        self._record(tok, reads, writes)

    def wait_all(self, e, keys):
        for key in keys:
            if key in self.sems and self.waited[e].get(key, 0) < self.count[key]:
                self.eng[e].wait_ge(self.sems[key], self.count[key])
                self.waited[e][key] = self.count[key]


def build_program(ngroups=NGROUPS, upto=99):
    nc = bass.Bass("TRN2", target_bir_lowering=False)
    x_d = nc.dram_tensor("x", [NSEQ * SEQ, D], F32, kind="ExternalInput").ap()
    win_d = nc.dram_tensor("w_in", [D, D_IN], F32, kind="ExternalInput").ap()
    wout_d = nc.dram_tensor("w_out", [D, D], F32, kind="ExternalInput").ap()
    cst_d = nc.dram_tensor("cst", [128, NCST], F32, kind="ExternalInput").ap()
    out_d = nc.dram_tensor("out", [NSEQ * SEQ, D], F32, kind="ExternalOutput").ap()

    with ExitStack() as st:
        def sb(name, shape, dt):
            return st.enter_context(nc.sbuf_tensor(name, shape, dt))

        CST = sb("cst_sb", [128, NCST], F32)
        W = sb("w_sb", [128, 8, D_IN], BF16)
        WO = sb("wo_sb", [128, 8, D], BF16)
        XR = sb("xr", [128, 2, 2, D], F32)
        XB = sb("xb", [128, D], BF16)
        XT = sb("xt", [128, 2, 8, GT], BF16)
        QKR = sb("qkr", [128, 2, 1024], F32)
        SQ = sb("sq", [128, 1024], F32)
        QKB = sb("qkb", [128, 1, 1024], BF16)
        KT = sb("kt", [128, 4, SEQ], BF16)
        VX = sb("vx", [128, 16, 4, 130], BF16)
        QT = sb("qt", [128, 2, 4, 2, GT], BF16)
        DGS = sb("dgs", [128, 2, 2, 512], F32)
        GGS = sb("ggs", [128, 2, 512], F32)
        GQK = sb("gqk", [128, 2, 512], F32)
        GV = sb("gv", [128, 2, 512], BF16)
        GLT = sb("glt", [16, GT], F32)
        PT = sb("pt", [128, 3, 2, GT], BF16)
        OAH = sb("oah", [128, 2, 128], F32)
        OCA = sb("oca", [128, 2, 512], BF16)
        OCB = sb("ocb", [128, 2, 512], BF16)
        OCT = sb("oct", [128, 2, 2, 8, 128], BF16)
        ZL = sb("zl", [128, 1, 256], F32)
        EQ = sb("eq", [128, 256], F32)
        EK = sb("ek", [128, 256], F32)
        ED = sb("ed", [128, 256], F32)
        QG = sb("qg", [128, 256], BF16)
        KG = sb("kg", [128, 256], BF16)
        KD = sb("kd", [128, 256], BF16)
        QKTG = sb("qktg", [64, 8, 128], BF16)
        AT = sb("at", [128, 4, 128], BF16)
        S = sb("s_st", [64, 4, 128], F32)
        SB = sb("s_bf", [64, 4, 128], BF16)
        EB = sb("eb", [64, 4, 2], F32)
        STAT = sb("stat", [128, 320], F32)
        IDB = sb("idb", [128, 128], BF16)
        TRI2 = sb("tri2", [128, 2, 128], BF16)
        JUNK = sb("junk", [128, 128], BF16)
        PS = st.enter_context(nc.psum_tensor("ps", [128, 8 * 512], F32))

        T = Tracker(nc, st)

        def PE(fn, r=(), w=(), sig=True):
            T.op("pe", fn, r, w, sig)

        def ACT(fn, r=(), w=()):
            T.op("act", fn, r, w)

        def DVE(fn, r=(), w=()):
            T.op("dve", fn, r, w)

        def POOL(fn, r=(), w=()):
            T.op("pool", fn, r, w)

        def bank(b, n=512, off=0):
            return PS[:, b * 512 + off: b * 512 + off + n]

        def bank_bf(b):
            return PS[:, b * 512:(b + 1) * 512].bitcast(BF16)

        cst = lambda c, n: CST[:, c:c + n]
        ident_f = cst(C_ID, 128)
        tri_f = cst(C_TRI, 128)
        aft_f = cst(C_AFT, 128)
        ones_f = cst(C_ONE, 128)

        T.dma("sp", CST[:, :], cst_d[:, :], [], ["cst"], "d_cst")
        if upto >= -2:
            DVE(lambda e: e.tensor_copy(out=IDB[:, :], in_=ident_f), ["cst"], ["idb"])
            for m in range(2):
                DVE(lambda e, m=m: e.tensor_copy(out=TRI2[:, m, :], in_=tri_f), ["cst"], ["tri2"])
            DVE(lambda e: e.memset(VX[:, :, :, 128:130], 1.0), [], ["vx_ones"])
            DVE(lambda e: e.memset(QT[:, :, :, :, :].rearrange("p a h m c -> p (a h m c)"), 0.0), [], [("qt", 0), ("qt", 1)])
            DVE(lambda e: e.memset(S[:, :, :], 0.0), [], ["S"])
            DVE(lambda e: e.memset(SB[:, :, :], 0.0), [], ["SB"])

            LAMP = STAT[:, 0:128]
            LS = STAT[:, 128:130]
            LE = STAT[:, 130:132]
            NLAM = STAT[:, 132:133]
            for i in range(2):
                DVE(lambda e, i=i: e.tensor_tensor(out=LAMP[:, 0:64], in0=cst(C_LAM + 128 * i, 64),
                                                   in1=cst(C_LAM + 128 * i + 64, 64), op=ALU.mult),
                    ["cst"], ["lamp"])
                DVE(lambda e, i=i: e.tensor_reduce(out=LS[:, i:i + 1], in_=LAMP[:, 0:64], axis=AX.X,
                                                   op=ALU.add), ["lamp"], ["ls"])
            ACT(lambda e: e.activation(out=LE, in_=LS, func=AF.Exp), ["ls"], ["le"])
            DVE(lambda e: e.tensor_tensor(out=NLAM, in0=LE[:, 1:2], in1=LE[:, 0:1], op=ALU.subtract),
                ["le"], ["nlam"])
            DVE(lambda e: e.tensor_scalar(out=NLAM, in0=NLAM, scalar1=-LAMBDA_INIT, scalar2=None,
                                          op0=ALU.add), ["nlam"], ["nlam"])


        stage = lambda k: XR[:, k % 2, :, :].rearrange("p t d -> p (t d)")
        k = 0
        engs = None
        def load_xt(g, t):
            i = 2 * g + t
            wr = [("xf", t)] + ([("X", 0)] if g == 0 else [])
            T.dma("sp", XR[:, 0, t, :], x_d[i * 128:(i + 1) * 128, :], [], wr, "d_xf%d" % t)

        def load_y(g, t):
            i = 2 * g + t
            wr = [("y", t)] + ([("X", 1)] if g == 0 else [])
            T.dma("sp", XR[:, 1, t, :], x_d[i * 128:(i + 1) * 128, :], [], wr, "d_y%d" % t)

        load_xt(0, 0)
        load_xt(0, 1)
        win3 = win_d.rearrange("(kc p) c -> p kc c", p=128)
        stg = [XR[:, 1, :, :].rearrange("p t d -> p (t d)"),
               DGS[:, :, :, :].rearrange("p a t d -> p (a t d)")]
        stg_res = [("X", 1), "stgB"]
        ngb = CST[:, C_NG:C_NG + 8]
        order = [0, 1, 2, 3, 8, 9, 10, 11, 12, 13, 14, 4, 5, 6, 7]

        def wl_in(lo, hi):
            for k in range(lo, hi):
                cb = order[k]
                s_ = k % 2
                ncol = 256 if cb < 14 else 16
                c0 = cb * 256
                st3 = stg[s_][:, 0:8 * ncol].rearrange("p (kc c) -> p kc c", kc=8)
                T.dma("sp", st3, win3[:, :, c0:c0 + ncol], [], [stg_res[s_]], "d_x%d" % s_)
                for half, eng in ((0, DVE), (1, POOL)):
                    eng(lambda e: e.tensor_tensor(
                        out=W[:, 4 * half:4 * half + 4, c0:c0 + ncol], in0=st3[:, 4 * half:4 * half + 4, :],
                        in1=ngb[:, 4 * half:4 * half + 4][:, :, None].broadcast_to([128, 4, ncol]), op=ALU.mult),
                        [stg_res[s_], "cst"], [("W", cb, half)])
                yield

        def wl_out():
            for kc in range(8):
                T.dma("sp", stg[0][:, 0:1024], wout_d[kc * 128:(kc + 1) * 128, :], [], [stg_res[0]], "d_x0")
                gmul = (1.0 - LAMBDA_INIT) if kc < 4 else 1.0
                DVE(lambda e: e.tensor_scalar(out=WO[:, kc, 0:512], in0=stg[0][:, 0:512], scalar1=cst(C_OG + kc, 1),
                                              scalar2=gmul, op0=ALU.mult, op1=ALU.mult),
                    [stg_res[0], "cst"], [("WO", kc, 0)])
                POOL(lambda e: e.tensor_scalar(out=WO[:, kc, 512:1024], in0=stg[0][:, 512:1024],
                                               scalar1=cst(C_OG + kc, 1), scalar2=gmul, op0=ALU.mult, op1=ALU.mult),
                     [stg_res[0], "cst"], [("WO", kc, 1)])
                yield

        Wn = lambda n: [("W", cb, half) for cb in (2 * n, 2 * n + 1) for half in range(2)]
        Wlow = [("W", 14, 0), ("W", 14, 1)]
        WOres = [("WO", kc, q) for kc in range(8) for q in range(2)]

        evac_rr = [0]

        def evac_scaled(out, in_, sc, r, w):
            evac_rr[0] ^= 1
            if False:
                ACT(lambda e: e.activation(out=out, in_=in_, func=AF.Copy, scale=sc), r, w)
            else:
                DVE(lambda e: e.tensor_scalar(out=out, in0=in_, scalar1=sc, scalar2=None,
                                              op0=ALU.mult), r, w)

        pj = [0]
        POOL3 = [0, 1, 7]
        SS = STAT[:, 136:138]
        LNV = STAT[:, 138:140]
        RSTD2 = lambda xs: STAT[:, 288 + 2 * xs:290 + 2 * xs]

        def proj(xs, t, n, ncols, dst, dst_res):
            b = POOL3[pj[0]]
            pj[0] = (pj[0] + 1) % 3
            XTr = [("xt", xs, t)]
            for kc in range(8):
                PE(lambda e: e.matmul(out=bank(b, ncols), lhsT=XT[:, xs, kc, t * 128:(t + 1) * 128],
                                      rhs=W[:, kc, n * 512:n * 512 + ncols],
                                      start=(kc == 0), stop=(kc == 7)),
                   XTr + Wn(n), [("ps", b)], sig=(kc == 7))
            src = bank(b, ncols)
            if len(dst.shape) == 3:
                src = src.rearrange("p (h c) -> p h c", h=dst.shape[1])
            evac_scaled(dst, src, RSTD2(xs)[:, t:t + 1], [("ps", b), ("rstd", xs)], dst_res)

        def prep_qk(g, t):
            j = g % NG_SEQ
            xs = g % 2
            jt = 2 * j + t
            qk3 = QKR[:, t, :].rearrange("p (g d) -> p g d", d=64)
            SSQ = STAT[:, 144 + 16 * t:160 + 16 * t]
            LNQ = STAT[:, 176 + 16 * t:192 + 16 * t]
            RQ = STAT[:, 208 + 16 * t:224 + 16 * t]
            qr = ("qkr", t)
            ACT(lambda e: e.activation(out=SQ[:, :], in_=QKR[:, t, :], func=AF.Square), [qr], ["sq"])
            DVE(lambda e: e.tensor_reduce(out=SSQ, in_=SQ[:, :].rearrange("p (g d) -> p g d", d=64),
                                          axis=AX.X, op=ALU.add), ["sq"], [("ssq", t)])
            yield
            ACT(lambda e: e.activation(out=LNQ, in_=SSQ, func=AF.Ln, scale=1.0 / 64, bias=EPS),
                [("ssq", t)], [("lnq", t)])
            ACT(lambda e: e.activation(out=RQ, in_=LNQ, func=AF.Exp, scale=-0.5), [("lnq", t)], [("rq", t)])
            POOL(lambda e: e.tensor_tensor(out=qk3, in0=qk3, in1=RQ[:, :, None].broadcast_to([128, 16, 64]),
                                           op=ALU.mult), [qr, ("rq", t)], [qr])
            yield
            gain4 = CST[:, C_QKG:C_QKG + 128].rearrange("p (a d) -> p a d", a=2)[:, :, None, :] \
                .broadcast_to([128, 2, 8, 64])
            qk4 = QKR[:, t, :].rearrange("p (a g d) -> p a g d", a=2, g=8)
            DVE(lambda e: e.tensor_tensor(out=qk4, in0=qk4, in1=gain4, op=ALU.mult), [qr, "cst"], [qr])
            ACT(lambda e: e.activation(out=QKB[:, 0, :], in_=QKR[:, t, :], func=AF.Copy), [qr], ["qkb"])
            yield
            cosb = CST[:, C_COS + jt * 8:C_COS + jt * 8 + 8][:, None, :].broadcast_to([128, 16, 8])
            sinb = CST[:, C_SIN + jt * 8:C_SIN + jt * 8 + 8][:, None, :].broadcast_to([128, 16, 8])
            x1 = qk3[:, :, 0:8]
            x2 = qk3[:, :, 8:16]
            RA = SQ[:, 0:128].rearrange("p (g d) -> p g d", d=8)
            RB = SQ[:, 128:256].rearrange("p (g d) -> p g d", d=8)
            RC = SQ[:, 256:384].rearrange("p (g d) -> p g d", d=8)
            RD = SQ[:, 384:512].rearrange("p (g d) -> p g d", d=8)
            POOL(lambda e: e.tensor_tensor(out=RA, in0=x1, in1=cosb, op=ALU.mult), [qr, "cst"], ["ra", "sq"])
            POOL(lambda e: e.tensor_tensor(out=RB, in0=x2, in1=sinb, op=ALU.mult), [qr, "cst"], ["rb", "sq"])
            POOL(lambda e: e.tensor_tensor(out=RC, in0=x2, in1=cosb, op=ALU.mult), [qr, "cst"], ["rc", "sq"])
            POOL(lambda e: e.tensor_tensor(out=RD, in0=x1, in1=sinb, op=ALU.mult), [qr, "cst"], ["rd", "sq"])
            yield
            qb3 = QKB[:, 0, :].rearrange("p (g d) -> p g d", d=64)
            DVE(lambda e: e.tensor_tensor(out=qb3[:, :, 0:8], in0=RA, in1=RB, op=ALU.subtract),
                ["ra", "rb", "sq"], ["qkb"])
            DVE(lambda e: e.tensor_tensor(out=qb3[:, :, 8:16], in0=RC, in1=RD, op=ALU.add),
                ["rc", "rd", "sq"], ["qkb"])
            yield
            pb = POOL3[pj[0]]
            pj[0] = (pj[0] + 1) % 3
            pbf = bank_bf(pb).rearrange("p (k c) -> p k c", k=8)
            for blk in range(8):
                PE(lambda e: e.transpose(out=pbf[:, blk, :], in_=QKB[:, 0, blk * 128:(blk + 1) * 128],
                                         identity=IDB[:, :]),
                   ["qkb", "idb"], [("ps", pb)], sig=(blk == 7))
            DVE(lambda e: e.tensor_copy(out=QT[0:64, xs, :, 0, t * 128:(t + 1) * 128], in_=pbf[0:64, 0:4, :]),
                [("ps", pb)], [("qt", xs)])
            DVE(lambda e: e.tensor_copy(out=QT[64:128, xs, :, 1, t * 128:(t + 1) * 128], in_=pbf[64:128, 0:4, :]),
                [("ps", pb)], [("qt", xs)])
            DVE(lambda e: e.tensor_copy(out=KT[:, :, jt * 128:(jt + 1) * 128], in_=pbf[:, 4:8, :]),
                [("ps", pb)], [("kt", jt)])
            yield

        def gla(g, t):
            xs = g % 2
            gb = 6
            PE(lambda e: e.matmul(out=bank(gb, 256), lhsT=GLT[:, t * 128:(t + 1) * 128],
                                  rhs=CST[0:16, C_GUP:C_GUP + 256], start=True, stop=True),
               [("glt", t), "cst"], [("ps", gb)])
            zl = "zl"
            DVE(lambda e: e.scalar_tensor_tensor(out=ZL[:, 0, :], in0=bank(gb, 256), scalar=RSTD2(xs)[:, t:t + 1],
                                                 in1=cst(C_GB, 256), op0=ALU.mult, op1=ALU.add),
                [("ps", gb), ("rstd", xs), "cst"], [zl])
            ACT(lambda e: e.activation(out=ZL[:, 0, :], in_=ZL[:, 0, :], func=AF.Exp, scale=-1.0), [zl], [zl])
            ACT(lambda e: e.activation(out=ZL[:, 0, :], in_=ZL[:, 0, :], func=AF.Ln, bias=1.0), [zl], [zl])
            yield
            PE(lambda e: e.matmul(out=bank(gb, 256), lhsT=tri_f, rhs=ZL[:, 0, :], start=True, stop=True),
               [zl, "cst"], [("ps", gb)], sig=False)
            PE(lambda e: e.matmul(out=bank(gb, 256, 256), lhsT=aft_f, rhs=ZL[:, 0, :], start=True, stop=True),
               [zl, "cst"], [("ps", gb)])
            ACT(lambda e: e.activation(out=EQ[:, :], in_=bank(gb, 256), func=AF.Exp, scale=-1.0 / 16),
                [("ps", gb)], ["eq"])
            ACT(lambda e: e.activation(out=EK[:, :], in_=bank(gb, 256), func=AF.Exp, scale=1.0 / 16),
                [("ps", gb)], ["ek"])
            ACT(lambda e: e.activation(out=ED[:, :], in_=bank(gb, 256, 256), func=AF.Exp, scale=-1.0 / 16),
                [("ps", gb)], ["ed"])
            gq = ("gqk", t)
            DVE(lambda e: e.scalar_tensor_tensor(out=QG[:, :], in0=GQK[:, t, 0:256], scalar=0.125,
                                                 in1=EQ[:, :], op0=ALU.mult, op1=ALU.mult), [gq, "eq"], ["qg"])
            POOL(lambda e: e.tensor_tensor(out=KG[:, :], in0=GQK[:, t, 256:512], in1=EK[:, :], op=ALU.mult),
                 [gq, "ek"], ["kg"])
            POOL(lambda e: e.tensor_tensor(out=KD[:, :], in0=GQK[:, t, 256:512], in1=ED[:, :], op=ALU.mult),
                 [gq, "ed"], ["kd"])
            yield
            for h in range(4):
                PE(lambda e: e.matmul(out=PS[0:64, gb * 512 + 2 * h:gb * 512 + 2 * h + 2],
                                      lhsT=ZL[:, 0, h * 64:(h + 1) * 64], rhs=ones_f[:, 0:2],
                                      start=True, stop=True),
                   [zl, "cst"], [("ps", gb)], sig=(h == 3))
            ACT(lambda e: e.activation(out=EB[:, :, :].rearrange("p h c -> p (h c)"),
                                       in_=PS[0:64, gb * 512:gb * 512 + 8], func=AF.Exp, scale=-1.0 / 16),
                [("ps", gb)], ["eb"])
            yield
            tbf = PS[0:64, gb * 512:(gb + 1) * 512].bitcast(BF16).rearrange("p (k c) -> p k c", k=8)
            for h in range(4):
                PE(lambda e: e.transpose(out=tbf[:, h, :], in_=QG[:, h * 64:(h + 1) * 64],
                                         identity=IDB[:, :]), ["qg", "idb"], [("ps", gb)], sig=False)
                PE(lambda e: e.transpose(out=tbf[:, 4 + h, :], in_=KG[:, h * 64:(h + 1) * 64],
                                         identity=IDB[:, :]), ["kg", "idb"], [("ps", gb)], sig=(h == 3))
            DVE(lambda e: e.tensor_copy(out=QKTG[:, :, :], in_=tbf), [("ps", gb)], ["qktg"])
            yield
            a3 = bank(gb).rearrange("p (h c) -> p h c", h=4)
            for h in range(4):
                PE(lambda e: e.matmul(out=a3[:, h, :], lhsT=QKTG[:, 4 + h, :], rhs=QKTG[:, h, :],
                                      start=True, stop=True), ["qktg"], [("ps", gb)], sig=(h == 3))
            DVE(lambda e: e.tensor_tensor(out=AT[:, :, :], in0=a3,
                                          in1=tri_f[:, None, :].broadcast_to([128, 4, 128]), op=ALU.mult),
                [("ps", gb), "cst"], ["at"])
            yield
            o3 = bank(gb).rearrange("p (h c) -> p h c", h=4)
            for h in range(4):
                PE(lambda e: e.matmul(out=o3[:, h, :], lhsT=AT[:, h, :], rhs=GV[:, t, h * 128:(h + 1) * 128],
                                      start=(h == 0), stop=False, skip_group_check=True),
                   ["at", ("gv", t)], [("ps", gb)], sig=False)
                PE(lambda e: e.matmul(out=o3[:, h, :], lhsT=QKTG[:, h, :], rhs=SB[:, h, :],
                                      start=False, stop=True, skip_group_check=True),
                   ["qktg", "SB"], [("ps", gb)], sig=(h == 3))
            SSB = STAT[:, 276:280]
            LNB = STAT[:, 280:284]
            RNB = STAT[:, 284:288]
            for h in range(4):
                ACT(lambda e: e.activation(out=JUNK[:, 0:128], in_=o3[:, h, :], func=AF.Square,
                                           accum_out=SSB[:, h:h + 1]), [("ps", gb)], ["junk", "ssb"])
            ACT(lambda e: e.activation(out=LNB, in_=SSB, func=AF.Ln, scale=1.0 / 128, bias=EPS), ["ssb"], ["lnb"])
            ACT(lambda e: e.activation(out=RNB, in_=LNB, func=AF.Exp, scale=-0.5), ["lnb"], ["rnb"])
            yield
            for h in range(4):
                DVE(lambda e: e.scalar_tensor_tensor(
                    out=OCB[:, t, h * 128:(h + 1) * 128], in0=o3[:, h, :], scalar=RNB[:, h:h + 1],
                    in1=GGS[:, t, h * 128:(h + 1) * 128], op0=ALU.mult, op1=ALU.mult),
                    [("ps", gb), "rnb", ("ggs", t)], [("ocb", t)])
            yield
            d3 = PS[0:64, gb * 512:(gb + 1) * 512].rearrange("p (h c) -> p h c", h=4)
            for h in range(4):
                PE(lambda e: e.matmul(out=d3[:, h, :], lhsT=KD[:, h * 64:(h + 1) * 64],
                                      rhs=GV[:, t, h * 128:(h + 1) * 128], start=(h == 0), stop=True,
                                      skip_group_check=True),
                   ["kd", ("gv", t)], [("ps", gb)], sig=(h == 3))
            for h in range(4):
                DVE(lambda e: e.scalar_tensor_tensor(out=S[:, h, :], in0=S[:, h, :], scalar=EB[:, h, 0:1],
                                                     in1=d3[:, h, :], op0=ALU.mult, op1=ALU.add),
                    ["S", "eb", ("ps", gb)], ["S"])
            POOL(lambda e: e.tensor_copy(out=SB[:, :, :], in_=S[:, :, :]), ["S"], ["SB"])
            yield

        def chain(*gens):
            for gn in gens:
                yield from gn

        def merge(gens):
            live = list(gens)
            while live:
                for gn in list(live):
                    try:
                        next(gn)
                        yield
                    except StopIteration:
                        live.remove(gn)

        def front_a(g):
            xs = g % 2
            RS = RSTD2(xs)
            for t in range(2):
                ACT(lambda e: e.activation(out=SQ[:, :], in_=XR[:, 0, t, :], func=AF.Square,
                                           accum_out=SS[:, t:t + 1]), [("xf", t)], ["sq", "ss"])
            ACT(lambda e: e.activation(out=LNV, in_=SS, func=AF.Ln, scale=1.0 / D, bias=EPS), ["ss"], ["lnv"])
            ACT(lambda e: e.activation(out=RS, in_=LNV, func=AF.Exp, scale=-0.5), ["lnv"], [("rstd", xs)])
            for t in range(2):
                pb = POOL3[pj[0]]
                pj[0] = (pj[0] + 1) % 3
                pbf = bank_bf(pb).rearrange("p (k c) -> p k c", k=8)
                POOL(lambda e: e.tensor_copy(out=XB[:, 0:512], in_=XR[:, 0, t, 0:512]), [("xf", t)], ["xb0"])
                DVE(lambda e: e.tensor_copy(out=XB[:, 512:1024], in_=XR[:, 0, t, 512:1024]), [("xf", t)], ["xb1"])
                if g + 1 < ngroups:
                    load_xt(g + 1, t)
                yield
                for kc in range(8):
                    PE(lambda e: e.transpose(out=pbf[:, kc, :], in_=XB[:, kc * 128:(kc + 1) * 128],
                                             identity=IDB[:, :]), ["xb%d" % (kc // 4), "idb"], [("ps", pb)],
                       sig=(kc == 7))
                DVE(lambda e: e.tensor_copy(out=XT[:, xs, :, t * 128:(t + 1) * 128], in_=pbf),
                    [("ps", pb)], [("xt", xs, t)])
                yield
            for t in range(2):
                proj(xs, t, 0, 512, QKR[:, t, 0:512], [("qkr", t)])
                yield
                proj(xs, t, 1, 512, QKR[:, t, 512:1024], [("qkr", t)])
                yield

        def front_v(g):
            xs = g % 2
            jt0 = 2 * (g % NG_SEQ)
            for t in range(2):
                proj(xs, t, 2, 512, VX[:, jt0 + t, :, 0:128], [("vx", jt0 + t)])
                yield
                proj(xs, t, 3, 512, DGS[:, xs, t, :], [("dgs", xs, t)] + (["stgB"] if g == 0 else []))
                yield
            for t in range(2):
                ACT(lambda e: e.activation(out=DGS[:, xs, t, :], in_=DGS[:, xs, t, :], func=AF.Silu),
                    [("dgs", xs, t)], [("dgs", xs, t)])
            yield

        def front_p(g):
            yield from chain(prep_qk(g, 0), prep_qk(g, 1))

        def mid_p(g, t):
            xs = g % 2
            proj(xs, t, 4, 512, GQK[:, t, :], [("gqk", t)])
            yield
            proj(xs, t, 5, 512, GV[:, t, :], [("gv", t)])
            yield
            proj(xs, t, 6, 512, GGS[:, t, :], [("ggs", t)])
            yield
            b = POOL3[pj[0]]
            pj[0] = (pj[0] + 1) % 3
            for kc in range(8):
                PE(lambda e: e.matmul(out=PS[0:16, b * 512:b * 512 + 128], lhsT=W[:, kc, 3584:3600],
                                      rhs=XT[:, xs, kc, t * 128:(t + 1) * 128], start=(kc == 0), stop=(kc == 7)),
                   [("xt", xs, t)] + Wlow, [("ps", b)], sig=(kc == 7))
            DVE(lambda e: e.tensor_copy(out=GLT[:, t * 128:(t + 1) * 128], in_=PS[0:16, b * 512:b * 512 + 128]),
                [("ps", b)], [("glt", t)])
            ACT(lambda e: e.activation(out=GGS[:, t, :], in_=GGS[:, t, :], func=AF.Silu),
                [("ggs", t)], [("ggs", t)])
            yield

        def mid_g(g, t):
            if t == 0 and g % NG_SEQ == 0 and g > 0:
                DVE(lambda e: e.memset(S[:, :, :], 0.0), [], ["S"])
                DVE(lambda e: e.memset(SB[:, :, :], 0.0), [], ["SB"])
            yield from gla(g, t)
            pbf = bank_bf(6).rearrange("p (k c) -> p k c", k=8)
            for c in range(4):
                PE(lambda e: e.transpose(out=pbf[:, c, :], in_=OCB[:, t, c * 128:(c + 1) * 128],
                                         identity=IDB[:, :]), [("ocb", t), "idb"], [("ps", 6)], sig=(c == 3))
            DVE(lambda e: e.tensor_copy(out=OCT[:, g % 2, t, 4:8, :], in_=pbf[:, 0:4, :]), [("ps", 6)], [("oct", g % 2, t, 1)])
            yield

        def fin2b(g, h):
            xs = g % 2
            pb = POOL3[pj[0]]
            pj[0] = (pj[0] + 1) % 3
            pbf = bank_bf(pb).rearrange("p (k c) -> p k c", k=8)
            for qb in range(2):
                PE(lambda e: e.transpose(out=pbf[:, qb, :], in_=OCA[:, qb, h * 128:(h + 1) * 128],
                                         identity=IDB[:, :]), [("oca", h), "idb"], [("ps", pb)], sig=(qb == 1))
            DVE(lambda e: e.tensor_copy(out=OCT[:, xs, :, h, :], in_=pbf[:, 0:2, :]), [("ps", pb)], [("oct", xs, h, 0)])

        def back_heads(g):
            j = g % NG_SEQ
            xs = g % 2
            nkb = 2 * j + 2
            ptb = [0]
            stb = [0]
            oacc = lambda qb, m: (PS[:, 4 * 512 + (2 * qb + m) * 129: 4 * 512 + (2 * qb + m) * 129 + 129]
                                  if (2 * qb + m) < 3 else PS[:, 5 * 512:5 * 512 + 129])
            obank = lambda qb, m: 4 if (2 * qb + m) < 3 else 5
            RL = STAT[:, 256:260]
            NR = STAT[:, 260:264]
            SSA = STAT[:, 264:266]
            LNA = STAT[:, 268:270]
            RNA = STAT[:, 272:274]

            def fin2(h):
                for qb in range(2):
                    ACT(lambda e: e.activation(out=JUNK[:, 0:128], in_=OAH[:, qb, :], func=AF.Square,
                                               accum_out=SSA[:, qb:qb + 1]), ["oah"], ["junk", "ssa"])
                ACT(lambda e: e.activation(out=LNA, in_=SSA, func=AF.Ln, scale=1.0 / 128, bias=EPS), ["ssa"], ["lna"])
                ACT(lambda e: e.activation(out=RNA, in_=LNA, func=AF.Exp, scale=-0.5), ["lna"], ["rna"])
                for qb in range(2):
                    DVE(lambda e: e.scalar_tensor_tensor(
                        out=OCA[:, qb, h * 128:(h + 1) * 128], in0=OAH[:, qb, :], scalar=RNA[:, qb:qb + 1],
                        in1=DGS[:, xs, qb, h * 128:(h + 1) * 128], op0=ALU.mult, op1=ALU.mult),
                        ["oah", "rna", ("dgs", xs, qb)], [("oca", h)])

            pend = None
            pendb = None
            for h in range(4):
                first = {4: True, 5: True}
                ptis = {}

                def qk_exp(kb):
                    q0 = 128 if kb == nkb - 1 else 0
                    sbk = 2 + stb[0]
                    stb[0] ^= 1
                    st3 = bank(sbk).rearrange("p (m c) -> p m c", m=2)
                    if q0 == 0:
                        PE(lambda e: e.matmul(
                            out=bank(sbk), lhsT=KT[:, h, kb * 128:(kb + 1) * 128],
                            rhs=QT[:, xs, h, :, :].rearrange("p m c -> p (m c)"), start=True, stop=True),
                           [("qt", xs), ("kt", kb)], [("ps", sbk)])
                    else:
                        for m in range(2):
                            PE(lambda e: e.matmul(
                                out=st3[:, m, q0:GT], lhsT=KT[:, h, kb * 128:(kb + 1) * 128],
                                rhs=QT[:, xs, h, m, q0:GT], start=True, stop=True),
                               [("qt", xs), ("kt", kb)], [("ps", sbk)], sig=(m == 1))
                    pti = ptb[0]
                    ptb[0] = (ptb[0] + 1) % 3
                    ptis[kb] = pti
                    ptr = ("pt", pti)
                    ACT(lambda e: e.activation(out=PT[:, pti, :, q0:GT], in_=st3[:, :, q0:GT], func=AF.Exp,
                                               scale=0.125), [("ps", sbk)], [ptr])
                    if kb >= nkb - 2:
                        qd = kb - 2 * j
                        DVE(lambda e: e.tensor_tensor(
                            out=PT[:, pti, :, qd * 128:(qd + 1) * 128], in0=PT[:, pti, :, qd * 128:(qd + 1) * 128],
                            in1=TRI2[:, :, :], op=ALU.mult), [ptr, "tri2"], [ptr])

                def pv(kb):
                    pti = ptis[kb]
                    ptr = ("pt", pti)
                    qbs = [1] if kb == nkb - 1 else [0, 1]
                    for qb in qbs:
                        for m in range(2):
                            ob = obank(qb, m)
                            stt = first[ob]
                            first[ob] = False
                            PE(lambda e: e.matmul(
                                out=oacc(qb, m), lhsT=PT[:, pti, m, qb * 128:(qb + 1) * 128],
                                rhs=VX[:, kb, h, 0:129], start=stt, stop=(kb == 2 * j + qb),
                                skip_group_check=True),
                               [ptr, ("vx", kb), "vx_ones"], [("ps", ob)], sig=(qb == qbs[-1] and m == 1))

                qk_exp(0)
                qk_exp(1)
                for kb in range(nkb):
                    if kb + 2 < nkb:
                        qk_exp(kb + 2)
                    pv(kb)
                    if kb == 1 and pend is not None:
                        fin2(pend)
                        pendb = pend
                        pend = None
                    elif kb == min(6, nkb - 1) and pendb is not None:
                        fin2b(g, pendb)
                        pendb = None
                    yield
                if pend is not None:
                    fin2(pend)
                    pendb = pend
                    pend = None
                if pendb is not None:
                    fin2b(g, pendb)
                    pendb = None
                DVE(lambda e: e.reciprocal(out=RL[:, 0:3], in_=PS[:, 4 * 512:4 * 512 + 3 * 129]
                                           .rearrange("p (a c) -> p a c", c=129)[:, :, 128]), [("ps", 4)], ["rl"])
                DVE(lambda e: e.reciprocal(out=RL[:, 3:4], in_=PS[:, 5 * 512 + 128:5 * 512 + 129]),
                    [("ps", 5)], ["rl"])
                DVE(lambda e: e.tensor_scalar(out=NR[:, :], in0=RL[:, :], scalar1=NLAM, scalar2=None, op0=ALU.mult),
                    ["rl", "nlam"], ["nr"])
                for qb in range(2):
                    DVE(lambda e: e.tensor_scalar(out=OAH[:, qb, :], in0=oacc(qb, 1)[:, 0:128],
                                                  scalar1=NR[:, 2 * qb + 1:2 * qb + 2], scalar2=None, op0=ALU.mult),
                        [("ps", obank(qb, 1)), "nr"], ["oah"])
                    DVE(lambda e: e.scalar_tensor_tensor(
                        out=OAH[:, qb, :], in0=oacc(qb, 0)[:, 0:128], scalar=RL[:, 2 * qb:2 * qb + 1],
                        in1=OAH[:, qb, :], op0=ALU.mult, op1=ALU.add),
                        [("ps", obank(qb, 0)), "rl", "oah"], ["oah"])
                pend = h
                yield
            if g - 1 in TL:
                drain(TL[g - 1])
            fin2(pend)
            yield

        def back_tail(g):
            xs = g % 2
            octr = [("oct", xs, h, 0) for h in range(4)]
            load_y(g, 0)
            load_y(g, 1)
            fin2b(g, 3)
            yield
            for t in range(2):
                for n in range(2):
                    b = POOL3[pj[0]]
                    pj[0] = (pj[0] + 1) % 3
                    for kc in range(8):
                        PE(lambda e: e.matmul(out=bank(b), lhsT=OCT[:, xs, t, kc, :], rhs=WO[:, kc, n * 512:(n + 1) * 512],
                                              start=(kc == 0), stop=(kc == 7)),
                           octr + [("oct", xs, t, 1)] + WOres, [("ps", b)], sig=(kc == 7))
                    DVE(lambda e: e.tensor_tensor(out=XR[:, 1, t, n * 512:(n + 1) * 512], in0=bank(b),
                                                  in1=XR[:, 1, t, n * 512:(n + 1) * 512], op=ALU.add),
                        [("ps", b), ("y", t)], [("y", t)])
                    yield
                i = 2 * g + t
                T.dma("sp", out_d[i * 128:(i + 1) * 128, :], XR[:, 1, t, :], [("y", t)], [], "d_o%d" % t)
            yield

        def count_steps(make):
            T.dry = True
            save = (pj[0], evac_rr[0])
            n = sum(1 for _ in make())
            pj[0], evac_rr[0] = save
            T.dry = False
            return max(1, n)

        class Stream:
            def __init__(self, make, prio=1, after=(), ready0=0.0, name=""):
                self.name = name
                self.n = count_steps(make)
                self.gen = make()
                self.k = 0
                self.done = False
                self.prio = prio
                self.after = after
                self.ready = ready0

            def eligible(self):
                ok = (not self.done) and all(a.done for a in self.after)
                if ok and self.k == 0 and self.after:
                    self.ready = max(self.ready, max(a.ready for a in self.after))
                return ok

            def frac(self):
                return (self.k + 0.5) / self.n

            def step(self):
                T.step_t = 0.0
                try:
                    t_pe = T.eng_t["pe"]
                    next(self.gen)
                    self.k += 1
                    self.ready = T.step_t
                    SCHED_LOG.append((self.name, self.k, t_pe, T.step_t))
                except StopIteration:
                    self.done = True

        def drain(st):
            if st.done or T.dry:
                return
            for a_ in st.after:
                drain(a_)
            while not st.done:
                st.step()

        def run_streams(streams):
            while not all(s.done for s in streams):
                el = [s for s in streams if s.eligible()]
                now = T.eng_t["pe"]
                rdy = [s for s in el if s.ready <= now + SCHED['margin']]
                top = [s for s in el if s.prio < 0 and s.ready <= now + SCHED['hold']]
                if top:
                    rdy = top
                if rdy:
                    hp = min(s.prio for s in rdy)
                    pick = min([s for s in rdy if s.prio == hp], key=lambda s: s.frac())
                else:
                    pick = min(el, key=lambda s: s.ready)
                pick.step()

        FA, FP, FV, MP, MG, H, TL = {}, {}, {}, {}, {}, {}, {}

        def dep(*xs):
            return tuple(x for x in xs if x is not None)

        WL1 = Stream(name='WLa', make=lambda: wl_in(0, 4), prio=-2)
        WL2 = Stream(name='WLb', make=lambda: wl_in(4, 11), prio=0, after=(WL1,))
        WL3 = Stream(name='WLc', make=lambda: wl_in(11, 15), prio=0, after=(WL2,))
        WL4 = Stream(name='WLd', make=lambda: wl_out(), prio=0, after=(WL3,))
        for g in range(ngroups):
            new_seq = (g % NG_SEQ == 0 and g > 0)
            FA[g] = Stream(name='FA%d' % g, make=lambda g=g: front_a(g), prio=SCHED['fa'],
                           after=dep(WL1 if g == 0 else None, FA.get(g - 1), FP.get(g - 1), FV.get(g - 1),
                                     MP.get((g - 2, 0)), MP.get((g - 2, 1)), MG.get((g - 2, 1))))
            FP[g] = Stream(name='FP%d' % g, make=lambda g=g: front_p(g), prio=SCHED['fp'],
                           after=dep(FA[g], H.get(g - 2), H.get(g - 1) if new_seq else None))
            FV[g] = Stream(name='FV%d' % g, make=lambda g=g: front_v(g), prio=SCHED['fv'],
                           after=dep(FA[g], H.get(g - 2), H.get(g - 1) if new_seq else None, WL3 if g == 0 else None))
            for t in range(2):
                MP[(g, t)] = Stream(name='MP%d_%d' % (g, t), make=lambda g=g, t=t: mid_p(g, t), prio=SCHED['mp'],
                                    after=dep(FA[g], MG.get((g - 1, t)), WL2 if g == 0 else None))
            MG[(g, 0)] = Stream(name='MG%d_0' % g, make=lambda g=g: mid_g(g, 0), prio=SCHED['mg'],
                                after=dep(MP[(g, 0)], MG.get((g - 1, 1)), TL.get(g - 2)))
            MG[(g, 1)] = Stream(name='MG%d_1' % g, make=lambda g=g: mid_g(g, 1), prio=SCHED['mg'],
                                after=dep(MP[(g, 1)], MG[(g, 0)]))
            H[g] = Stream(name='H%d' % g, make=lambda g=g: back_heads(g), prio=SCHED['heads'],
                          after=dep(H.get(g - 1), FP[g], FV[g], TL.get(g - 2)))
            TL[g] = Stream(name='TL%d' % g, make=lambda g=g: back_tail(g), prio=SCHED['tail'],
                           after=dep(H[g], MG[(g, 1)], TL.get(g - 1), WL4 if g == 0 else None))
        allst = [WL1, WL2, WL3, WL4]
        for g in range(ngroups):
            allst += [FA[g], FP[g], FV[g], MP[(g, 0)], MP[(g, 1)], MG[(g, 0)], MG[(g, 1)], H[g], TL[g]]
        run_streams(allst)

        T.wait_all("sp", ["d_o0", "d_o1"])
        build_program.stats = (T.n_ins, T.n_wait, dict(T.count), dict(T.eng_t))
    return nc


def _host_constants(inp):
    c = np.zeros((128, NCST), np.float32)
    c[:, C_NG:C_NG + 8] = inp["norm_gain"][0].reshape(8, 128).T
    og = np.concatenate([np.tile(inp["diff_out_gain"][0], 4), np.tile(inp["gla_out_gain"][0], 4)])
    c[:, C_OG:C_OG + 8] = og.reshape(8, 128).T
    c[:, C_QKG:C_QKG + 64] = inp["q_norm_gain"][0][None, :]
    c[:, C_QKG + 64:C_QKG + 128] = inp["k_norm_gain"][0][None, :]
    for i, nm in enumerate(["lambda_q1", "lambda_k1", "lambda_q2", "lambda_k2"]):
        c[:, C_LAM + 64 * i:C_LAM + 64 * (i + 1)] = inp[nm][0][None, :]
    c[:, C_GB:C_GB + 256] = inp["gk_bias"][0][None, :]
    pos = np.arange(SEQ, dtype=np.float32)
    inv_freq = (np.float32(500000.0) ** (-np.arange(0, 16, 2, dtype=np.float32) / np.float32(16))).astype(np.float32)
    ang = (pos[:, None] * inv_freq[None, :]).astype(np.float32)
    cos = np.cos(ang).astype(np.float32).reshape(16, 128, 8).transpose(1, 0, 2).reshape(128, 128)
    sin = np.sin(ang).astype(np.float32).reshape(16, 128, 8).transpose(1, 0, 2).reshape(128, 128)
    c[:, C_COS:C_COS + 128] = cos
    c[:, C_SIN:C_SIN + 128] = sin
    p = np.arange(128)
    c[:, C_ID:C_ID + 128] = np.eye(128, dtype=np.float32)
    c[:, C_TRI:C_TRI + 128] = (p[:, None] <= p[None, :]).astype(np.float32)
    c[:, C_AFT:C_AFT + 128] = (p[:, None] > p[None, :]).astype(np.float32)
    c[:, C_ONE:C_ONE + 128] = 1.0
    c[0:16, C_GUP:C_GUP + 256] = inp["gk_up"][0]
    return c


_NC_CACHE = {}


def kernel(**inputs):
    inp = {k: np.asarray(v) for k, v in inputs.items()}
    x = np.ascontiguousarray(inp["x"], dtype=np.float32)
    cst = _host_constants(inp)
    w_in = np.ascontiguousarray(inp["w_in"][0], dtype=np.float32)
    w_out = np.ascontiguousarray(inp["w_out"][0], dtype=np.float32)
    if "nc" not in _NC_CACHE:
        _NC_CACHE["nc"] = build_program()
    nc = _NC_CACHE["nc"]
    in_maps = []
    for c in range(N_CORES):
        xs = x[NSEQ * c:NSEQ * (c + 1)].reshape(NSEQ * SEQ, D)
        in_maps.append({"x": xs, "w_in": w_in, "w_out": w_out, "cst": cst})
    res = run_bass_kernel_spmd(nc, in_maps, core_ids=list(range(N_CORES)))
    outs = [np.asarray(r["out"]).reshape(NSEQ, SEQ, D) for r in res.results]
    return np.concatenate(outs, axis=0).astype(np.float32)
```

```python
import numpy as np
from contextlib import ExitStack

import concourse.bass as bass
import concourse.mybir as mybir
from concourse.bass_utils import run_bass_kernel_spmd

F32 = mybir.dt.float32
BF16 = mybir.dt.bfloat16
AF = mybir.ActivationFunctionType
ALU = mybir.AluOpType
AX = mybir.AxisListType

N_CORES = 8
SCHED_LOG = []
SCHED = {'heads': -1, 'mp': 1, 'mg': 0, 'tail': 0, 'fa': 1, 'fp': 0, 'fv': 1, 'margin': 150.0, 'hold': 0.0}
SEQ = 2048
D = 1024
NSEQ = 2
D_IN = 3600
EPS = 1e-6
LAMBDA_INIT = 0.8 - 0.6
GT = 256
NG_SEQ = SEQ // GT
NGROUPS = NSEQ * NG_SEQ

C_NG = 0
C_OG = 8
C_QKG = 16
C_LAM = 144
C_GB = 400
C_COS = 656
C_SIN = 784
C_ID = 912
C_TRI = 1040
C_AFT = 1168
C_ONE = 1296
C_GUP = 1424
NCST = 1680


class Tok:
    __slots__ = ("key", "val", "t")

    def __init__(self, key, val=None, t=0.0):
        self.key = key
        self.val = val
        self.t = t


def _nfree(ap):
    n = 1
    for d in ap.shape[1:]:
        n *= int(d)
    return n


class _CostProxy:
    def __init__(self, tracker, ename, eng):
        self._t = tracker
        self._e = ename
        self._eng = eng

    def __getattr__(self, name):
        real = getattr(self._eng, name)
        e = self._e

        def call(*a, **kw):
            ap = kw.get("rhs") if e == "pe" and name == "matmul" else None
            if ap is None:
                ap = kw.get("in_", kw.get("in0", kw.get("out", a[0] if a else None)))
            n = _nfree(ap) if ap is not None else 64
            if e == "pe":
                c = max(107.0, 0.42 * n)
                if name == "matmul" and kw["rhs"].dtype == F32:
                    c *= 4.0
            elif e == "act":
                c = 190.0 + 0.83 * n + (90.0 if kw.get("accum_out") is not None else 0.0)
            elif e == "dve":
                two = name in ("tensor_tensor", "scalar_tensor_tensor")
                c = 70.0 + (1.3 if not two else 1.5) * n
                if name == "reciprocal":
                    c = 70.0 + 8.0 * n
            else:
                c = 120.0 + 3.0 * n
            self._t._cost = c
            return real(*a, **kw)

        return call


class Tracker:
    def __init__(self, nc, stack):
        self.nc = nc
        self.stack = stack
        self.eng = {"pe": nc.tensor, "act": nc.scalar, "dve": nc.vector,
                    "pool": nc.gpsimd, "sp": nc.sync}
        self.sems = {}
        self.count = {}
        self.waited = {e: {} for e in self.eng}
        self.pending = {e: [] for e in self.eng}
        self.last_w = {}
        self.readers = {}
        self.n_wait = 0
        self.n_ins = 0
        self.dry = False
        self.proxy = {e: _CostProxy(self, e, en) for e, en in self.eng.items()}
        self.eng_t = {e: 0.0 for e in self.eng}
        self._cost = 100.0
        self._ready = 0.0
        self.step_t = 0.0

    def sem(self, key):
        if key not in self.sems:
            self.sems[key] = self.stack.enter_context(self.nc.semaphore("s_" + key))
            self.count[key] = 0
        return self.sems[key]

    def _deps(self, e, reads, writes, is_dma):
        need = {}
        self._ready = 0.0

        def add(tok, force):
            if tok is None:
                return
            if tok.t > self._ready:
                self._ready = tok.t
            if not force and tok.key == e and e == "pe":
                return
            assert tok.val is not None, ("unresolved token", tok.key, e, reads, writes)
            if need.get(tok.key, 0) < tok.val:
                need[tok.key] = tok.val

        for r in reads:
            add(self.last_w.get(r), True)
        for w in writes:
            add(self.last_w.get(w), is_dma)
            for tok in self.readers.get(w, {}).values():
                add(tok, is_dma)
        for key, val in need.items():
            if self.waited[e].get(key, 0) >= val:
                continue
            self.eng[e].wait_ge(self.sem(key), val)
            self.waited[e][key] = val
            self.n_wait += 1

    def _record(self, tok, reads, writes):
        for w in writes:
            self.last_w[w] = tok
            self.readers[w] = {}
        for r in reads:
            if r in writes:
                continue
            self.readers.setdefault(r, {})[tok.key] = tok

    def op(self, e, fn, reads=(), writes=(), sig=True):
        if self.dry:
            return
        excl = [r for r in reads if isinstance(r, tuple) and r[0] == "ps" and r not in writes]
        if excl:
            writes = list(writes) + excl
        self._deps(e, reads, writes, False)
        ins = fn(self.proxy[e])
        self.n_ins += 1
        fin = max(self.eng_t[e], self._ready + 60.0) + self._cost
        self.eng_t[e] = fin
        if fin > self.step_t:
            self.step_t = fin
        tok = Tok(e, None, fin)
        if sig:
            s = self.sem(e)
            self.count[e] += 1
            ins.then_inc(s, 1)
            tok.val = self.count[e]
            for p in self.pending[e]:
                p.val = tok.val
            self.pending[e] = []
        else:
            self.pending[e].append(tok)
        self._record(tok, reads, writes)

    def dma(self, q, out, in_, reads, writes, semkey):
        if self.dry:
            return
        self._deps(q, reads, writes, True)
        s = self.sem(semkey)
        self.count[semkey] += 16
        self.eng[q].dma_start(out=out, in_=in_).then_inc(s, 16)
        self.n_ins += 1
        fin = max(self.eng_t[q], self._ready + 60.0) + 2500.0 + _nfree(out) * 128 * 4 / 160.0
        self.eng_t[q] = max(self.eng_t[q], self._ready) + 100.0
        tok = Tok(semkey, self.count[semkey], fin)
        self._record(tok, reads, writes)

    def wait_all(self, e, keys):
        for key in keys:
            if key in self.sems and self.waited[e].get(key, 0) < self.count[key]:
                self.eng[e].wait_ge(self.sems[key], self.count[key])
                self.waited[e][key] = self.count[key]


def build_program(ngroups=NGROUPS, upto=99):
    nc = bass.Bass("TRN2", target_bir_lowering=False)
    x_d = nc.dram_tensor("x", [NSEQ * SEQ, D], F32, kind="ExternalInput").ap()
    win_d = nc.dram_tensor("w_in", [D, D_IN], F32, kind="ExternalInput").ap()
    wout_d = nc.dram_tensor("w_out", [D, D], F32, kind="ExternalInput").ap()
    cst_d = nc.dram_tensor("cst", [128, NCST], F32, kind="ExternalInput").ap()
    out_d = nc.dram_tensor("out", [NSEQ * SEQ, D], F32, kind="ExternalOutput").ap()

    with ExitStack() as st:
        def sb(name, shape, dt):
            return st.enter_context(nc.sbuf_tensor(name, shape, dt))

        CST = sb("cst_sb", [128, NCST], F32)
        W = sb("w_sb", [128, 8, D_IN], BF16)
        WO = sb("wo_sb", [128, 8, D], BF16)
        XR = sb("xr", [128, 2, 2, D], F32)
        XB = sb("xb", [128, D], BF16)
        XT = sb("xt", [128, 2, 8, GT], BF16)
        QKR = sb("qkr", [128, 2, 1024], F32)
        SQ = sb("sq", [128, 1024], F32)
        QKB = sb("qkb", [128, 1, 1024], BF16)
        KT = sb("kt", [128, 4, SEQ], BF16)
        VX = sb("vx", [128, 16, 4, 130], BF16)
        QT = sb("qt", [128, 2, 4, 2, GT], BF16)
        DGS = sb("dgs", [128, 2, 2, 512], F32)
        GGS = sb("ggs", [128, 2, 512], F32)
        GQK = sb("gqk", [128, 2, 512], F32)
        GV = sb("gv", [128, 2, 512], BF16)
        GLT = sb("glt", [16, GT], F32)
        PT = sb("pt", [128, 3, 2, GT], BF16)
        OAH = sb("oah", [128, 2, 128], F32)
        OCA = sb("oca", [128, 2, 512], BF16)
        OCB = sb("ocb", [128, 2, 512], BF16)
        OCT = sb("oct", [128, 2, 2, 8, 128], BF16)
        ZL = sb("zl", [128, 1, 256], F32)
        EQ = sb("eq", [128, 256], F32)
        EK = sb("ek", [128, 256], F32)
        ED = sb("ed", [128, 256], F32)
        QG = sb("qg", [128, 256], BF16)
        KG = sb("kg", [128, 256], BF16)
        KD = sb("kd", [128, 256], BF16)
        QKTG = sb("qktg", [64, 8, 128], BF16)
        AT = sb("at", [128, 4, 128], BF16)
        S = sb("s_st", [64, 4, 128], F32)
        SB = sb("s_bf", [64, 4, 128], BF16)
        EB = sb("eb", [64, 4, 2], F32)
        STAT = sb("stat", [128, 320], F32)
        IDB = sb("idb", [128, 128], BF16)
        TRI2 = sb("tri2", [128, 2, 128], BF16)
        JUNK = sb("junk", [128, 128], BF16)
        PS = st.enter_context(nc.psum_tensor("ps", [128, 8 * 512], F32))

        T = Tracker(nc, st)

        def PE(fn, r=(), w=(), sig=True):
            T.op("pe", fn, r, w, sig)

        def ACT(fn, r=(), w=()):
            T.op("act", fn, r, w)

        def DVE(fn, r=(), w=()):
            T.op("dve", fn, r, w)

        def POOL(fn, r=(), w=()):
            T.op("pool", fn, r, w)

        def bank(b, n=512, off=0):
            return PS[:, b * 512 + off: b * 512 + off + n]

        def bank_bf(b):
            return PS[:, b * 512:(b + 1) * 512].bitcast(BF16)

        cst = lambda c, n: CST[:, c:c + n]
        ident_f = cst(C_ID, 128)
        tri_f = cst(C_TRI, 128)
        aft_f = cst(C_AFT, 128)
        ones_f = cst(C_ONE, 128)

        T.dma("sp", CST[:, :], cst_d[:, :], [], ["cst"], "d_cst")
        if upto >= -2:
            DVE(lambda e: e.tensor_copy(out=IDB[:, :], in_=ident_f), ["cst"], ["idb"])
            for m in range(2):
                DVE(lambda e, m=m: e.tensor_copy(out=TRI2[:, m, :], in_=tri_f), ["cst"], ["tri2"])
            DVE(lambda e: e.memset(VX[:, :, :, 128:130], 1.0), [], ["vx_ones"])
            DVE(lambda e: e.memset(QT[:, :, :, :, :].rearrange("p a h m c -> p (a h m c)"), 0.0), [], [("qt", 0), ("qt", 1)])
            DVE(lambda e: e.memset(S[:, :, :], 0.0), [], ["S"])
            DVE(lambda e: e.memset(SB[:, :, :], 0.0), [], ["SB"])

            LAMP = STAT[:, 0:128]
            LS = STAT[:, 128:130]
            LE = STAT[:, 130:132]
            NLAM = STAT[:, 132:133]
            for i in range(2):
                DVE(lambda e, i=i: e.tensor_tensor(out=LAMP[:, 0:64], in0=cst(C_LAM + 128 * i, 64),
                                                   in1=cst(C_LAM + 128 * i + 64, 64), op=ALU.mult),
                    ["cst"], ["lamp"])
                DVE(lambda e, i=i: e.tensor_reduce(out=LS[:, i:i + 1], in_=LAMP[:, 0:64], axis=AX.X,
                                                   op=ALU.add), ["lamp"], ["ls"])
            ACT(lambda e: e.activation(out=LE, in_=LS, func=AF.Exp), ["ls"], ["le"])
            DVE(lambda e: e.tensor_tensor(out=NLAM, in0=LE[:, 1:2], in1=LE[:, 0:1], op=ALU.subtract),
                ["le"], ["nlam"])
            DVE(lambda e: e.tensor_scalar(out=NLAM, in0=NLAM, scalar1=-LAMBDA_INIT, scalar2=None,
                                          op0=ALU.add), ["nlam"], ["nlam"])


        stage = lambda k: XR[:, k % 2, :, :].rearrange("p t d -> p (t d)")
        k = 0
        engs = None
        def load_xt(g, t):
            i = 2 * g + t
            wr = [("xf", t)] + ([("X", 0)] if g == 0 else [])
            T.dma("sp", XR[:, 0, t, :], x_d[i * 128:(i + 1) * 128, :], [], wr, "d_xf%d" % t)

        def load_y(g, t):
            i = 2 * g + t
            wr = [("y", t)] + ([("X", 1)] if g == 0 else [])
            T.dma("sp", XR[:, 1, t, :], x_d[i * 128:(i + 1) * 128, :], [], wr, "d_y%d" % t)

        load_xt(0, 0)
        load_xt(0, 1)
        win3 = win_d.rearrange("(kc p) c -> p kc c", p=128)
        stg = [XR[:, 1, :, :].rearrange("p t d -> p (t d)"),
               DGS[:, :, :, :].rearrange("p a t d -> p (a t d)")]
        stg_res = [("X", 1), "stgB"]
        ngb = CST[:, C_NG:C_NG + 8]
        order = [0, 1, 2, 3, 8, 9, 10, 11, 12, 13, 14, 4, 5, 6, 7]

        def wl_in(lo, hi):
            for k in range(lo, hi):
                cb = order[k]
                s_ = k % 2
                ncol = 256 if cb < 14 else 16
                c0 = cb * 256
                st3 = stg[s_][:, 0:8 * ncol].rearrange("p (kc c) -> p kc c", kc=8)
                T.dma("sp", st3, win3[:, :, c0:c0 + ncol], [], [stg_res[s_]], "d_x%d" % s_)
                for half, eng in ((0, DVE), (1, POOL)):
                    eng(lambda e: e.tensor_tensor(
                        out=W[:, 4 * half:4 * half + 4, c0:c0 + ncol], in0=st3[:, 4 * half:4 * half + 4, :],
                        in1=ngb[:, 4 * half:4 * half + 4][:, :, None].broadcast_to([128, 4, ncol]), op=ALU.mult),
                        [stg_res[s_], "cst"], [("W", cb, half)])
                yield

        def wl_out():
            for kc in range(8):
                T.dma("sp", stg[0][:, 0:1024], wout_d[kc * 128:(kc + 1) * 128, :], [], [stg_res[0]], "d_x0")
                gmul = (1.0 - LAMBDA_INIT) if kc < 4 else 1.0
                DVE(lambda e: e.tensor_scalar(out=WO[:, kc, 0:512], in0=stg[0][:, 0:512], scalar1=cst(C_OG + kc, 1),
                                              scalar2=gmul, op0=ALU.mult, op1=ALU.mult),
                    [stg_res[0], "cst"], [("WO", kc, 0)])
                POOL(lambda e: e.tensor_scalar(out=WO[:, kc, 512:1024], in0=stg[0][:, 512:1024],
                                               scalar1=cst(C_OG + kc, 1), scalar2=gmul, op0=ALU.mult, op1=ALU.mult),
                     [stg_res[0], "cst"], [("WO", kc, 1)])
                yield

        Wn = lambda n: [("W", cb, half) for cb in (2 * n, 2 * n + 1) for half in range(2)]
        Wlow = [("W", 14, 0), ("W", 14, 1)]
        WOres = [("WO", kc, q) for kc in range(8) for q in range(2)]

        evac_rr = [0]

        def evac_scaled(out, in_, sc, r, w):
            evac_rr[0] ^= 1
            if False:
                ACT(lambda e: e.activation(out=out, in_=in_, func=AF.Copy, scale=sc), r, w)
            else:
                DVE(lambda e: e.tensor_scalar(out=out, in0=in_, scalar1=sc, scalar2=None,
                                              op0=ALU.mult), r, w)

        pj = [0]
        POOL3 = [0, 1, 7]

        def nb():
            b_ = POOL3[pj[0]]
            pj[0] = (pj[0] + 1) % 3
            return b_
        SS = STAT[:, 136:138]
        LNV = STAT[:, 138:140]
        RSTD2 = lambda xs: STAT[:, 288 + 2 * xs:290 + 2 * xs]

        def proj(xs, t, n, ncols, dst, dst_res):
            b = POOL3[pj[0]]
            pj[0] = (pj[0] + 1) % 3
            XTr = [("xt", xs, t)]
            for kc in range(8):
                PE(lambda e: e.matmul(out=bank(b, ncols), lhsT=XT[:, xs, kc, t * 128:(t + 1) * 128],
                                      rhs=W[:, kc, n * 512:n * 512 + ncols],
                                      start=(kc == 0), stop=(kc == 7)),
                   XTr + Wn(n), [("ps", b)], sig=(kc == 7))
            src = bank(b, ncols)
            if len(dst.shape) == 3:
                src = src.rearrange("p (h c) -> p h c", h=dst.shape[1])
            evac_scaled(dst, src, RSTD2(xs)[:, t:t + 1], [("ps", b), ("rstd", xs)], dst_res)

        def prep_qk(g, t):
            j = g % NG_SEQ
            xs = g % 2
            jt = 2 * j + t
            qk3 = QKR[:, t, :].rearrange("p (g d) -> p g d", d=64)
            SSQ = STAT[:, 144 + 16 * t:160 + 16 * t]
            LNQ = STAT[:, 176 + 16 * t:192 + 16 * t]
            RQ = STAT[:, 208 + 16 * t:224 + 16 * t]
            qr = ("qkr", t)
            ACT(lambda e: e.activation(out=SQ[:, :], in_=QKR[:, t, :], func=AF.Square), [qr], ["sq"])
            DVE(lambda e: e.tensor_reduce(out=SSQ, in_=SQ[:, :].rearrange("p (g d) -> p g d", d=64),
                                          axis=AX.X, op=ALU.add), ["sq"], [("ssq", t)])
            yield
            ACT(lambda e: e.activation(out=LNQ, in_=SSQ, func=AF.Ln, scale=1.0 / 64, bias=EPS),
                [("ssq", t)], [("lnq", t)])
            ACT(lambda e: e.activation(out=RQ, in_=LNQ, func=AF.Exp, scale=-0.5), [("lnq", t)], [("rq", t)])
            POOL(lambda e: e.tensor_tensor(out=qk3, in0=qk3, in1=RQ[:, :, None].broadcast_to([128, 16, 64]),
                                           op=ALU.mult), [qr, ("rq", t)], [qr])
            yield
            gain4 = CST[:, C_QKG:C_QKG + 128].rearrange("p (a d) -> p a d", a=2)[:, :, None, :] \
                .broadcast_to([128, 2, 8, 64])
            qk4 = QKR[:, t, :].rearrange("p (a g d) -> p a g d", a=2, g=8)
            DVE(lambda e: e.tensor_tensor(out=qk4, in0=qk4, in1=gain4, op=ALU.mult), [qr, "cst"], [qr])
            ACT(lambda e: e.activation(out=QKB[:, 0, :], in_=QKR[:, t, :], func=AF.Copy), [qr], ["qkb"])
            yield
            cosb = CST[:, C_COS + jt * 8:C_COS + jt * 8 + 8][:, None, :].broadcast_to([128, 16, 8])
            sinb = CST[:, C_SIN + jt * 8:C_SIN + jt * 8 + 8][:, None, :].broadcast_to([128, 16, 8])
            x1 = qk3[:, :, 0:8]
            x2 = qk3[:, :, 8:16]
            RA = SQ[:, 0:128].rearrange("p (g d) -> p g d", d=8)
            RB = SQ[:, 128:256].rearrange("p (g d) -> p g d", d=8)
            RC = SQ[:, 256:384].rearrange("p (g d) -> p g d", d=8)
            RD = SQ[:, 384:512].rearrange("p (g d) -> p g d", d=8)
            POOL(lambda e: e.tensor_tensor(out=RA, in0=x1, in1=cosb, op=ALU.mult), [qr, "cst"], ["ra", "sq"])
            POOL(lambda e: e.tensor_tensor(out=RB, in0=x2, in1=sinb, op=ALU.mult), [qr, "cst"], ["rb", "sq"])
            POOL(lambda e: e.tensor_tensor(out=RC, in0=x2, in1=cosb, op=ALU.mult), [qr, "cst"], ["rc", "sq"])
            POOL(lambda e: e.tensor_tensor(out=RD, in0=x1, in1=sinb, op=ALU.mult), [qr, "cst"], ["rd", "sq"])
            yield
            qb3 = QKB[:, 0, :].rearrange("p (g d) -> p g d", d=64)
            DVE(lambda e: e.tensor_tensor(out=qb3[:, :, 0:8], in0=RA, in1=RB, op=ALU.subtract),
                ["ra", "rb", "sq"], ["qkb"])
            DVE(lambda e: e.tensor_tensor(out=qb3[:, :, 8:16], in0=RC, in1=RD, op=ALU.add),
                ["rc", "rd", "sq"], ["qkb"])
            yield
            pb = POOL3[pj[0]]
            pj[0] = (pj[0] + 1) % 3
            pbf = bank_bf(pb).rearrange("p (k c) -> p k c", k=8)
            for blk in range(8):
                PE(lambda e: e.transpose(out=pbf[:, blk, :], in_=QKB[:, 0, blk * 128:(blk + 1) * 128],
                                         identity=IDB[:, :]),
                   ["qkb", "idb"], [("ps", pb)], sig=(blk == 7))
            DVE(lambda e: e.tensor_copy(out=QT[0:64, xs, :, 0, t * 128:(t + 1) * 128], in_=pbf[0:64, 0:4, :]),
                [("ps", pb)], [("qt", xs)])
            DVE(lambda e: e.tensor_copy(out=QT[64:128, xs, :, 1, t * 128:(t + 1) * 128], in_=pbf[64:128, 0:4, :]),
                [("ps", pb)], [("qt", xs)])
            DVE(lambda e: e.tensor_copy(out=KT[:, :, jt * 128:(jt + 1) * 128], in_=pbf[:, 4:8, :]),
                [("ps", pb)], [("kt", jt)])
            yield

        def gla(g, t):
            xs = g % 2
            gb = nb()
            PE(lambda e: e.matmul(out=bank(gb, 256), lhsT=GLT[:, t * 128:(t + 1) * 128],
                                  rhs=CST[0:16, C_GUP:C_GUP + 256], start=True, stop=True),
               [("glt", t), "cst"], [("ps", gb)])
            zl = "zl"
            DVE(lambda e: e.scalar_tensor_tensor(out=ZL[:, 0, :], in0=bank(gb, 256), scalar=RSTD2(xs)[:, t:t + 1],
                                                 in1=cst(C_GB, 256), op0=ALU.mult, op1=ALU.add),
                [("ps", gb), ("rstd", xs), "cst"], [zl])
            ACT(lambda e: e.activation(out=ZL[:, 0, :], in_=ZL[:, 0, :], func=AF.Exp, scale=-1.0), [zl], [zl])
            ACT(lambda e: e.activation(out=ZL[:, 0, :], in_=ZL[:, 0, :], func=AF.Ln, bias=1.0), [zl], [zl])
            yield
            gb = nb()
            PE(lambda e: e.matmul(out=bank(gb, 256), lhsT=tri_f, rhs=ZL[:, 0, :], start=True, stop=True),
               [zl, "cst"], [("ps", gb)], sig=False)
            PE(lambda e: e.matmul(out=bank(gb, 256, 256), lhsT=aft_f, rhs=ZL[:, 0, :], start=True, stop=True),
               [zl, "cst"], [("ps", gb)])
            ACT(lambda e: e.activation(out=EQ[:, :], in_=bank(gb, 256), func=AF.Exp, scale=-1.0 / 16),
                [("ps", gb)], ["eq"])
            ACT(lambda e: e.activation(out=EK[:, :], in_=bank(gb, 256), func=AF.Exp, scale=1.0 / 16),
                [("ps", gb)], ["ek"])
            ACT(lambda e: e.activation(out=ED[:, :], in_=bank(gb, 256, 256), func=AF.Exp, scale=-1.0 / 16),
                [("ps", gb)], ["ed"])
            gq = ("gqk", t)
            DVE(lambda e: e.scalar_tensor_tensor(out=QG[:, :], in0=GQK[:, t, 0:256], scalar=0.125,
                                                 in1=EQ[:, :], op0=ALU.mult, op1=ALU.mult), [gq, "eq"], ["qg"])
            POOL(lambda e: e.tensor_tensor(out=KG[:, :], in0=GQK[:, t, 256:512], in1=EK[:, :], op=ALU.mult),
                 [gq, "ek"], ["kg"])
            POOL(lambda e: e.tensor_tensor(out=KD[:, :], in0=GQK[:, t, 256:512], in1=ED[:, :], op=ALU.mult),
                 [gq, "ed"], ["kd"])
            yield
            gb = nb()
            for h in range(4):
                PE(lambda e: e.matmul(out=PS[0:64, gb * 512 + 2 * h:gb * 512 + 2 * h + 2],
                                      lhsT=ZL[:, 0, h * 64:(h + 1) * 64], rhs=ones_f[:, 0:2],
                                      start=True, stop=True),
                   [zl, "cst"], [("ps", gb)], sig=(h == 3))
            ACT(lambda e: e.activation(out=EB[:, :, :].rearrange("p h c -> p (h c)"),
                                       in_=PS[0:64, gb * 512:gb * 512 + 8], func=AF.Exp, scale=-1.0 / 16),
                [("ps", gb)], ["eb"])
            yield
            gb = nb()
            tbf = PS[0:64, gb * 512:(gb + 1) * 512].bitcast(BF16).rearrange("p (k c) -> p k c", k=8)
            for h in range(4):
                PE(lambda e: e.transpose(out=tbf[:, h, :], in_=QG[:, h * 64:(h + 1) * 64],
                                         identity=IDB[:, :]), ["qg", "idb"], [("ps", gb)], sig=False)
                PE(lambda e: e.transpose(out=tbf[:, 4 + h, :], in_=KG[:, h * 64:(h + 1) * 64],
                                         identity=IDB[:, :]), ["kg", "idb"], [("ps", gb)], sig=(h == 3))
            DVE(lambda e: e.tensor_copy(out=QKTG[:, :, :], in_=tbf), [("ps", gb)], ["qktg"])
            yield
            gb = nb()
            a3 = bank(gb).rearrange("p (h c) -> p h c", h=4)
            for h in range(4):
                PE(lambda e: e.matmul(out=a3[:, h, :], lhsT=QKTG[:, 4 + h, :], rhs=QKTG[:, h, :],
                                      start=True, stop=True), ["qktg"], [("ps", gb)], sig=(h == 3))
            DVE(lambda e: e.tensor_tensor(out=AT[:, :, :], in0=a3,
                                          in1=tri_f[:, None, :].broadcast_to([128, 4, 128]), op=ALU.mult),
                [("ps", gb), "cst"], ["at"])
            yield
            gb = nb()
            o3 = bank(gb).rearrange("p (h c) -> p h c", h=4)
            for h in range(4):
                PE(lambda e: e.matmul(out=o3[:, h, :], lhsT=AT[:, h, :], rhs=GV[:, t, h * 128:(h + 1) * 128],
                                      start=(h == 0), stop=False, skip_group_check=True),
                   ["at", ("gv", t)], [("ps", gb)], sig=False)
                PE(lambda e: e.matmul(out=o3[:, h, :], lhsT=QKTG[:, h, :], rhs=SB[:, h, :],
                                      start=False, stop=True, skip_group_check=True),
                   ["qktg", "SB"], [("ps", gb)], sig=(h == 3))
            SSB = STAT[:, 276:280]
            LNB = STAT[:, 280:284]
            RNB = STAT[:, 284:288]
            for h in range(4):
                ACT(lambda e: e.activation(out=JUNK[:, 0:128], in_=o3[:, h, :], func=AF.Square,
                                           accum_out=SSB[:, h:h + 1]), [("ps", gb)], ["junk", "ssb"])
            ACT(lambda e: e.activation(out=LNB, in_=SSB, func=AF.Ln, scale=1.0 / 128, bias=EPS), ["ssb"], ["lnb"])
            ACT(lambda e: e.activation(out=RNB, in_=LNB, func=AF.Exp, scale=-0.5), ["lnb"], ["rnb"])
            for h in range(4):
                DVE(lambda e: e.scalar_tensor_tensor(
                    out=OCB[:, t, h * 128:(h + 1) * 128], in0=o3[:, h, :], scalar=RNB[:, h:h + 1],
                    in1=GGS[:, t, h * 128:(h + 1) * 128], op0=ALU.mult, op1=ALU.mult),
                    [("ps", gb), "rnb", ("ggs", t)], [("ocb", t)])
            yield
            gb = nb()
            d3 = PS[0:64, gb * 512:(gb + 1) * 512].rearrange("p (h c) -> p h c", h=4)
            for h in range(4):
                PE(lambda e: e.matmul(out=d3[:, h, :], lhsT=KD[:, h * 64:(h + 1) * 64],
                                      rhs=GV[:, t, h * 128:(h + 1) * 128], start=(h == 0), stop=True,
                                      skip_group_check=True),
                   ["kd", ("gv", t)], [("ps", gb)], sig=(h == 3))
            for h in range(4):
                DVE(lambda e: e.scalar_tensor_tensor(out=S[:, h, :], in0=S[:, h, :], scalar=EB[:, h, 0:1],
                                                     in1=d3[:, h, :], op0=ALU.mult, op1=ALU.add),
                    ["S", "eb", ("ps", gb)], ["S"])
            POOL(lambda e: e.tensor_copy(out=SB[:, :, :], in_=S[:, :, :]), ["S"], ["SB"])
            yield

        def chain(*gens):
            for gn in gens:
                yield from gn

        def merge(gens):
            live = list(gens)
            while live:
                for gn in list(live):
                    try:
                        next(gn)
                        yield
                    except StopIteration:
                        live.remove(gn)

        def front_a(g):
            xs = g % 2
            RS = RSTD2(xs)
            for t in range(2):
                ACT(lambda e: e.activation(out=SQ[:, :], in_=XR[:, 0, t, :], func=AF.Square,
                                           accum_out=SS[:, t:t + 1]), [("xf", t)], ["sq", "ss"])
            ACT(lambda e: e.activation(out=LNV, in_=SS, func=AF.Ln, scale=1.0 / D, bias=EPS), ["ss"], ["lnv"])
            ACT(lambda e: e.activation(out=RS, in_=LNV, func=AF.Exp, scale=-0.5), ["lnv"], [("rstd", xs)])
            for t in range(2):
                pb = POOL3[pj[0]]
                pj[0] = (pj[0] + 1) % 3
                pbf = bank_bf(pb).rearrange("p (k c) -> p k c", k=8)
                POOL(lambda e: e.tensor_copy(out=XB[:, 0:512], in_=XR[:, 0, t, 0:512]), [("xf", t)], ["xb0"])
                DVE(lambda e: e.tensor_copy(out=XB[:, 512:1024], in_=XR[:, 0, t, 512:1024]), [("xf", t)], ["xb1"])
                if g + 1 < ngroups:
                    load_xt(g + 1, t)
                yield
                for kc in range(8):
                    PE(lambda e: e.transpose(out=pbf[:, kc, :], in_=XB[:, kc * 128:(kc + 1) * 128],
                                             identity=IDB[:, :]), ["xb%d" % (kc // 4), "idb"], [("ps", pb)],
                       sig=(kc == 7))
                DVE(lambda e: e.tensor_copy(out=XT[:, xs, :, t * 128:(t + 1) * 128], in_=pbf),
                    [("ps", pb)], [("xt", xs, t)])
                yield
            for t in range(2):
                proj(xs, t, 0, 512, QKR[:, t, 0:512], [("qkr", t)])
                yield
                proj(xs, t, 1, 512, QKR[:, t, 512:1024], [("qkr", t)])
                yield

        def front_v(g):
            xs = g % 2
            jt0 = 2 * (g % NG_SEQ)
            for t in range(2):
                proj(xs, t, 2, 512, VX[:, jt0 + t, :, 0:128], [("vx", jt0 + t)])
                yield
                proj(xs, t, 3, 512, DGS[:, xs, t, :], [("dgs", xs, t)] + (["stgB"] if g == 0 else []))
                yield
            for t in range(2):
                ACT(lambda e: e.activation(out=DGS[:, xs, t, :], in_=DGS[:, xs, t, :], func=AF.Silu),
                    [("dgs", xs, t)], [("dgs", xs, t)])
            yield

        def front_p(g):
            yield from chain(prep_qk(g, 0), prep_qk(g, 1))

        def mid_p(g, t):
            xs = g % 2
            proj(xs, t, 4, 512, GQK[:, t, :], [("gqk", t)])
            yield
            proj(xs, t, 5, 512, GV[:, t, :], [("gv", t)])
            yield
            proj(xs, t, 6, 512, GGS[:, t, :], [("ggs", t)])
            yield
            b = POOL3[pj[0]]
            pj[0] = (pj[0] + 1) % 3
            for kc in range(8):
                PE(lambda e: e.matmul(out=PS[0:16, b * 512:b * 512 + 128], lhsT=W[:, kc, 3584:3600],
                                      rhs=XT[:, xs, kc, t * 128:(t + 1) * 128], start=(kc == 0), stop=(kc == 7)),
                   [("xt", xs, t)] + Wlow, [("ps", b)], sig=(kc == 7))
            DVE(lambda e: e.tensor_copy(out=GLT[:, t * 128:(t + 1) * 128], in_=PS[0:16, b * 512:b * 512 + 128]),
                [("ps", b)], [("glt", t)])
            ACT(lambda e: e.activation(out=GGS[:, t, :], in_=GGS[:, t, :], func=AF.Silu),
                [("ggs", t)], [("ggs", t)])
            yield

        def mid_g(g, t):
            if t == 0 and g % NG_SEQ == 0 and g > 0:
                DVE(lambda e: e.memset(S[:, :, :], 0.0), [], ["S"])
                DVE(lambda e: e.memset(SB[:, :, :], 0.0), [], ["SB"])
            yield from gla(g, t)
            pb = nb()
            pbf = bank_bf(pb).rearrange("p (k c) -> p k c", k=8)
            for c in range(4):
                PE(lambda e: e.transpose(out=pbf[:, c, :], in_=OCB[:, t, c * 128:(c + 1) * 128],
                                         identity=IDB[:, :]), [("ocb", t), "idb"], [("ps", pb)], sig=(c == 3))
            DVE(lambda e: e.tensor_copy(out=OCT[:, g % 2, t, 4:8, :], in_=pbf[:, 0:4, :]), [("ps", pb)], [("oct", g % 2, t, 1)])
            yield

        def fin2b(g, h):
            xs = g % 2
            pb = POOL3[pj[0]]
            pj[0] = (pj[0] + 1) % 3
            pbf = bank_bf(pb).rearrange("p (k c) -> p k c", k=8)
            for qb in range(2):
                PE(lambda e: e.transpose(out=pbf[:, qb, :], in_=OCA[:, qb, h * 128:(h + 1) * 128],
                                         identity=IDB[:, :]), [("oca", h), "idb"], [("ps", pb)], sig=(qb == 1))
            DVE(lambda e: e.tensor_copy(out=OCT[:, xs, :, h, :], in_=pbf[:, 0:2, :]), [("ps", pb)], [("oct", xs, h, 0)])

        def back_heads(g):
            j = g % NG_SEQ
            xs = g % 2
            nkb = 2 * j + 2
            ptb = [0]
            stb = [0]
            STB = [2, 3, 6]
            oacc = lambda qb, m: (PS[:, 4 * 512 + (2 * qb + m) * 129: 4 * 512 + (2 * qb + m) * 129 + 129]
                                  if (2 * qb + m) < 3 else PS[:, 5 * 512:5 * 512 + 129])
            obank = lambda qb, m: 4 if (2 * qb + m) < 3 else 5
            RL = STAT[:, 256:260]
            NR = STAT[:, 260:264]
            SSA = STAT[:, 264:266]
            LNA = STAT[:, 268:270]
            RNA = STAT[:, 272:274]

            def fin2(h):
                for qb in range(2):
                    ACT(lambda e: e.activation(out=JUNK[:, 0:128], in_=OAH[:, qb, :], func=AF.Square,
                                               accum_out=SSA[:, qb:qb + 1]), ["oah"], ["junk", "ssa"])
                ACT(lambda e: e.activation(out=LNA, in_=SSA, func=AF.Ln, scale=1.0 / 128, bias=EPS), ["ssa"], ["lna"])
                ACT(lambda e: e.activation(out=RNA, in_=LNA, func=AF.Exp, scale=-0.5), ["lna"], ["rna"])
                for qb in range(2):
                    DVE(lambda e: e.scalar_tensor_tensor(
                        out=OCA[:, qb, h * 128:(h + 1) * 128], in0=OAH[:, qb, :], scalar=RNA[:, qb:qb + 1],
                        in1=DGS[:, xs, qb, h * 128:(h + 1) * 128], op0=ALU.mult, op1=ALU.mult),
                        ["oah", "rna", ("dgs", xs, qb)], [("oca", h)])

            pend = None
            pendb = None
            for h in range(4):
                first = {4: True, 5: True}
                ptis = {}

                sbanks = {}

                def qk(kb):
                    q0 = 128 if kb == nkb - 1 else 0
                    sbk = STB[stb[0]]
                    stb[0] = (stb[0] + 1) % 3
                    sbanks[kb] = sbk
                    st3 = bank(sbk).rearrange("p (m c) -> p m c", m=2)
                    if q0 == 0:
                        PE(lambda e: e.matmul(
                            out=bank(sbk), lhsT=KT[:, h, kb * 128:(kb + 1) * 128],
                            rhs=QT[:, xs, h, :, :].rearrange("p m c -> p (m c)"), start=True, stop=True),
                           [("qt", xs), ("kt", kb)], [("ps", sbk)])
                    else:
                        for m in range(2):
                            PE(lambda e: e.matmul(
                                out=st3[:, m, q0:GT], lhsT=KT[:, h, kb * 128:(kb + 1) * 128],
                                rhs=QT[:, xs, h, m, q0:GT], start=True, stop=True),
                               [("qt", xs), ("kt", kb)], [("ps", sbk)], sig=(m == 1))

                def ex(kb):
                    q0 = 128 if kb == nkb - 1 else 0
                    sbk = sbanks[kb]
                    st3 = bank(sbk).rearrange("p (m c) -> p m c", m=2)
                    pti = ptb[0]
                    ptb[0] = (ptb[0] + 1) % 3
                    ptis[kb] = pti
                    ptr = ("pt", pti)
                    ACT(lambda e: e.activation(out=PT[:, pti, :, q0:GT], in_=st3[:, :, q0:GT], func=AF.Exp,
                                               scale=0.125), [("ps", sbk)], [ptr])
                    if kb >= nkb - 2:
                        qd = kb - 2 * j
                        DVE(lambda e: e.tensor_tensor(
                            out=PT[:, pti, :, qd * 128:(qd + 1) * 128], in0=PT[:, pti, :, qd * 128:(qd + 1) * 128],
                            in1=TRI2[:, :, :], op=ALU.mult), [ptr, "tri2"], [ptr])

                def pv(kb):
                    pti = ptis[kb]
                    ptr = ("pt", pti)
                    qbs = [1] if kb == nkb - 1 else [0, 1]
                    for qb in qbs:
                        for m in range(2):
                            ob = obank(qb, m)
                            stt = first[ob]
                            first[ob] = False
                            PE(lambda e: e.matmul(
                                out=oacc(qb, m), lhsT=PT[:, pti, m, qb * 128:(qb + 1) * 128],
                                rhs=VX[:, kb, h, 0:129], start=stt, stop=(kb == 2 * j + qb),
                                skip_group_check=True),
                               [ptr, ("vx", kb), "vx_ones"], [("ps", ob)], sig=(qb == qbs[-1] and m == 1))

                qk(0)
                ex(0)
                qk(1)
                ex(1)
                if nkb > 2:
                    qk(2)
                for kb in range(nkb):
                    if kb + 3 < nkb:
                        qk(kb + 3)
                    if kb + 2 < nkb:
                        ex(kb + 2)
                    pv(kb)
                    if kb == 1 and pend is not None:
                        fin2(pend)
                        pendb = pend
                        pend = None
                    elif kb == min(6, nkb - 1) and pendb is not None:
                        fin2b(g, pendb)
                        pendb = None
                    yield
                if pend is not None:
                    fin2(pend)
                    pendb = pend
                    pend = None
                if pendb is not None:
                    fin2b(g, pendb)
                    pendb = None
                DVE(lambda e: e.reciprocal(out=RL[:, 0:3], in_=PS[:, 4 * 512:4 * 512 + 3 * 129]
                                           .rearrange("p (a c) -> p a c", c=129)[:, :, 128]), [("ps", 4)], ["rl"])
                DVE(lambda e: e.reciprocal(out=RL[:, 3:4], in_=PS[:, 5 * 512 + 128:5 * 512 + 129]),
                    [("ps", 5)], ["rl"])
                DVE(lambda e: e.tensor_scalar(out=NR[:, :], in0=RL[:, :], scalar1=NLAM, scalar2=None, op0=ALU.mult),
                    ["rl", "nlam"], ["nr"])
                for qb in range(2):
                    DVE(lambda e: e.tensor_scalar(out=OAH[:, qb, :], in0=oacc(qb, 1)[:, 0:128],
                                                  scalar1=NR[:, 2 * qb + 1:2 * qb + 2], scalar2=None, op0=ALU.mult),
                        [("ps", obank(qb, 1)), "nr"], ["oah"])
                    DVE(lambda e: e.scalar_tensor_tensor(
                        out=OAH[:, qb, :], in0=oacc(qb, 0)[:, 0:128], scalar=RL[:, 2 * qb:2 * qb + 1],
                        in1=OAH[:, qb, :], op0=ALU.mult, op1=ALU.add),
                        [("ps", obank(qb, 0)), "rl", "oah"], ["oah"])
                pend = h
                yield
            if g - 1 in TL:
                drain(TL[g - 1])
            fin2(pend)
            yield

        def back_tail(g):
            xs = g % 2
            octr = [("oct", xs, h, 0) for h in range(4)]
            load_y(g, 0)
            load_y(g, 1)
            fin2b(g, 3)
            yield
            for t in range(2):
                for n in range(2):
                    b = POOL3[pj[0]]
                    pj[0] = (pj[0] + 1) % 3
                    for kc in range(8):
                        PE(lambda e: e.matmul(out=bank(b), lhsT=OCT[:, xs, t, kc, :], rhs=WO[:, kc, n * 512:(n + 1) * 512],
                                              start=(kc == 0), stop=(kc == 7)),
                           octr + [("oct", xs, t, 1)] + WOres, [("ps", b)], sig=(kc == 7))
                    DVE(lambda e: e.tensor_tensor(out=XR[:, 1, t, n * 512:(n + 1) * 512], in0=bank(b),
                                                  in1=XR[:, 1, t, n * 512:(n + 1) * 512], op=ALU.add),
                        [("ps", b), ("y", t)], [("y", t)])
                    yield
                i = 2 * g + t
                T.dma("sp", out_d[i * 128:(i + 1) * 128, :], XR[:, 1, t, :], [("y", t)], [], "d_o%d" % t)
            yield

        def count_steps(make):
            T.dry = True
            save = (pj[0], evac_rr[0])
            n = sum(1 for _ in make())
            pj[0], evac_rr[0] = save
            T.dry = False
            return max(1, n)

        class Stream:
            def __init__(self, make, prio=1, after=(), ready0=0.0, name=""):
                self.name = name
                self.n = count_steps(make)
                self.gen = make()
                self.k = 0
                self.done = False
                self.prio = prio
                self.after = after
                self.ready = ready0

            def eligible(self):
                ok = (not self.done) and all(a.done for a in self.after)
                if ok and self.k == 0 and self.after:
                    self.ready = max(self.ready, max(a.ready for a in self.after))
                return ok

            def frac(self):
                return (self.k + 0.5) / self.n

            def step(self):
                T.step_t = 0.0
                try:
                    t_pe = T.eng_t["pe"]
                    next(self.gen)
                    self.k += 1
                    self.ready = T.step_t
                    SCHED_LOG.append((self.name, self.k, t_pe, T.step_t))
                except StopIteration:
                    self.done = True

        def drain(st):
            if st.done or T.dry:
                return
            for a_ in st.after:
                drain(a_)
            while not st.done:
                st.step()

        def run_streams(streams):
            while not all(s.done for s in streams):
                el = [s for s in streams if s.eligible()]
                now = T.eng_t["pe"]
                rdy = [s for s in el if s.ready <= now + SCHED['margin']]
                top = [s for s in el if s.prio < 0 and s.ready <= now + SCHED['hold']]
                if top:
                    rdy = top
                if rdy:
                    hp = min(s.prio for s in rdy)
                    pick = min([s for s in rdy if s.prio == hp], key=lambda s: s.frac())
                else:
                    pick = min(el, key=lambda s: s.ready)
                pick.step()

        FA, FP, FV, MP, MG, H, TL = {}, {}, {}, {}, {}, {}, {}

        def dep(*xs):
            return tuple(x for x in xs if x is not None)

        WL1 = Stream(name='WLa', make=lambda: wl_in(0, 4), prio=-2)
        WL2 = Stream(name='WLb', make=lambda: wl_in(4, 11), prio=0, after=(WL1,))
        WL3 = Stream(name='WLc', make=lambda: wl_in(11, 15), prio=0, after=(WL2,))
        WL4 = Stream(name='WLd', make=lambda: wl_out(), prio=0, after=(WL3,))
        for g in range(ngroups):
            new_seq = (g % NG_SEQ == 0 and g > 0)
            FA[g] = Stream(name='FA%d' % g, make=lambda g=g: front_a(g), prio=SCHED['fa'],
                           after=dep(WL1 if g == 0 else None, FA.get(g - 1), FP.get(g - 1), FV.get(g - 1),
                                     MP.get((g - 2, 0)), MP.get((g - 2, 1)), MG.get((g - 2, 1))))
            FP[g] = Stream(name='FP%d' % g, make=lambda g=g: front_p(g), prio=SCHED['fp'],
                           after=dep(FA[g], H.get(g - 2), H.get(g - 1) if new_seq else None))
            FV[g] = Stream(name='FV%d' % g, make=lambda g=g: front_v(g), prio=SCHED['fv'],
                           after=dep(FA[g], H.get(g - 2), H.get(g - 1) if new_seq else None, WL3 if g == 0 else None))
            for t in range(2):
                MP[(g, t)] = Stream(name='MP%d_%d' % (g, t), make=lambda g=g, t=t: mid_p(g, t), prio=SCHED['mp'],
                                    after=dep(FA[g], MG.get((g - 1, t)), WL2 if g == 0 else None))
            MG[(g, 0)] = Stream(name='MG%d_0' % g, make=lambda g=g: mid_g(g, 0), prio=SCHED['mg'],
                                after=dep(MP[(g, 0)], MG.get((g - 1, 1)), TL.get(g - 2)))
            MG[(g, 1)] = Stream(name='MG%d_1' % g, make=lambda g=g: mid_g(g, 1), prio=SCHED['mg'],
                                after=dep(MP[(g, 1)], MG[(g, 0)]))
            H[g] = Stream(name='H%d' % g, make=lambda g=g: back_heads(g), prio=SCHED['heads'],
                          after=dep(H.get(g - 1), FP[g], FV[g], TL.get(g - 2)))
            TL[g] = Stream(name='TL%d' % g, make=lambda g=g: back_tail(g), prio=SCHED['tail'],
                           after=dep(H[g], MG[(g, 1)], TL.get(g - 1), WL4 if g == 0 else None))
        allst = [WL1, WL2, WL3, WL4]
        for g in range(ngroups):
            allst += [FA[g], FP[g], FV[g], MP[(g, 0)], MP[(g, 1)], MG[(g, 0)], MG[(g, 1)], H[g], TL[g]]
        run_streams(allst)

        T.wait_all("sp", ["d_o0", "d_o1"])
        build_program.stats = (T.n_ins, T.n_wait, dict(T.count), dict(T.eng_t))
    return nc


def _host_constants(inp):
    c = np.zeros((128, NCST), np.float32)
    c[:, C_NG:C_NG + 8] = inp["norm_gain"][0].reshape(8, 128).T
    og = np.concatenate([np.tile(inp["diff_out_gain"][0], 4), np.tile(inp["gla_out_gain"][0], 4)])
    c[:, C_OG:C_OG + 8] = og.reshape(8, 128).T
    c[:, C_QKG:C_QKG + 64] = inp["q_norm_gain"][0][None, :]
    c[:, C_QKG + 64:C_QKG + 128] = inp["k_norm_gain"][0][None, :]
    for i, nm in enumerate(["lambda_q1", "lambda_k1", "lambda_q2", "lambda_k2"]):
        c[:, C_LAM + 64 * i:C_LAM + 64 * (i + 1)] = inp[nm][0][None, :]
    c[:, C_GB:C_GB + 256] = inp["gk_bias"][0][None, :]
    pos = np.arange(SEQ, dtype=np.float32)
    inv_freq = (np.float32(500000.0) ** (-np.arange(0, 16, 2, dtype=np.float32) / np.float32(16))).astype(np.float32)
    ang = (pos[:, None] * inv_freq[None, :]).astype(np.float32)
    cos = np.cos(ang).astype(np.float32).reshape(16, 128, 8).transpose(1, 0, 2).reshape(128, 128)
    sin = np.sin(ang).astype(np.float32).reshape(16, 128, 8).transpose(1, 0, 2).reshape(128, 128)
    c[:, C_COS:C_COS + 128] = cos
    c[:, C_SIN:C_SIN + 128] = sin
    p = np.arange(128)
    c[:, C_ID:C_ID + 128] = np.eye(128, dtype=np.float32)
    c[:, C_TRI:C_TRI + 128] = (p[:, None] <= p[None, :]).astype(np.float32)
    c[:, C_AFT:C_AFT + 128] = (p[:, None] > p[None, :]).astype(np.float32)
    c[:, C_ONE:C_ONE + 128] = 1.0
    c[0:16, C_GUP:C_GUP + 256] = inp["gk_up"][0]
    return c


_NC_CACHE = {}


def kernel(**inputs):
    inp = {k: np.asarray(v) for k, v in inputs.items()}
    x = np.ascontiguousarray(inp["x"], dtype=np.float32)
    cst = _host_constants(inp)
    w_in = np.ascontiguousarray(inp["w_in"][0], dtype=np.float32)
    w_out = np.ascontiguousarray(inp["w_out"][0], dtype=np.float32)
    if "nc" not in _NC_CACHE:
        _NC_CACHE["nc"] = build_program()
    nc = _NC_CACHE["nc"]
    in_maps = []
    for c in range(N_CORES):
        xs = x[NSEQ * c:NSEQ * (c + 1)].reshape(NSEQ * SEQ, D)
        in_maps.append({"x": xs, "w_in": w_in, "w_out": w_out, "cst": cst})
    res = run_bass_kernel_spmd(nc, in_maps, core_ids=list(range(N_CORES)))
    outs = [np.asarray(r["out"]).reshape(NSEQ, SEQ, D) for r in res.results]
    return np.concatenate(outs, axis=0).astype(np.float32)
```
